# Optimizing a Trainium2 kernel written in Bass

```python
import jax, jax.numpy as jnp
from jax import lax
import numpy as np

D_MODEL = 1024
BATCH = 4
SEQ = 4096
DEPTH = 4
DEC_BATCH = 128
DEC_SEQ = 8
PAST_LEN = 2048
PAGE_SIZE = 128

N_MIXERS = 4
N_RWKV = len(range(0, DEPTH, N_MIXERS))
N_GLA = len(range(1, DEPTH, N_MIXERS))
N_SB = len(range(2, DEPTH, N_MIXERS))
N_HG = len(range(3, DEPTH, N_MIXERS))
RW_HEAD = 64
RW_HEADS = D_MODEL // RW_HEAD
RW_DECAY_LORA = 64
RW_A_LORA = 64
RW_GN_EPS = 64e-5
GLA_HEADS = 4
GLA_DK = D_MODEL // 2 // GLA_HEADS
GLA_DV = D_MODEL // GLA_HEADS
GLA_GATE_LORA = 16
GLA_GATE_NORMALIZER = 16.0
SB_HEADS = 16
SB_HEAD = D_MODEL // SB_HEADS
SB_BLOCK = 128
SB_BIAS_INIT = -7.0
HG_EXPAND = 128
HG_HEADS = D_MODEL // HG_EXPAND
HG_DV = D_MODEL // HG_HEADS
CHUNK = 64
NORM_EPS = 1e-6

kernel_name = 'hybrid_rwkv7_gla_stickbreak_hgrn2_step'

F32 = jnp.float32


def rmsnorm(x, g):
    xf = x.astype(F32)
    y = xf * lax.rsqrt(jnp.mean(xf * xf, axis=-1, keepdims=True) + NORM_EPS)
    return (y * g.astype(F32)).astype(x.dtype)


def ada_modulation(c, w, b):
    mod = jax.nn.silu(c) @ w + b
    shift, scale, gate = jnp.split(mod, 3, axis=-1)
    return shift[:, None], scale[:, None], gate[:, None]


def group_norm_heads(o, w, b):
    B, T, H, N = o.shape
    of = o.astype(F32)
    mu = jnp.mean(of, axis=-1, keepdims=True)
    var = jnp.mean(jnp.square(of - mu), axis=-1, keepdims=True)
    y = ((of - mu) * lax.rsqrt(var + RW_GN_EPS)).reshape(B, T, H * N)
    return y * w.astype(F32) + b.astype(F32)


def rwkv7_recurrence(r, decay, k, v, kk, a, S0):
    def step(S, inp):
        r_t, w_t, k_t, v_t, kk_t, a_t = inp
        sa = jnp.einsum('bhvk,bhk->bhv', S, -kk_t)
        S = (S * w_t[:, :, None, :] + sa[..., None] * (kk_t * a_t)[:, :, None, :]
             + v_t[..., None] * k_t[:, :, None, :])
        return S, jnp.einsum('bhvk,bhk->bhv', S, r_t)
    xs = tuple(jnp.moveaxis(t.astype(F32), 1, 0) for t in (r, decay, k, v, kk, a))
    S, o = lax.scan(step, S0.astype(F32), xs)
    return jnp.moveaxis(o, 0, 1), S


def rwkv7_mixer(h, x_prev, S0, mix, w_rkvg, w0, w1, w2, a0, a1, a2, k_k, k_a, r_k, gn_w, gn_b, w_o):
    B, T, D = h.shape
    xx = jnp.concatenate([x_prev[:, None].astype(h.dtype), h[:, :-1]], axis=1) - h
    xr, xw, xk, xv, xa, xg = (h + xx * mix[n] for n in range(6))
    r = xr @ w_rkvg[0]
    k = xk @ w_rkvg[1]
    v = xv @ w_rkvg[2]
    gate = jax.nn.silu(xg @ w_rkvg[3])
    log_w = -jax.nn.softplus(-(w0 + jnp.tanh(xw @ w1) @ w2).astype(F32)) - 0.5
    decay = jnp.exp(-jnp.exp(log_w))
    a = jax.nn.sigmoid(a0 + (xa @ a1) @ a2)
    heads = lambda t: t.reshape(B, T, RW_HEADS, RW_HEAD)
    kk = heads(k * k_k).astype(F32)
    kk = kk / jnp.maximum(jnp.sqrt(jnp.sum(kk * kk, axis=-1, keepdims=True)), 1e-12)
    k = k * (1 + (a - 1) * k_a)
    r_h, k_h, v_h = heads(r), heads(k), heads(v)
    o, S = rwkv7_recurrence(r_h, heads(decay), k_h, v_h, kk, heads(a), S0)
    o = group_norm_heads(o, gn_w, gn_b)
    bonus = jnp.sum((r_h * k_h * r_k.reshape(RW_HEADS, RW_HEAD)).astype(F32), axis=-1, keepdims=True) * v_h.astype(F32)
    o = (o + bonus.reshape(B, T, D)) * gate.astype(F32)
    return o.astype(h.dtype) @ w_o, h[:, -1], S


def chunk_gated_linear_attn(q, k, v, g, S0):
    B, T, H, K = q.shape
    V = v.shape[-1]
    C = CHUNK if T % CHUNK == 0 else T
    N = T // C
    to_chunks = lambda t: jnp.moveaxis(t.astype(F32).reshape(B, N, C, H, t.shape[-1]), 1, 0)
    causal = jnp.tril(jnp.ones((C, C), dtype=bool))

    def step(S, inp):
        q_c, k_c, v_c, g_c = inp
        b = jnp.cumsum(g_c, axis=1)
        rel = jnp.where(causal[None, :, :, None, None], b[:, :, None] - b[:, None, :], -jnp.inf)
        attn = jnp.sum(q_c[:, :, None] * k_c[:, None] * jnp.exp(rel), axis=-1)
        o = (jnp.einsum('btsh,bshv->bthv', attn, v_c)
             + jnp.einsum('bthk,bhkv->bthv', q_c * jnp.exp(b), S))
        b_last = b[:, -1]
        S = (S * jnp.exp(b_last)[..., None]
             + jnp.einsum('bshk,bshv->bhkv', k_c * jnp.exp(b_last[:, None] - b), v_c))
        return S, o

    S, o = lax.scan(step, S0.astype(F32), tuple(map(to_chunks, (q, k, v, g))))
    return jnp.moveaxis(o, 0, 1).reshape(B, T, H, V), S


def gla_mixer(h, S0, w_in, w_gk2, b_gk2, gn_w, w_o):
    B, T, D = h.shape
    qk = GLA_HEADS * GLA_DK
    q, k, v, gate, gk_low = jnp.split(h @ w_in, [qk, 2 * qk, 2 * qk + D, 2 * qk + 2 * D], axis=-1)
    gk = jax.nn.log_sigmoid((gk_low @ w_gk2 + b_gk2).astype(F32)) / GLA_GATE_NORMALIZER
    kh = lambda t: t.reshape(B, T, GLA_HEADS, GLA_DK)
    o, S = chunk_gated_linear_attn(kh(q) * GLA_DK ** -0.5, kh(k), v.reshape(B, T, GLA_HEADS, GLA_DV), kh(gk), S0)
    o = rmsnorm(o, gn_w).reshape(B, T, D) * jax.nn.silu(gate.astype(F32))
    return o.astype(h.dtype) @ w_o, S


def hgrn2_mixer(h, S0, lb, w_in, gn_w, w_o):
    B, T, D = h.shape
    q, f, i, gate = jnp.split(h @ w_in, 4, axis=-1)
    lb = lb.astype(F32)
    forget = lb + (1 - lb) * jax.nn.sigmoid(f.astype(F32))
    fh = lambda t: t.reshape(B, T, HG_HEADS, HG_EXPAND)
    o, S = chunk_gated_linear_attn(fh(jax.nn.silu(q)) * HG_EXPAND ** -0.5, fh(1 - forget),
                                   i.reshape(B, T, HG_HEADS, HG_DV), fh(jnp.log(forget)), S0)
    o = rmsnorm(o, gn_w).reshape(B, T, D) * jax.nn.silu(gate.astype(F32))
    return o.astype(h.dtype) @ w_o, S


def stick_breaking_block(q, k, v, bias, q_pos, k_pos):
    z = jnp.einsum('bthd,bshd->bhts', q, k, preferred_element_type=F32) + bias.astype(F32)[None, :, None, None]
    before = k_pos[None, :] < q_pos[:, None]
    log_1mb = jnp.where(before, jax.nn.log_sigmoid(-z), 0.0)
    tail = lax.cumsum(log_1mb, axis=3, reverse=True) - log_1mb
    weight = jnp.exp(jnp.where(before, jax.nn.log_sigmoid(z) + tail, -jnp.inf))
    return jnp.einsum('bhts,bshd->bthd', weight, v.astype(F32))


def stick_breaking_attention(q, k, v, k_past, v_past, bias):
    P, T = k_past.shape[1], q.shape[1]
    k_all = jnp.concatenate([k_past, k], axis=1)
    v_all = jnp.concatenate([v_past, v], axis=1)
    outs = []
    for start in range(0, T, SB_BLOCK):
        stop = min(start + SB_BLOCK, T)
        q_pos = P + jnp.arange(start, stop)
        k_pos = jnp.arange(P + stop)
        outs.append(stick_breaking_block(q[:, start:stop], k_all[:, :P + stop], v_all[:, :P + stop], bias, q_pos, k_pos))
    return jnp.concatenate(outs, axis=1)


def stick_breaking_mixer(h, k_past, v_past, w_in, bias, w_o):
    B, T, D = h.shape
    q, k, v, gate = jnp.split(h @ w_in, 4, axis=-1)
    heads = lambda t: t.reshape(B, T, SB_HEADS, SB_HEAD)
    q, k, v = heads(q) * SB_HEAD ** -0.5, heads(k), heads(v)
    o = stick_breaking_attention(q, k, v, k_past.astype(k.dtype), v_past.astype(v.dtype), bias)
    o = o.reshape(B, T, D) * jax.nn.silu(gate.astype(F32))
    return o.astype(h.dtype) @ w_o, k, v


def gather_pages(pool, page_table):
    rows = pool[page_table]
    return rows.reshape(page_table.shape[0], -1, pool.shape[2], pool.shape[3])


def setup_inputs(seed: int = 0) -> dict:
    key = jax.random.key(seed)
    ks = iter(jax.random.split(key, 64))
    nrm = lambda shape, s=1.0: jax.random.normal(next(ks), shape, F32) * s
    unif = lambda shape: jax.random.uniform(next(ks), shape, F32)
    D = D_MODEL
    n_pages = PAST_LEN // PAGE_SIZE
    n_used = DEC_BATCH * n_pages
    n_pool = n_used + (n_used + 3) // 4
    perm = jax.random.permutation(next(ks), n_pool)
    page_table = perm[:n_used].reshape(DEC_BATCH, n_pages).astype(jnp.int32)
    qk = GLA_HEADS * GLA_DK
    return {
        'x_prompt': nrm((BATCH, SEQ, D)),
        'x_sample': nrm((DEC_BATCH, DEC_SEQ, D)),
        'c_prompt': nrm((BATCH, D)),
        'c_sample': nrm((DEC_BATCH, D)),
        'state_rwkv': nrm((N_RWKV, DEC_BATCH, RW_HEADS, RW_HEAD, RW_HEAD), 0.3),
        'cache_rwkv_shift': nrm((N_RWKV, DEC_BATCH, D)),
        'state_gla': nrm((N_GLA, DEC_BATCH, GLA_HEADS, GLA_DK, GLA_DV), 0.5),
        'cache_sb_k': nrm((N_SB, n_pool, PAGE_SIZE, SB_HEADS, SB_HEAD)),
        'cache_sb_v': nrm((N_SB, n_pool, PAGE_SIZE, SB_HEADS, SB_HEAD)),
        'state_hgrn': nrm((N_HG, DEC_BATCH, HG_HEADS, HG_EXPAND, HG_DV), 0.5),
        'page_table': page_table,
        'norm_g': 1.0 + nrm((DEPTH, D), 0.02),
        'ada_w': nrm((DEPTH, D, 3 * D), 0.5 * D ** -0.5),
        'ada_b': nrm((DEPTH, 3 * D), 0.02),
        'final_g': 1.0 + nrm((D,), 0.02),
        'rw_mix': unif((N_RWKV, 6, D)),
        'rw_w_rkvg': nrm((N_RWKV, 4, D, D), D ** -0.5),
        'rw_w0': nrm((N_RWKV, D), 0.5),
        'rw_w1': nrm((N_RWKV, D, RW_DECAY_LORA), D ** -0.5),
        'rw_w2': nrm((N_RWKV, RW_DECAY_LORA, D), 0.3 * RW_DECAY_LORA ** -0.5),
        'rw_a0': nrm((N_RWKV, D), 0.1),
        'rw_a1': nrm((N_RWKV, D, RW_A_LORA), D ** -0.5),
        'rw_a2': nrm((N_RWKV, RW_A_LORA, D), 0.3 * RW_A_LORA ** -0.5),
        'rw_k_k': 0.85 + nrm((N_RWKV, D), 0.02),
        'rw_k_a': 1.0 + nrm((N_RWKV, D), 0.02),
        'rw_r_k': nrm((N_RWKV, D), 0.1),
        'rw_gn_w': 1.0 + nrm((N_RWKV, D), 0.02),
        'rw_gn_b': nrm((N_RWKV, D), 0.02),
        'rw_w_o': nrm((N_RWKV, D, D), D ** -0.5),
        'gla_w_in': nrm((N_GLA, D, 2 * qk + 2 * D + GLA_GATE_LORA), D ** -0.5),
        'gla_w_gk2': nrm((N_GLA, GLA_GATE_LORA, qk), GLA_GATE_LORA ** -0.5),
        'gla_b_gk2': nrm((N_GLA, qk), 0.1),
        'gla_gn_w': 1.0 + nrm((N_GLA, GLA_DV), 0.02),
        'gla_w_o': nrm((N_GLA, D, D), D ** -0.5),
        'sb_w_in': nrm((N_SB, D, 4 * D), D ** -0.5),
        'sb_bias': SB_BIAS_INIT + nrm((N_SB, SB_HEADS), 0.5),
        'sb_w_o': nrm((N_SB, D, D), D ** -0.5),
        'hg_w_in': nrm((N_HG, D, 4 * D), D ** -0.5),
        'hg_lower': nrm((DEPTH, D), 0.1),
        'hg_gn_w': 1.0 + nrm((N_HG, HG_DV), 0.02),
        'hg_w_o': nrm((N_HG, D, D), D ** -0.5),
    }


def reference(x_prompt, x_sample, c_prompt, c_sample, state_rwkv, cache_rwkv_shift, state_gla,
              cache_sb_k, cache_sb_v, state_hgrn, page_table, norm_g, ada_w, ada_b, final_g,
              rw_mix, rw_w_rkvg, rw_w0, rw_w1, rw_w2, rw_a0, rw_a1, rw_a2, rw_k_k, rw_k_a, rw_r_k,
              rw_gn_w, rw_gn_b, rw_w_o, gla_w_in, gla_w_gk2, gla_b_gk2, gla_gn_w, gla_w_o,
              sb_w_in, sb_bias, sb_w_o, hg_w_in, hg_lower, hg_gn_w, hg_w_o):
    lb_soft = jax.nn.softmax(hg_lower.astype(F32), axis=0)
    lb_all = jnp.cumsum(lb_soft, axis=0) - lb_soft[0]

    def trunk(x, c, rw_S, rw_shift, gla_S, sb_past, hg_S):
        new_rw_S, new_rw_shift, new_gla_S, new_sb_k, new_sb_v, new_hg_S = [], [], [], [], [], []
        for i in range(DEPTH):
            kind, j = i % N_MIXERS, i // N_MIXERS
            shift, scale, gate = ada_modulation(c, ada_w[i], ada_b[i])
            h = rmsnorm(x, norm_g[i]) * (1 + scale) + shift
            if kind == 0:
                out, sh, S = rwkv7_mixer(h, rw_shift[j], rw_S[j], rw_mix[j], rw_w_rkvg[j], rw_w0[j], rw_w1[j],
                                         rw_w2[j], rw_a0[j], rw_a1[j], rw_a2[j], rw_k_k[j], rw_k_a[j],
                                         rw_r_k[j], rw_gn_w[j], rw_gn_b[j], rw_w_o[j])
                new_rw_S.append(S)
                new_rw_shift.append(sh)
            elif kind == 1:
                out, S = gla_mixer(h, gla_S[j], gla_w_in[j], gla_w_gk2[j], gla_b_gk2[j], gla_gn_w[j], gla_w_o[j])
                new_gla_S.append(S)
            elif kind == 2:
                k_past, v_past = sb_past(j)
                out, k_new, v_new = stick_breaking_mixer(h, k_past, v_past, sb_w_in[j], sb_bias[j], sb_w_o[j])
                new_sb_k.append(k_new)
                new_sb_v.append(v_new)
            else:
                out, S = hgrn2_mixer(h, hg_S[j], lb_all[i], hg_w_in[j], hg_gn_w[j], hg_w_o[j])
                new_hg_S.append(S)
            x = x + gate * out
        dt = x.dtype
        stack = lambda xs: jnp.stack(xs).astype(dt)
        return (rmsnorm(x, final_g), stack(new_rw_S), stack(new_rw_shift), stack(new_gla_S),
                stack(new_sb_k), stack(new_sb_v), stack(new_hg_S))

    B = x_prompt.shape[0]
    dt = x_prompt.dtype
    empty_kv = jnp.zeros((B, 0, SB_HEADS, SB_HEAD), dt)
    (y_prompt, rwkv_state_p, rwkv_shift_p, gla_state_p, sb_k_p, sb_v_p, hgrn_state_p) = trunk(
        x_prompt, c_prompt,
        jnp.zeros((N_RWKV, B, RW_HEADS, RW_HEAD, RW_HEAD), dt),
        jnp.zeros((N_RWKV, B, D_MODEL), dt),
        jnp.zeros((N_GLA, B, GLA_HEADS, GLA_DK, GLA_DV), dt),
        lambda j: (empty_kv, empty_kv),
        jnp.zeros((N_HG, B, HG_HEADS, HG_EXPAND, HG_DV), dt))

    (y_sample, rwkv_state_s, rwkv_shift_s, gla_state_s, sb_k_s, sb_v_s, hgrn_state_s) = trunk(
        x_sample, c_sample, state_rwkv, cache_rwkv_shift, state_gla,
        lambda j: (gather_pages(cache_sb_k[j], page_table), gather_pages(cache_sb_v[j], page_table)),
        state_hgrn)

    return (y_prompt, y_sample, rwkv_state_p, rwkv_shift_p, gla_state_p, sb_k_p, sb_v_p, hgrn_state_p,
            rwkv_state_s, rwkv_shift_s, gla_state_s, sb_k_s, sb_v_s, hgrn_state_s)
```

```python
import numpy as np
import concourse.bass as bass
import concourse.mybir as mybir
from concourse.bass_utils import run_bass_kernel_spmd

F32 = mybir.dt.float32
BF16 = mybir.dt.bfloat16
I32 = mybir.dt.int32
AF = mybir.ActivationFunctionType
ALU = mybir.AluOpType
AX = mybir.AxisListType

import os
DBG = float(os.environ.get("RWDBG", "99"))
D = 1024
KC = 8
NCORE = 8
NB = 16
TS = 8
EPS = 1e-6
GN_EPS = 64e-5
CDEC = -0.6065306597126334


class Tr:
    __slots__ = ("w", "r", "dsem", "dcnt", "x")

    def __init__(self, x=False):
        self.x = x
        self.w = None
        self.r = []
        self.dsem = None
        self.dcnt = 0


class V:
    def __init__(self, ap, trs):
        self.ap = ap
        self.trs = trs

    def __getitem__(self, k):
        return V(self.ap[k], self.trs)

    def rr(self, pat, **kw):
        return V(self.ap.rearrange(pat, **kw), self.trs)

    def bc(self, shape):
        return V(self.ap.broadcast_to(list(shape)), self.trs)

    def us(self, ax):
        return V(self.ap.unsqueeze(ax), self.trs)

    def bitcast(self, dt):
        return V(self.ap.bitcast(dt), self.trs)

    @property
    def shape(self):
        return tuple(self.ap.shape)


class Eng:
    def __init__(self, name, h, sem):
        self.name = name
        self.h = h
        self.sem = sem
        self.cnt = 0
        self.waited = {}


class MK:
    def __init__(self, nc):
        self.nc = nc
        self.E = {}
        for name, h in (("pe", nc.tensor), ("dve", nc.vector), ("act", nc.scalar),
                        ("pool", nc.gpsimd), ("sp", nc.sync)):
            self.E[name] = Eng(name, h, nc.alloc_semaphore("sem_" + name))
        self.nsem = 5
        self.uid = 0
        self.dram_trs = []
        self.all_dsems = []
        self.dsem_map = {}
        self.free_dsems = []
        self.stack = None
        self.phase_trs = []
        self.psum = nc.alloc_psum_tensor("psum", [128, 4096], F32)
        self.ps_tr = [Tr(True) for _ in range(8)]
        self.ps_ptr = 0

    def sb(self, shape, dt, name=None):
        self.uid += 1
        nm = "%s_%d" % (name or "t", self.uid)
        if self.stack is not None:
            t = self.stack.enter_context(self.nc.sbuf_tensor(nm, list(shape), dt))
        else:
            t = self.nc.alloc_sbuf_tensor(nm, list(shape), dt)
        tr = Tr()
        if self.stack is not None:
            self.phase_trs.append(tr)
        return V(t[tuple(slice(None) for _ in shape)], (tr,))

    def phase_begin(self):
        import contextlib
        assert self.stack is None
        self.stack = contextlib.ExitStack()
        self.phase_trs = []

    def phase_end(self):
        self.finish()
        for tr in self.phase_trs:
            if tr.dsem is not None:
                if not tr.dsem.name.startswith("gsem"):
                    self.free_dsems.append((tr.dsem, tr.dcnt))
                self.all_dsems.remove(tr)
                del self.dsem_map[tr.dsem.num]
                tr.dsem = None
        for tr in self.ps_tr:
            tr.w = None
            tr.r = []
        self.stack.close()
        self.stack = None
        self.phase_trs = []

    def dram(self, name, shape, dt, kind="Internal"):
        t = self.nc.dram_tensor(name, list(shape), dt, kind=kind)
        return t.ap()

    def dv(self, ap):
        return V(ap, (Tr(),))

    def psb(self, b, n=1):
        assert b + n <= 8
        return V(self.psum[:, b * 512:(b + n) * 512], tuple(self.ps_tr[b:b + n]))

    def _wait(self, e, ev):
        if ev is None:
            return
        sem, val = ev
        trd = self.dsem_map.get(sem.num)
        if trd is not None:
            val = trd.dcnt
        if e.waited.get(sem.num, 0) >= val:
            return
        e.h.wait_ge(sem, val)
        e.waited[sem.num] = val

    def _deps(self, e, reads, writes):
        for v in reads:
            for tr in v.trs:
                if tr.w is not None and not (e.name == "pe" and tr.w[0] is e.sem):
                    self._wait(e, tr.w)
                if tr.x:
                    for ev in tr.r:
                        if ev[0] is not e.sem:
                            self._wait(e, ev)
        pe = e.name == "pe"
        for v in writes:
            for tr in v.trs:
                if tr.w is not None and not (pe and tr.w[0] is e.sem):
                    self._wait(e, tr.w)
                for ev in tr.r:
                    if not (pe and ev[0] is e.sem):
                        self._wait(e, ev)

    def _commit(self, ev, reads, writes):
        for v in writes:
            for tr in v.trs:
                tr.w = ev
                tr.r = []
        wset = set(id(tr) for v in writes for tr in v.trs)
        for v in reads:
            for tr in v.trs:
                if id(tr) not in wset:
                    tr.r.append(ev)
                    if len(tr.r) > 24:
                        best = {}
                        for s, val in tr.r:
                            if s.num not in best or best[s.num][1] < val:
                                best[s.num] = (s, val)
                        tr.r = list(best.values())

    def op(self, en, fn, reads, writes):
        e = self.E[en]
        reads = [v for v in reads if isinstance(v, V)]
        self._deps(e, reads, writes)
        inst = fn(e.h)
        e.cnt += 1
        inst.then_inc(e.sem, 1)
        self._commit((e.sem, e.cnt), reads, writes)

    def dma(self, out, in_, q="sp"):
        q = "sp"
        e = self.E[q]
        cand = [v for v in (out, in_) if isinstance(v, V) and not self._is_dram(v)]
        assert cand, "dma needs one SBUF side"
        tr0 = cand[0].trs[0]
        if tr0.dsem is None:
            if self.free_dsems:
                tr0.dsem, tr0.dcnt = self.free_dsems.pop()
            else:
                tr0.dsem = self.nc.alloc_semaphore("dsem_%d" % self.nsem)
                self.nsem += 1
            self.all_dsems.append(tr0)
            self.dsem_map[tr0.dsem.num] = tr0
        reads = [in_] if isinstance(in_, V) else []
        writes = [out] if isinstance(out, V) else []
        for v in reads:
            for tr in v.trs:
                if tr.w is not None:
                    self._wait(e, tr.w)
        for v in writes:
            for tr in v.trs:
                if tr.w is not None and tr.w[0] is not tr0.dsem:
                    self._wait(e, tr.w)
                for ev in tr.r:
                    self._wait(e, ev)
        oap = out.ap if isinstance(out, V) else out
        iap = in_.ap if isinstance(in_, V) else in_
        inst = e.h.dma_start(out=oap, in_=iap)
        tr0.dcnt += 16
        inst.then_inc(tr0.dsem, 16)
        self._commit((tr0.dsem, tr0.dcnt), reads, writes)

    @staticmethod
    def _is_dram(v):
        return str(v.ap.space) == "DRAM"

    def finish(self):
        evs = [(e.sem, e.cnt) for e in self.E.values() if e.cnt]
        evs += [(tr.dsem, tr.dcnt) for tr in self.all_dsems]
        for e in self.E.values():
            for ev in evs:
                if ev[0] is not e.sem:
                    self._wait(e, ev)

    def mm(self, out, lhsT, rhs, start=True, stop=True, nogrp=False):
        kw = {"skip_group_check": True} if nogrp else {}
        self.op("pe", lambda h: h.matmul(out.ap, lhsT.ap, rhs.ap, start=start, stop=stop, **kw),
                [lhsT, rhs] + ([] if start else [out]), [out])

    def tp(self, out, in_, ident):
        self.op("pe", lambda h: h.transpose(out.ap, in_.ap, ident.ap), [in_, ident], [out])

    def act(self, out, in_, func, bias=None, scale=None, accum=None, en="act"):
        kw = {}
        rd = [in_]
        if bias is not None:
            kw["bias"] = bias.ap if isinstance(bias, V) else bias
            rd.append(bias)
        if scale is not None:
            kw["scale"] = scale.ap if isinstance(scale, V) else scale
            rd.append(scale)
        wr = [out]
        if accum is not None:
            kw["accum_out"] = accum.ap
            wr.append(accum)
        self.op(en, lambda h: h.activation(out.ap, in_.ap, func, **kw), rd, wr)

    def tt(self, out, a, b, op, en="dve"):
        self.op(en, lambda h: h.tensor_tensor(out.ap, a.ap, b.ap, op), [a, b], [out])

    def ts(self, out, a, s1, s2, op0, op1=None, en="dve"):
        g = lambda s: s.ap if isinstance(s, V) else s
        if op1 is None:
            self.op(en, lambda h: h.tensor_scalar(out.ap, a.ap, g(s1), 0.0, op0, ALU.add), [a, s1], [out])
        else:
            self.op(en, lambda h: h.tensor_scalar(out.ap, a.ap, g(s1), g(s2), op0, op1),
                    [a, s1, s2], [out])

    def stt(self, out, a, s, b, op0, op1, en="dve"):
        g = lambda x: x.ap if isinstance(x, V) else x
        self.op(en, lambda h: h.scalar_tensor_tensor(out.ap, a.ap, g(s), b.ap, op0, op1),
                [a, s, b], [out])

    def cp(self, out, a, en="dve"):
        if en == "act":
            self.op(en, lambda h: h.activation(out.ap, a.ap, AF.Copy), [a], [out])
        else:
            self.op(en, lambda h: h.tensor_copy(out.ap, a.ap), [a], [out])

    def red(self, out, a, op=ALU.add, en="dve"):
        self.op(en, lambda h: h.tensor_reduce(out.ap, a.ap, AX.X, op), [a], [out])

    def rsqrt(self, out, a, scale, eps):
        self.act(out, a, AF.Sqrt, bias=eps, scale=scale)
        self.op("dve", lambda h: h.reciprocal(out.ap, out.ap), [out], [out])

    def recip(self, out, a):
        self.op("dve", lambda h: h.reciprocal(out.ap, a.ap), [a], [out])

    def memset(self, out, val, en="dve"):
        self.op(en, lambda h: h.memset(out.ap, val), [], [out])


def build_consts():
    p = np.arange(128)
    f = np.arange(128)
    same = (p[:, None] // TS) == (f[None, :] // TS)
    c = {}
    c["ID"] = (p[:, None] == f[None, :])
    c["INCL_P"] = (p[:, None] <= f[None, :])
    c["STRICT_P"] = (p[:, None] < f[None, :])
    c["REM_P"] = (p[:, None] > f[None, :])
    c["UPI"] = -1.0 * (p[:, None] >= f[None, :])
    c["NONES"] = -1.0 * np.ones((128, 128))
    c["ONES"] = np.ones((128, 128))
    c["INCL_S"] = c["INCL_P"] & same
    c["STRICT_S"] = c["STRICT_P"] & same
    c["REM_S"] = c["REM_P"] & same
    c["SEG_S"] = (p[:, None] // TS) == np.arange(NB)[None, :]
    sel = np.zeros((128, 256))
    sel[0, 0:128] = 1.0
    for b in range(NB):
        sel[1 + b, 128 + b * TS:128 + (b + 1) * TS] = 1.0
    c["SEL17"] = sel
    mb = np.zeros((128, NB, TS))
    for s in range(128):
        for t in range(TS):
            if (s % TS) < t:
                mb[s, s // TS, t] = 1.0
    c["MASKB"] = mb.reshape(128, NB * TS)
    c["IOTA"] = p[:, None].astype(np.float32)
    off = {}
    cols = []
    o = 0
    for k, v in c.items():
        v = np.asarray(v, np.float32)
        off[k] = (o, v.shape[1])
        cols.append(v)
        o += v.shape[1]
    A = np.concatenate(cols, axis=1).astype(np.float32)
    bm = np.zeros((128, NB, 128), np.float32)
    for b in range(NB):
        bm[:, b, b * TS:(b + 1) * TS] = 1.0
    return A, off, bm.reshape(128, NB * 128)


CONST_A, COFF, CONST_B = build_consts()
NCA = CONST_A.shape[1]


class Prog:
    def __init__(self, cfg):
        self.NCH = NCH = cfg["NCH"]
        self.NPG = NPG = cfg["NPG"]
        self.NPOOL = NPOOL = cfg["NPOOL"]
        self.LAYERS = tuple(cfg["LAYERS"])
        nc = self.nc = bass.Bass("TRN2", target_bir_lowering=False)
        M = self.M = MK(nc)
        NT = NCH * 128
        self.I = I = {}
        self.O = O = {}

        def di(n, sh, dt=F32):
            I[n] = nc.dram_tensor(n, list(sh), dt, kind="ExternalInput").ap()

        def do(n, sh, dt=F32):
            O[n] = nc.dram_tensor(n, list(sh), dt, kind="ExternalOutput").ap()

        for n, sh in [("xp", [NT, D]), ("xs", [128, D]), ("c17", [17, D]), ("st_rw", [NB, 8, 128, 64]),
                      ("sh_rw", [NB, D]), ("st_gla", [NB, 4, 128, 256]), ("st_hg", [NB, 8, 128, 128]),
                      ("ck", [NPOOL, 128, D]), ("cv", [NPOOL, 128, D]),
                      ("norm_g", [4, D]), ("ada_w", [4, D, 3 * D]), ("ada_b", [4, 3 * D]), ("final_g", [1, D]),
                      ("rw_mix", [6, D]), ("rw_w_rkvg", [4, D, D]), ("rw_w0", [1, D]), ("rw_w1", [D, 64]),
                      ("rw_w2", [64, D]), ("rw_a0", [1, D]), ("rw_a1", [D, 64]), ("rw_a2", [64, D]),
                      ("rw_k_k", [1, D]), ("rw_k_a", [1, D]), ("rw_r_k", [1, D]), ("rw_gn_w", [1, D]),
                      ("rw_gn_b", [1, D]), ("rw_w_o", [D, D]),
                      ("gla_w_in", [D, 3088]), ("gla_w_gk2", [16, 512]), ("gla_b_gk2", [1, 512]),
                      ("gla_gn_w", [1, 256]), ("gla_w_o", [D, D]),
                      ("sb_w_in", [D, 4 * D]), ("sb_bias", [1, 16]), ("sb_w_o", [D, D]),
                      ("hg_w_in", [D, 4 * D]), ("hg_lower", [4, D]), ("hg_gn_w", [1, 128]), ("hg_w_o", [D, D]),
                      ("constA", [128, NCA]), ("constB", [128, NB * 128])]:
            di(n, sh)
        di("ptab", [1, NB * NPG], I32)
        for n, sh in [("y_p", [NT, D]), ("y_s", [128, D]), ("rw_state_p", [8, 128, 64]), ("rw_shift", [17, D]),
                      ("gla_state_p", [4, 128, 256]), ("sb_k_p", [NT, D]), ("sb_v_p", [NT, D]),
                      ("hg_state_p", [8, 128, 128]), ("rw_state_s", [NB, 8, 128, 64]),
                      ("gla_state_s", [NB, 4, 128, 256]), ("sb_k_s", [128, D]), ("sb_v_s", [128, D]),
                      ("hg_state_s", [NB, 8, 128, 128])]:
            do(n, sh)
        xs_ = M.dram("XS", [NT + 128, D], F32)
        os_ = M.dram("OS", [NT + 128, D], F32)
        kt_ = M.dram("KTS", [NCH, 128, D], BF16)
        vb_ = M.dram("VBS", [NCH, 128, D], BF16)
        self.XS = [M.dv(xs_[c * 128:(c + 1) * 128, :]) for c in range(NCH + 1)]
        self.OS = [M.dv(os_[c * 128:(c + 1) * 128, :]) for c in range(NCH + 1)]
        self.KTS = [M.dv(kt_[c]) for c in range(NCH)]
        self.VBS = [M.dv(vb_[c]) for c in range(NCH)]
        self.os_ap = os_
        self.CA = M.sb([128, NCA], F32, "CA")
        self.CAb = M.sb([128, NCA], BF16, "CAb")
        self.BMb = M.sb([128, NB * 128], BF16, "BMb")
        self.scT = M.sb([128, 8, 17], F32, "scT")
        self.AT = {m: M.sb([128, 8, 16], F32, "AT" + m) for m in "ps"}
        self.ST = {m: M.sb([128, 8, 16], F32, "ST" + m) for m in "ps"}
        self.gbc = {m: M.sb([128, D], F32, "gbc" + m) for m in "ps"}
        self.HL = M.sb([128, 8, 17], F32, "HL")
        self.xbuf = [M.sb([128, D], F32, "xbuf") for _ in range(2)]
        self.xi = 0
        self.xn = M.sb([128, D], F32, "xn")
        self.htmp = M.sb([128, D], F32, "htmp")
        self.junk = M.sb([128, D], BF16, "junk")
        self.ssv = M.sb([128, 4], F32, "ssv")
        self.first_layer = self.LAYERS[0]
        self.last_layer = self.LAYERS[-1]
        self.build()

    def c(self, name, rows=128):
        o, w = COFF[name]
        return self.CA[0:rows, o:o + w]

    def cb(self, name, rows=128):
        o, w = COFF[name]
        return self.CAb[0:rows, o:o + w]

    def chunks(self):
        return list(range(self.NCH + 1))

    def mode(self, c):
        return "s" if c == self.NCH else "p"

    def load_w(self, dst, src, stage):
        M = self.M
        ncols = src.shape[1]
        k = 0
        for kc in range(KC):
            for c0 in range(0, ncols, 1024):
                w = min(1024, ncols - c0)
                st = stage[k % len(stage)]
                M.dma(st[:, 0:w], src[kc * 128:(kc + 1) * 128, c0:c0 + w])
                M.cp(dst[:, kc, c0:c0 + w], st[:, 0:w], en=("dve" if k % 2 == 0 else "pool"))
                k += 1

    def bcast_load(self, dst, src_row):
        self.M.dma(dst, src_row.partition_broadcast(dst.shape[0]))

    def setup(self):
        M, I = self.M, self.I
        M.phase_begin()
        M.dma(self.CA, I["constA"])
        M.cp(self.CAb, self.CA)
        st = M.sb([128, NB * 128], F32, "stB")
        M.dma(st, I["constB"])
        M.cp(self.BMb, st)
        c17 = M.sb([17, D], F32, "c17")
        M.dma(c17, I["c17"])
        ps = M.psb(0)
        for kc in range(KC):
            M.tp(ps[:, kc * 17:(kc + 1) * 17], c17[0:17, kc * 128:(kc + 1) * 128], self.c("ID", 17)[:, 0:17])
        M.act(self.scT.rr("p k b -> p (k b)"), ps[:, 0:KC * 17], AF.Silu)
        M.memset(self.HL, 0.0)
        M.phase_end()

    def layer_setup(self, i):
        M, I = self.M, self.I
        M.phase_begin()
        wb = [M.sb([128, D], F32, "adaw") for _ in range(3)]
        psm = [M.psb(b) for b in range(6)]
        k = 0
        for kc in range(KC):
            for cb in range(3):
                wt = wb[k % 3]
                k += 1
                M.dma(wt, I["ada_w"][i, kc * 128:(kc + 1) * 128, cb * 1024:(cb + 1) * 1024])
                for j in range(2):
                    M.mm(psm[cb * 2 + j][0:17, :], self.scT[:, kc, :], wt[:, j * 512:(j + 1) * 512],
                         start=(kc == 0), stop=(kc == KC - 1))
        MT = M.sb([18, 3 * D], F32, "MT")
        bb = M.sb([17, 3 * D], F32, "adab")
        M.memset(MT, 0.0)
        self.bcast_load(bb, I["ada_b"][i])
        for cbk in range(6):
            M.tt(MT[0:17, cbk * 512:(cbk + 1) * 512], psm[cbk][0:17, :], bb[:, cbk * 512:(cbk + 1) * 512], ALU.add)
        M.dma(MT[17:18, D:2 * D], I["norm_g"][i:i + 1, :])
        o, _ = COFF["SEL17"]
        for m, so in (("p", o), ("s", o + 128)):
            for cb in range(2):
                ps = M.psb(6 + cb)
                M.mm(ps, self.CA[0:17, so:so + 128], MT[0:17, 2 * D + cb * 512:2 * D + (cb + 1) * 512])
                M.cp(self.gbc[m][:, cb * 512:(cb + 1) * 512], ps, en="act")
        ps = M.psb(0)
        for blk in range(16):
            M.tp(ps[:, blk * 18:(blk + 1) * 18], MT[0:18, blk * 128:(blk + 1) * 128], self.c("ID", 18)[:, 0:18])
        mt = M.sb([128, 16, 18], F32, "mtT")
        M.cp(mt.rr("p a b -> p (a b)"), ps[:, 0:16 * 18])
        M.cp(self.ST["p"], mt[:, 0:8, 0:1].bc([128, 8, 16]))
        M.cp(self.ST["s"], mt[:, 0:8, 1:17])
        M.stt(self.AT["p"], mt[:, 8:16, 0:1].bc([128, 8, 16]), 1.0, mt[:, 8:16, 17:18].bc([128, 8, 16]),
              ALU.add, ALU.mult)
        M.stt(self.AT["s"], mt[:, 8:16, 1:17], 1.0, mt[:, 8:16, 17:18].bc([128, 8, 16]), ALU.add, ALU.mult)
        M.phase_end()

    def front(self, i, c, out):
        M, I = self.M, self.I
        m = self.mode(c)
        xt = self.xbuf[self.xi % 2]
        self.xi += 1
        if i == self.first_layer:
            src = I["xp"][c * 128:(c + 1) * 128, :] if m == "p" else I["xs"]
        else:
            src = self.XS[c]
        M.dma(xt, src)
        ss = self.ssv[:, 0:1]
        rstd = self.ssv[:, 1:2]
        M.memset(ss, 0.0)
        M.act(self.junk, xt, AF.Square, accum=ss)
        M.rsqrt(rstd, ss, 1.0 / D, EPS)
        M.act(self.xn, xt, AF.Copy, scale=rstd)
        pst = M.psb(6, 2)
        for kc in range(KC):
            M.tp(pst[:, kc * 128:(kc + 1) * 128], self.xn[:, kc * 128:(kc + 1) * 128], self.c("ID"))
        v4 = lambda t: t.rr("p (k b t) -> p k b t", k=8, b=16)
        M.tt(v4(self.htmp), v4(pst), self.AT[m].us(3).bc([128, 8, 16, 8]), ALU.mult)
        M.tt(v4(out), v4(self.htmp), self.ST[m].us(3).bc([128, 8, 16, 8]), ALU.add)
        return xt

    def rw_prep(self):
        M, I = self.M, self.I
        r = {}
        mx = M.sb([6, D], F32, "mx")
        M.dma(mx, I["rw_mix"])
        sc = M.sb([NB, D], F32, "shc")
        M.dma(sc, I["sh_rw"])
        ps = M.psb(0)
        for kc in range(KC):
            M.tp(ps[:, kc * 6:(kc + 1) * 6], mx[0:6, kc * 128:(kc + 1) * 128], self.c("ID", 6)[:, 0:6])
            M.tp(ps[:, 64 + kc * 16:64 + (kc + 1) * 16], sc[0:NB, kc * 128:(kc + 1) * 128],
                 self.c("ID", NB)[:, 0:NB])
        r["mixT"] = M.sb([128, 8, 6], F32, "mixT")
        r["cacheT"] = M.sb([128, 8, 16], F32, "cacheT")
        M.cp(r["mixT"].rr("p k n -> p (k n)"), ps[:, 0:48])
        M.cp(r["cacheT"].rr("p k n -> p (k n)"), ps[:, 64:64 + 128])
        r["carry"] = M.sb([128, 8, 1], F32, "carry")
        M.memset(r["carry"], 0.0)
        r["xx"] = M.sb([128, D], F32, "xx")
        r["mtmp"] = M.sb([128, D], F32, "mtmp")
        return r

    def rw_xx(self, c, h32, r):
        M = self.M
        xx = r["xx"]
        if self.mode(c) == "p":
            v3 = lambda t: t.rr("p (k t) -> p k t", k=8)
            M.tt(v3(xx)[:, :, 1:128], v3(h32)[:, :, 0:127], v3(h32)[:, :, 1:128], ALU.subtract)
            M.tt(v3(xx)[:, :, 0:1], r["carry"], v3(h32)[:, :, 0:1], ALU.subtract)
            M.cp(r["carry"], v3(h32)[:, :, 127:128], en="pool")
        else:
            v4 = lambda t: t.rr("p (k b t) -> p k b t", k=8, b=16)
            M.tt(v4(xx)[:, :, :, 1:8], v4(h32)[:, :, :, 0:7], v4(h32)[:, :, :, 1:8], ALU.subtract)
            M.tt(v4(xx)[:, :, :, 0:1], r["cacheT"].us(3), v4(h32)[:, :, :, 0:1], ALU.subtract)

    def rw_mixed(self, out, h32, r, n, en="dve"):
        M = self.M
        v3 = lambda t: t.rr("p (k t) -> p k t", k=8)
        M.tt(v3(r["mtmp"]), v3(r["xx"]), r["mixT"][:, :, n:n + 1].bc([128, 8, 128]), ALU.mult, en=en)
        M.tt(out, r["mtmp"], h32, ALU.add, en=en)

    def phase2(self, i):
        M, I, O = self.M, self.I, self.O
        kind = i % 4
        M.phase_begin()
        Wg = M.sb([128, 8, D], BF16, "Wg")
        Wo = M.sb([128, 8, D], BF16, "Wo")
        stage = [M.sb([128, D], F32, "stage") for _ in range(2)]
        gsrc = {0: I["rw_w_rkvg"][3], 1: I["gla_w_in"][:, 2048:3072], 2: I["sb_w_in"][:, 3072:4096],
                3: I["hg_w_in"][:, 3072:4096]}[kind]
        osrc = {0: I["rw_w_o"], 1: I["gla_w_o"], 2: I["sb_w_o"], 3: I["hg_w_o"]}[kind]
        self.load_w(Wg, gsrc, stage)
        self.load_w(Wo, osrc, stage)
        h = M.sb([128, D], F32 if kind == 0 else BF16, "h")
        sg = M.sb([128, D], F32, "sg")
        op = M.sb([128, D], F32, "op")
        ogT = M.sb([128, D], BF16, "ogT")
        t3 = stage[0]
        last = (i == self.last_layer)
        if last:
            fg = M.sb([128, D], F32, "fg")
            self.bcast_load(fg, I["final_g"][0])
            t4 = stage[1]
        if kind == 0:
            r = self.rw_prep()
            xg = M.sb([128, D], BF16, "xg")
        for c in self.chunks():
            m = self.mode(c)
            xt = self.front(i, c, h)
            gin = h
            if kind == 0:
                self.rw_xx(c, h, r)
                self.rw_mixed(xg, h, r, 5)
                gin = xg
            psg = M.psb(0, 2)
            for kc in range(KC):
                for j in range(2):
                    M.mm(psg[:, j * 512:(j + 1) * 512], gin[:, kc * 128:(kc + 1) * 128],
                         Wg[:, kc, j * 512:(j + 1) * 512], start=(kc == 0), stop=(kc == KC - 1))
            M.act(sg, psg, AF.Silu)
            M.dma(op, self.OS[c])
            M.tt(sg, sg, op, ALU.mult)
            pst = M.psb(2, 2)
            for kc in range(KC):
                M.tp(pst[:, kc * 128:(kc + 1) * 128], sg[:, kc * 128:(kc + 1) * 128], self.c("ID"))
            M.cp(ogT, pst, en="act")
            pso = M.psb(4, 2)
            for kc in range(KC):
                for j in range(2):
                    M.mm(pso[:, j * 512:(j + 1) * 512], ogT[:, kc * 128:(kc + 1) * 128],
                         Wo[:, kc, j * 512:(j + 1) * 512], start=(kc == 0), stop=(kc == KC - 1))
            M.tt(t3, pso, self.gbc[m], ALU.mult)
            M.tt(t3, t3, xt, ALU.add)
            if last:
                ss = self.ssv[:, 2:3]
                rs = self.ssv[:, 3:4]
                M.memset(ss, 0.0)
                M.act(self.junk, t3, AF.Square, accum=ss)
                M.rsqrt(rs, ss, 1.0 / D, EPS)
                M.stt(t4, t3, rs, fg, ALU.mult, ALU.mult)
                dst = O["y_p"][c * 128:(c + 1) * 128, :] if m == "p" else O["y_s"]
                M.dma(dst, t4, q="pool")
            else:
                M.dma(self.XS[c], t3, q="pool")
        M.phase_end()

    def la_phase1(self, i):
        M, I, O = self.M, self.I, self.O
        kind = i % 4
        H, K, Vd = (4, 128, 256) if kind == 1 else (8, 128, 128)
        HK, HV = H * K, H * Vd
        M.phase_begin()
        stage = [M.sb([128, D], F32, "stage") for _ in range(2)]
        if kind == 1:
            W = M.sb([128, 8, 2064], BF16, "W")
            self.load_w(W[:, :, 0:2048], I["gla_w_in"][:, 0:2048], stage)
            self.load_w(W[:, :, 2048:2064], I["gla_w_in"][:, 3072:3088], stage)
            wgk2 = M.sb([16, 512], F32, "wgk2")
            M.dma(wgk2, I["gla_w_gk2"])
            bgk2 = M.sb([1, 512], F32, "bgk2")
            M.dma(bgk2, I["gla_b_gk2"])
            glT = M.sb([16, 128], F32, "glT")
            gnw = M.sb([128, Vd], F32, "gnw")
            self.bcast_load(gnw, I["gla_gn_w"][0])
            coef, qscale = -1.0 / 16.0, K ** -0.5
            st_in, st_out, st_p = I["st_gla"], O["gla_state_s"], O["gla_state_p"]
        else:
            W = M.sb([128, 8, 3 * D], BF16, "W")
            self.load_w(W, I["hg_w_in"][:, 0:3 * D], stage)
            gnw = M.sb([128, Vd], F32, "gnw")
            self.bcast_load(gnw, I["hg_gn_w"][0])
            LB = M.sb([128, D], F32, "LB")
            OMLB = M.sb([128, D], F32, "OMLB")
            et = M.sb([128, D], F32, "et")
            for j in range(4):
                self.bcast_load(stage[0], I["hg_lower"][j])
                M.act(et, stage[0], AF.Exp)
                if j == 0:
                    M.cp(OMLB, et)
                    M.memset(LB, 0.0)
                else:
                    M.tt(OMLB, OMLB, et, ALU.add)
                    if j <= i:
                        M.tt(LB, LB, et, ALU.add)
            M.recip(OMLB, OMLB)
            M.tt(LB, LB, OMLB, ALU.mult)
            M.ts(OMLB, LB, -1.0, 1.0, ALU.mult, ALU.add)
            coef, qscale = 1.0, K ** -0.5
            st_in, st_out, st_p = I["st_hg"], O["hg_state_s"], O["hg_state_p"]
            q32 = M.sb([128, D], F32, "q32")
            k32 = M.sb([128, D], F32, "k32")
        hT = M.sb([128, D], BF16, "hT")
        L = M.sb([128, HK], F32, "L")
        E1 = M.sb([128, HK], F32, "E1")
        E2 = M.sb([128, HK], F32, "E2")
        qt = M.sb([128, HK], BF16, "qt")
        kt = M.sb([128, HK], BF16, "kt")
        kh = M.sb([128, HK], BF16, "kh")
        vbf = M.sb([128, HV], BF16, "vbf")
        qkT = M.sb([128, 2 * H * 128], BF16, "qkT")
        attn = M.sb([128, H * 128], BF16, "attn")
        dcol = M.sb([128, H * 16], F32, "dcol")
        o32 = M.sb([128, HV], F32, "o32")
        sq = M.sb([128, HV], F32, "sq")
        rs = M.sb([128, 2 * H], F32, "rs")
        S32 = M.sb([128, HV], F32, "S32")
        Sbf = M.sb([128, HV], BF16, "Sbf")
        M.memset(S32, 0.0)
        M.memset(Sbf, 0.0)
        qx = [M.sb([128, NB * 128], BF16, "qx") for _ in range(2)]
        kx = [M.sb([128, NB * 128], BF16, "kx") for _ in range(2)]
        ngrp = 2 if Vd == 256 else 1
        nbg = NB // ngrp
        Sf = [M.sb([128, nbg * Vd], F32, "Sf") for _ in range(2)]
        Sb_ = [M.sb([128, nbg * Vd], BF16, "Sb") for _ in range(2)]
        ones1 = self.c("ONES", 1)
        vh = lambda t, n: t.rr("p (h x) -> p h x", h=n)
        for c in self.chunks():
            m = self.mode(c)
            nseg = 1 if m == "p" else NB
            INCL = self.c("INCL_P" if m == "p" else "INCL_S")
            REM = self.c("REM_P" if m == "p" else "REM_S")
            SEG = self.c("ONES")[:, 0:1] if m == "p" else self.c("SEG_S")
            self.front(i, c, hT)

            def proj(ps, c0, n):
                for kc in range(KC):
                    for j in range(n // 512):
                        M.mm(ps[:, j * 512:(j + 1) * 512], hT[:, kc * 128:(kc + 1) * 128],
                             W[:, kc, c0 + j * 512:c0 + (j + 1) * 512], start=(kc == 0), stop=(kc == KC - 1))
            if kind == 1:
                psqk = M.psb(0, 2)
                proj(psqk, 0, 1024)
                psv = M.psb(2, 2)
                proj(psv, 1024, 1024)
                psl = M.psb(4)
                for kc in range(KC):
                    M.mm(psl[0:16, 0:128], W[:, kc, 2048:2064], hT[:, kc * 128:(kc + 1) * 128],
                         start=(kc == 0), stop=(kc == KC - 1))
                M.cp(glT, psl[0:16, 0:128], en="act")
                psg = M.psb(5)
                M.mm(psg, glT, wgk2, start=True, stop=False)
                M.mm(psg, ones1, bgk2, start=False, stop=True)
                M.act(E1, psg, AF.Exp, scale=-1.0)
                M.act(L, E1, AF.Ln, bias=1.0)
                q_src, k_src, v_src = psqk[:, 0:512], psqk[:, 512:1024], psv
                bB, bR, bD, bT, bA, bO, bS = 6, 7, 4, 5, 6, 0, 2
            else:
                psq = M.psb(0, 2)
                proj(psq, 0, 1024)
                psf = M.psb(2, 2)
                proj(psf, 1024, 1024)
                psi = M.psb(4, 2)
                proj(psi, 2048, 1024)
                M.act(q32, psq, AF.Silu)
                M.act(E1, psf, AF.Sigmoid)
                M.tt(E1, E1, OMLB, ALU.mult)
                M.tt(E1, E1, LB, ALU.add)
                M.act(L, E1, AF.Ln)
                M.ts(k32, E1, -1.0, 1.0, ALU.mult, ALU.add)
                M.cp(vbf, psi, en="act")
                q_src, k_src, v_src = q32, k32, None
                bB, bR, bD, bT, bA, bO, bS = 0, 2, 6, 4, 0, 2, 4
            nbk = HK // 512
            psB = M.psb(bB, nbk)
            psR = M.psb(bR, nbk)
            for j in range(nbk):
                M.mm(psB[:, j * 512:(j + 1) * 512], INCL, L[:, j * 512:(j + 1) * 512])
                M.mm(psR[:, j * 512:(j + 1) * 512], REM, L[:, j * 512:(j + 1) * 512])
            M.act(E1, psB, AF.Exp, scale=coef)
            M.stt(qt, q_src, qscale, E1, ALU.mult, ALU.mult)
            M.act(E2, psB, AF.Exp, scale=-coef)
            M.tt(kt, k_src, E2, ALU.mult)
            M.act(E1, psR, AF.Exp, scale=coef)
            M.tt(kh, k_src, E1, ALU.mult)
            if v_src is not None:
                M.cp(vbf, v_src, en="act")
            psD = M.psb(bD)
            for h in range(H):
                M.mm(psD[:, h * nseg:(h + 1) * nseg], L[:, h * K:(h + 1) * K], SEG)
            M.act(dcol[:, 0:H * nseg], psD[:, 0:H * nseg], AF.Exp, scale=coef)
            nTb = (2 * H * 128) // 1024
            psT = M.psb(bT, nTb).bitcast(BF16)
            for h in range(H):
                M.tp(psT[:, h * 128:(h + 1) * 128], qt[:, h * K:(h + 1) * K], self.cb("ID"))
                M.tp(psT[:, (H + h) * 128:(H + h + 1) * 128], kt[:, h * K:(h + 1) * K], self.cb("ID"))
            M.cp(qkT, psT)
            psA = M.psb(bA, (H * 128) // 512)
            for h in range(H):
                M.mm(psA[:, h * 128:(h + 1) * 128], qkT[:, (H + h) * 128:(H + h + 1) * 128],
                     qkT[:, h * 128:(h + 1) * 128])
            M.tt(vh(attn, H), vh(psA, H), INCL.us(1).bc([128, H, 128]), ALU.mult)
            psO = M.psb(bO, 2)
            if m == "p":
                for h in range(H):
                    M.mm(psO[:, h * Vd:(h + 1) * Vd], attn[:, h * 128:(h + 1) * 128], vbf[:, h * Vd:(h + 1) * Vd],
                         start=True, stop=False)
                    M.mm(psO[:, h * Vd:(h + 1) * Vd], qkT[:, h * 128:(h + 1) * 128], Sbf[:, h * Vd:(h + 1) * Vd],
                         start=False, stop=True)
                psS = M.psb(bS, 2)
                for h in range(H):
                    M.mm(psS[:, h * Vd:(h + 1) * Vd], kh[:, h * K:(h + 1) * K], vbf[:, h * Vd:(h + 1) * Vd])
                M.tt(vh(S32, H), vh(S32, H), dcol[:, 0:H].us(2).bc([128, H, Vd]), ALU.mult)
                M.tt(S32, S32, psS, ALU.add)
                M.cp(Sbf, S32, en="pool")
            else:
                k = 0
                for h in range(H):
                    M.mm(psO[:, h * Vd:(h + 1) * Vd], attn[:, h * 128:(h + 1) * 128], vbf[:, h * Vd:(h + 1) * Vd],
                         start=True, stop=False)
                    qx_, kx_ = qx[h % 2], kx[h % 2]
                    v3 = lambda t: t.rr("p (b x) -> p b x", b=NB)
                    M.tt(v3(qx_), qkT[:, h * 128:(h + 1) * 128].us(1).bc([128, NB, 128]), v3(self.BMb),
                         ALU.mult, en="pool")
                    M.tt(v3(kx_), kh[:, h * K:(h + 1) * K].us(1).bc([128, NB, 128]),
                         self.cb("SEG_S").us(2).bc([128, NB, 128]), ALU.mult, en="pool")
                    for g in range(ngrp):
                        b0 = g * nbg
                        sf, sb_ = Sf[k % 2], Sb_[k % 2]
                        k += 1
                        vb = lambda t: t.rr("p (b x) -> p b x", b=nbg)
                        M.dma(vb(sf), st_in[b0:b0 + nbg, h].rearrange("b k v -> k b v"))
                        M.cp(sb_, sf, en="act")
                        for bb in range(nbg):
                            b = b0 + bb
                            M.mm(psO[:, h * Vd:(h + 1) * Vd], qx_[:, b * 128:(b + 1) * 128],
                                 sb_[:, bb * Vd:(bb + 1) * Vd], start=False, stop=(b == NB - 1))
                        psS = M.psb(4, 4)
                        for bb in range(nbg):
                            b = b0 + bb
                            M.mm(psS[:, bb * Vd:(bb + 1) * Vd], kx_[:, b * 128:(b + 1) * 128],
                                 vbf[:, h * Vd:(h + 1) * Vd])
                        M.tt(vb(sf), vb(sf), dcol[:, h * NB + b0:h * NB + b0 + nbg].us(2).bc([128, nbg, Vd]),
                             ALU.mult)
                        M.tt(sf, sf, psS[:, 0:nbg * Vd], ALU.add)
                        M.dma(st_out[b0:b0 + nbg, h].rearrange("b k v -> k b v"), vb(sf), q="pool")
            M.cp(o32, psO, en="act")
            M.act(sq, psO, AF.Square)
            M.red(rs[:, 0:H], vh(sq, H))
            M.rsqrt(rs[:, 0:H], rs[:, 0:H], 1.0 / Vd, EPS)
            M.tt(vh(o32, H), vh(o32, H), rs[:, 0:H].us(2).bc([128, H, Vd]), ALU.mult)
            M.tt(vh(o32, H), vh(o32, H), gnw.us(1).bc([128, H, Vd]), ALU.mult)
            M.dma(self.OS[c], o32, q="pool")
        M.dma(st_p.rearrange("h k v -> k h v"), vh(S32, H), q="pool")
        M.phase_end()

    def rw_phase1(self, i):
        M, I, O = self.M, self.I, self.O
        NCH = self.NCH
        M.phase_begin()
        stage = [M.sb([128, D], F32, "stage") for _ in range(2)]
        W = M.sb([128, 8, 3 * D], BF16, "W")
        for n in range(3):
            self.load_w(W[:, :, n * D:(n + 1) * D], I["rw_w_rkvg"][n], stage)
        w1 = M.sb([128, 8, 64], BF16, "w1")
        a1 = M.sb([128, 8, 64], BF16, "a1")
        self.load_w(w1, I["rw_w1"], stage)
        self.load_w(a1, I["rw_a1"], stage)
        w2x = M.sb([65, D], BF16, "w2x")
        a2x = M.sb([65, D], BF16, "a2x")
        for dst, s2, s0 in ((w2x, "rw_w2", "rw_w0"), (a2x, "rw_a2", "rw_a0")):
            M.dma(stage[0][0:64, :], I[s2])
            M.dma(stage[0][64:65, :], I[s0])
            M.cp(dst, stage[0][0:65, :])
        bcs = {}
        for n in ("rw_k_k", "rw_k_a", "rw_r_k", "rw_gn_w", "rw_gn_b"):
            bcs[n] = M.sb([128, D], BF16, n)
            self.bcast_load(stage[1], I[n][0])
            M.cp(bcs[n], stage[1])
        tuT = M.sb([65, 128], BF16, "tuT")
        auT = M.sb([65, 128], BF16, "auT")
        M.memset(tuT[64:65, :], 1.0)
        M.memset(auT[64:65, :], 1.0)
        r = self.rw_prep()
        E, t1 = stage
        o32, sq = r["xx"], r["mtmp"]
        h32 = M.sb([128, D], F32, "h32")
        kk = h32
        r32 = M.sb([128, D], F32, "r32")
        k32 = M.sb([128, D], F32, "k32")
        lw = M.sb([128, D], F32, "lw")
        a32 = M.sb([128, D], F32, "a32")
        xT = [M.sb([128, D], BF16, "xT") for _ in range(2)]
        X1, U = xT
        rt = M.sb([128, D], BF16, "rt")
        kt_ = M.sb([128, D], BF16, "kt")
        bt = M.sb([128, D], BF16, "bt")
        at = M.sb([128, D], BF16, "at")
        kh, bh = rt, kt_
        vbf = M.sb([128, D], BF16, "vbf")
        ARt = M.sb([128, 2 * D], BF16, "ARt")
        KBt = M.sb([128, 2 * D], BF16, "KBt")
        AK = M.sb([128, 16, 128], BF16, "AK")
        RK = M.sb([128, 16, 128], BF16, "RK")
        RB = M.sb([128, 16, 128], BF16, "RB")
        TT = M.sb([128, 16, 128], BF16, "TT")
        NS = [[M.sb([128, 4, 128], BF16, "nm") for _ in range(3)] for _ in range(2)]
        S32 = M.sb([128, 512], F32, "S32")
        Sbf = M.sb([128, 512], BF16, "Sbf")
        M.memset(S32, 0.0)
        M.memset(Sbf, 0.0)
        dcol = M.sb([128, 8 * NB], F32, "dcol")
        st = M.sb([128, 80], F32, "st")
        XP = [E.bitcast(BF16), t1.bitcast(BF16), r32.bitcast(BF16), k32.bitcast(BF16)]
        Sf = lw
        Sb_ = at
        hlo = a32[0:17, :]
        vh = lambda t: t.rr("p (h x) -> p h x", h=16)
        v3k = lambda t: t.rr("p (k t) -> p k t", k=8)
        v4k = lambda t: t.rr("p (k b t) -> p k b t", k=8, b=16)
        vbx = lambda t: t.rr("p (b x) -> p b x", b=NB)
        IDb = self.cb("ID")

        def projx(ps, x_, c0):
            for kc in range(KC):
                for j in range(2):
                    M.mm(ps[:, j * 512:(j + 1) * 512], x_[:, kc * 128:(kc + 1) * 128],
                         W[:, kc, c0 + j * 512:c0 + (j + 1) * 512], start=(kc == 0), stop=(kc == KC - 1))

        for c in self.chunks():
            m = self.mode(c)
            nseg = 1 if m == "p" else NB
            INCL = self.c("INCL_P" if m == "p" else "INCL_S")
            STRICT = self.c("STRICT_P" if m == "p" else "STRICT_S")
            REM = self.c("REM_P" if m == "p" else "REM_S")
            SEG = self.c("ONES")[:, 0:1] if m == "p" else self.c("SEG_S")
            nlev = 6 if m == "p" else 2
            self.front(i, c, h32)
            self.rw_xx(c, h32, r)
            if m == "p" and c == NCH - 1:
                M.cp(self.HL[:, :, 0:1], v3k(h32)[:, :, 127:128], en="pool")
            if m == "s":
                M.cp(self.HL[:, :, 1:17], v4k(h32)[:, :, :, 7], en="pool")
            self.rw_mixed(xT[0], h32, r, 0)
            psr = M.psb(0, 2)
            projx(psr, xT[0], 0)
            self.rw_mixed(xT[1], h32, r, 2)
            psk = M.psb(2, 2)
            projx(psk, xT[1], D)
            self.rw_mixed(xT[0], h32, r, 3)
            psv = M.psb(4, 2)
            projx(psv, xT[0], 2 * D)
            self.rw_mixed(xT[1], h32, r, 1)
            psu = M.psb(6)
            for kc in range(KC):
                M.mm(psu[0:64, 0:128], w1[:, kc, :], xT[1][:, kc * 128:(kc + 1) * 128],
                     start=(kc == 0), stop=(kc == KC - 1))
            M.act(tuT[0:64, :], psu[0:64, 0:128], AF.Tanh)
            self.rw_mixed(xT[0], h32, r, 4)
            psu2 = M.psb(7)
            for kc in range(KC):
                M.mm(psu2[0:64, 0:128], a1[:, kc, :], xT[0][:, kc * 128:(kc + 1) * 128],
                     start=(kc == 0), stop=(kc == KC - 1))
            M.cp(auT[0:64, :], psu2[0:64, 0:128], en="act")
            M.cp(r32, psr, en="act")
            M.cp(k32, psk, en="act")
            M.cp(vbf, psv, en="act")
            psw = M.psb(0, 2)
            psa = M.psb(2, 2)
            for j in range(2):
                M.mm(psw[:, j * 512:(j + 1) * 512], tuT[0:65, :], w2x[0:65, j * 512:(j + 1) * 512])
                M.mm(psa[:, j * 512:(j + 1) * 512], auT[0:65, :], a2x[0:65, j * 512:(j + 1) * 512])
            if DBG < 1:
                continue
            M.act(lw, psw, AF.Sigmoid)
            M.act(a32, psa, AF.Sigmoid)
            M.tt(kk, k32, bcs["rw_k_k"], ALU.mult)
            M.tt(t1, kk, kk, ALU.mult)
            M.red(st[:, 0:16], vh(t1))
            M.act(st[:, 0:16], st[:, 0:16], AF.Sqrt)
            M.ts(st[:, 0:16], st[:, 0:16], 1e-12, None, ALU.max)
            M.recip(st[:, 0:16], st[:, 0:16])
            M.tt(vh(kk), vh(kk), st[:, 0:16].us(2).bc([128, 16, 64]), ALU.mult)
            M.stt(t1, a32, -1.0, bcs["rw_k_a"], ALU.add, ALU.mult)
            M.stt(k32, t1, 1.0, k32, ALU.add, ALU.mult)
            M.tt(a32, kk, a32, ALU.mult)
            M.tt(t1, r32, k32, ALU.mult)
            M.tt(t1, t1, bcs["rw_r_k"], ALU.mult)
            M.red(st[:, 16:32], vh(t1))
            psB = M.psb(4, 2)
            psX = M.psb(6, 2)
            psR = M.psb(0, 2)
            for j in range(2):
                sl = slice(j * 512, (j + 1) * 512)
                M.mm(psB[:, sl], INCL, lw[:, sl])
                M.mm(psX[:, sl], STRICT, lw[:, sl])
                M.mm(psR[:, sl], REM, lw[:, sl])
            M.act(E, psB, AF.Exp, scale=CDEC)
            M.tt(rt, r32, E, ALU.mult)
            M.act(E, psB, AF.Exp, scale=-CDEC)
            M.tt(kt_, k32, E, ALU.mult)
            M.tt(bt, a32, E, ALU.mult)
            M.act(E, psX, AF.Exp, scale=CDEC)
            M.stt(at, kk, -1.0, E, ALU.mult, ALU.mult)
            psT = M.psb(2, 2).bitcast(BF16)
            psT2 = M.psb(4, 2).bitcast(BF16)
            for pr in range(8):
                bl = slice(pr * 128, (pr + 1) * 128)
                M.tp(psT[:, (pr * 2) * 128:(pr * 2 + 1) * 128], at[:, bl], IDb)
                M.tp(psT[:, (pr * 2 + 1) * 128:(pr * 2 + 2) * 128], rt[:, bl], IDb)
                M.tp(psT2[:, (pr * 2) * 128:(pr * 2 + 1) * 128], kt_[:, bl], IDb)
                M.tp(psT2[:, (pr * 2 + 1) * 128:(pr * 2 + 2) * 128], bt[:, bl], IDb)
            M.cp(ARt, psT, en="act")
            M.cp(KBt, psT2)
            M.act(E, psR, AF.Exp, scale=CDEC)
            M.tt(kh, k32, E, ALU.mult)
            M.tt(bh, a32, E, ALU.mult)
            psD = M.psb(6)
            for pr in range(8):
                M.mm(psD[:, pr * nseg:(pr + 1) * nseg], lw[:, pr * 128:(pr + 1) * 128], SEG)
            M.act(dcol[:, 0:8 * nseg], psD[:, 0:8 * nseg], AF.Exp, scale=CDEC)
            if DBG < 2.1:
                continue
            for hg in range(4):
                psA1 = M.psb(0, 2)
                psA2 = M.psb(2, 2)
                psM = M.psb(4, 2)
                for hh in range(4):
                    h = hg * 4 + hh
                    pr, pb = h // 2, (h % 2) * 64
                    KT_h = KBt[pb:pb + 64, (pr * 2) * 128:(pr * 2 + 1) * 128]
                    BT_h = KBt[pb:pb + 64, (pr * 2 + 1) * 128:(pr * 2 + 2) * 128]
                    AR_h = ARt[pb:pb + 64, pr * 256:(pr + 1) * 256]
                    AT_h = ARt[pb:pb + 64, (pr * 2) * 128:(pr * 2 + 1) * 128]
                    ca = (hh % 2) * 512 + (hh // 2) * 256
                    cm = (hh % 2) * 512 + (hh // 2) * 128
                    M.mm(psA1[:, ca:ca + 256], KT_h, AR_h)
                    M.mm(psA2[:, ca:ca + 256], BT_h, AR_h)
                    M.mm(psM[:, cm:cm + 128], AT_h, BT_h)
                if DBG < 2.2:
                    continue
                va = lambda t, a: t.rr("p (h2 hp a t) -> p h2 hp a t", h2=2, hp=2, a=2)[:, :, :, a, :]
                vo = lambda t: t.rr("p (hp h2) t -> p h2 hp t", h2=2)
                hs = slice(hg * 4, (hg + 1) * 4)
                sb4 = STRICT.us(1).us(1).bc([128, 2, 2, 128])
                ib4 = INCL.us(1).us(1).bc([128, 2, 2, 128])
                P, PT, X = NS[0]
                M.tt(vo(AK[:, hs, :]), va(psA1, 0), sb4, ALU.mult)
                M.tt(vo(RK[:, hs, :]), va(psA1, 1), ib4, ALU.mult)
                M.tt(vo(P), va(psA2, 0), sb4, ALU.mult)
                M.tt(vo(RB[:, hs, :]), va(psA2, 1), ib4, ALU.mult)
                vm = psM.rr("p (h2 x) -> p h2 x", h2=2)[:, :, 0:256].rr("p h2 (hp t) -> p h2 hp t", hp=2)
                M.tt(vo(PT), vm, REM.us(1).us(1).bc([128, 2, 2, 128]), ALU.mult)
                if DBG < 2.4:
                    continue
                M.tt(X, P, IDb.us(1).bc([128, 4, 128]), ALU.add, en="pool")
                if DBG < 2.6:
                    continue
                cur = 0
                for lev in range(1, nlev + 1):
                    lastl = (lev == nlev)
                    Pn, PTn, Xn = NS[1 - cur]
                    psP = M.psb(6)
                    psPT = M.psb(7)
                    psXn = M.psb(0)
                    for hh in range(4):
                        bl = slice(hh * 128, (hh + 1) * 128)
                        if not lastl:
                            M.mm(psP[:, bl], PT[:, hh, :], P[:, hh, :])
                        M.mm(psPT[:, bl], P[:, hh, :], PT[:, hh, :])
                    if not lastl:
                        M.cp(Pn.rr("p h t -> p (h t)"), psP, en="act")
                    M.cp(PTn.rr("p h t -> p (h t)"), psPT)
                    for hh in range(4):
                        bl = slice(hh * 128, (hh + 1) * 128)
                        M.mm(psXn[:, bl], PTn[:, hh, :], X[:, hh, :])
                    M.tt(Xn.rr("p h t -> p (h t)"), psXn, X.rr("p h t -> p (h t)"), ALU.add)
                    P, PT, X = Pn, PTn, Xn
                    cur = 1 - cur
                M.cp(TT[:, hs, :], X, en="pool")
            if DBG < 3:
                continue
            psX1 = M.psb(0, 2)
            psU = M.psb(2, 2)
            psO = M.psb(4, 2)

            def scan(prs, smp):
                heads = [2 * pr + h2 for pr in prs for h2 in (0, 1)]
                cs = slice(heads[0] * 64, (heads[-1] + 1) * 64)
                for h in heads:
                    pr, pb = h // 2, (h % 2) * 64
                    hb = slice(h * 64, (h + 1) * 64)
                    if not smp:
                        M.mm(psX1[:, hb], ARt[pb:pb + 64, (pr * 2) * 128:(pr * 2 + 1) * 128],
                             Sbf[pb:pb + 64, pr * 64:(pr + 1) * 64], start=True, stop=False)
                    else:
                        for b in range(NB):
                            M.mm(psX1[:, hb], XP[0][pb:pb + 64, b * 128:(b + 1) * 128],
                                 Sb_[pb:pb + 64, b * 64:(b + 1) * 64], start=(b == 0), stop=False)
                    M.mm(psX1[:, hb], AK[:, h, :], vbf[:, hb], start=False, stop=True)
                M.cp(X1[:, cs], psX1[:, cs], en="act")
                for h in heads:
                    hb = slice(h * 64, (h + 1) * 64)
                    M.mm(psU[:, hb], TT[:, h, :], X1[:, hb])
                M.cp(U[:, cs], psU[:, cs])
                for h in heads:
                    pr, pb = h // 2, (h % 2) * 64
                    hb = slice(h * 64, (h + 1) * 64)
                    if not smp:
                        M.mm(psO[:, hb], ARt[pb:pb + 64, (pr * 2 + 1) * 128:(pr * 2 + 2) * 128],
                             Sbf[pb:pb + 64, pr * 64:(pr + 1) * 64], start=True, stop=False)
                    else:
                        for b in range(NB):
                            M.mm(psO[:, hb], XP[1][pb:pb + 64, b * 128:(b + 1) * 128],
                                 Sb_[pb:pb + 64, b * 64:(b + 1) * 64], start=(b == 0), stop=False)
                    M.mm(psO[:, hb], RB[:, h, :], U[:, hb], start=False, stop=False)
                    M.mm(psO[:, hb], RK[:, h, :], vbf[:, hb], start=False, stop=True)
                if not smp:
                    psS = M.psb(6)
                    for h in heads:
                        pr, pb = h // 2, (h % 2) * 64
                        hb = slice(h * 64, (h + 1) * 64)
                        ob = psS[pb:pb + 64, pr * 64:(pr + 1) * 64]
                        M.mm(ob, bh[:, hb], U[:, hb], start=True, stop=False)
                        M.mm(ob, kh[:, hb], vbf[:, hb], start=False, stop=True)
                    v8 = lambda t: t.rr("p (a x) -> p a x", a=8)
                    M.tt(v8(S32), v8(S32), dcol[:, 0:8].us(2).bc([128, 8, 64]), ALU.mult)
                    M.tt(S32, S32, psS, ALU.add)
                    M.cp(Sbf, S32, en="pool")
                else:
                    pr = prs[0]
                    psS = M.psb(6, 2)
                    for h2 in (0, 1):
                        h = 2 * pr + h2
                        pb = h2 * 64
                        hb = slice(h * 64, (h + 1) * 64)
                        for b in range(NB):
                            ob = psS[pb:pb + 64, b * 64:(b + 1) * 64]
                            M.mm(ob, XP[2][:, b * 128 + pb:b * 128 + pb + 64], U[:, hb], start=True, stop=False)
                            M.mm(ob, XP[3][:, b * 128 + pb:b * 128 + pb + 64], vbf[:, hb], start=False, stop=True)
                    v16 = lambda t: t.rr("p (b x) -> p b x", b=NB)
                    M.tt(v16(Sf), v16(Sf), dcol[:, pr * NB:(pr + 1) * NB].us(2).bc([128, NB, 64]), ALU.mult)
                    M.tt(Sf, Sf, psS, ALU.add)
                    M.dma(O["rw_state_s"][:, pr].rearrange("b p v -> p b v"), v16(Sf))

            if m == "p":
                scan(list(range(8)), False)
            else:
                segb = self.cb("SEG_S").us(2).bc([128, NB, 128])
                for pr in range(8):
                    M.dma(vbx(Sf), I["st_rw"][:, pr].rearrange("b p v -> p b v"))
                    M.cp(Sb_, Sf, en="act")
                    M.tt(vbx(XP[0]), ARt[:, (pr * 2) * 128:(pr * 2 + 1) * 128].us(1).bc([128, NB, 128]),
                         vbx(self.BMb), ALU.mult, en="pool")
                    M.tt(vbx(XP[1]), ARt[:, (pr * 2 + 1) * 128:(pr * 2 + 2) * 128].us(1).bc([128, NB, 128]),
                         vbx(self.BMb), ALU.mult, en="pool")
                    M.tt(vbx(XP[2]), bh[:, pr * 128:(pr + 1) * 128].us(1).bc([128, NB, 128]), segb, ALU.mult)
                    M.tt(vbx(XP[3]), kh[:, pr * 128:(pr + 1) * 128].us(1).bc([128, NB, 128]), segb, ALU.mult)
                    scan([pr], True)
            if DBG < 4:
                continue
            M.cp(o32, psO, en="act")
            M.act(sq, psO, AF.Square)
            s1, s2, mean, var = st[:, 32:48], st[:, 48:64], st[:, 64:80], st[:, 48:64]
            M.red(s1, vh(o32))
            M.red(s2, vh(sq))
            M.ts(mean, s1, 1.0 / 64, None, ALU.mult)
            M.tt(s1, mean, mean, ALU.mult)
            M.stt(var, s2, 1.0 / 64, s1, ALU.mult, ALU.subtract)
            M.rsqrt(var, var, 1.0, GN_EPS)
            M.tt(vh(o32), vh(o32), mean.us(2).bc([128, 16, 64]), ALU.subtract)
            M.tt(vh(o32), vh(o32), var.us(2).bc([128, 16, 64]), ALU.mult)
            M.tt(o32, o32, bcs["rw_gn_w"], ALU.mult)
            M.tt(o32, o32, bcs["rw_gn_b"], ALU.add)
            M.tt(vh(sq), vh(vbf), st[:, 16:32].us(2).bc([128, 16, 64]), ALU.mult)
            M.tt(o32, o32, sq, ALU.add)
            M.dma(self.OS[c], o32)
        M.dma(O["rw_state_p"].rearrange("a p v -> p a v"), S32.rr("p (a x) -> p a x", a=8))
        ps = M.psb(0, 2)
        for kc in range(KC):
            M.tp(ps[0:17, kc * 128:(kc + 1) * 128], self.HL[:, kc, :], self.c("ID"))
        M.cp(hlo, ps[0:17, :])
        M.dma(O["rw_shift"], hlo)
        M.phase_end()

    def sb_phase1(self, i):
        M, I, O = self.M, self.I, self.O
        NCH, NPG = self.NCH, self.NPG
        NT = NCH * 128
        M.phase_begin()
        stage = [M.sb([128, D], F32, "stage") for _ in range(2)]
        W = M.sb([128, 8, 3 * D], BF16, "W")
        self.load_w(W, I["sb_w_in"][:, 0:3 * D], stage)
        b16 = M.sb([65, 16], F32, "b16")
        M.dma(b16[0:1, :], I["sb_bias"])
        M.dma(b16[64:65, :], I["sb_bias"])
        brow = M.sb([65, 16 * 128], BF16, "brow")
        for pb in (0, 64):
            M.cp(brow[pb:pb + 1, :].rr("p (h t) -> p h t", h=16), b16[pb:pb + 1, :].us(2).bc([1, 16, 128]))
        onesb = self.cb("ONES")
        NUPI = self.cb("UPI")
        NONES = self.cb("NONES")
        STRb = self.cb("STRICT_P")
        hT = M.sb([128, D], BF16, "hT")
        q_bf = M.sb([128, D], BF16, "q_bf")
        k_bf = M.sb([128, D], BF16, "k_bf")
        k32, v32 = stage
        vbf = M.sb([128, D], BF16, "vbf")
        qT = M.sb([128, D], BF16, "qT")
        kTc = M.sb([128, D], BF16, "kTc")
        Kb = [M.sb([128, D], BF16, "Kb") for _ in range(2)]
        Vb = [M.sb([128, D], BF16, "Vb") for _ in range(2)]
        eb = [M.sb([128, D], F32, "e") for _ in range(2)]
        spb = [M.sb([128, D], BF16, "sp") for _ in range(2)]
        wb = [M.sb([128, D], BF16, "w") for _ in range(2)]
        e, sp, w = eb[0], spb[0], wb[0]
        Cs32 = [M.sb([128, D], F32, "Cs32") for _ in range(2)]
        Csb = [M.sb([128, D], BF16, "Csb") for _ in range(2)]
        o32 = M.sb([128, D], F32, "o32")
        kp32 = [M.sb([128, D], F32, "kp32") for _ in range(2)]
        vp32 = [M.sb([128, D], F32, "vp32") for _ in range(2)]
        kpb = M.sb([128, D], BF16, "kpb")
        vh8 = lambda t: t.rr("p (h t) -> p h t", h=8)
        if DBG > 45:
            pt_i = M.sb([128, NB * NPG], I32, "pt_i")
            M.dma(pt_i, I["ptab"][0].partition_broadcast(128))
            pt_f = M.sb([128, NB * NPG], F32, "pt_f")
            M.cp(pt_f, pt_i)
            M.ts(pt_f, pt_f, 128.0, self.c("IOTA"), ALU.mult, ALU.add)
            self.rows = M.sb([128, NB * NPG], I32, "rows")
            M.cp(self.rows, pt_f)

        def proj(ps, c0):
            for kc in range(KC):
                for j in range(2):
                    M.mm(ps[:, j * 512:(j + 1) * 512], hT[:, kc * 128:(kc + 1) * 128],
                         W[:, kc, c0 + j * 512:c0 + (j + 1) * 512], start=(kc == 0), stop=(kc == KC - 1))

        nk = 0
        for c in self.chunks():
            m = self.mode(c)
            self.front(i, c, hT)
            psq = M.psb(0, 2)
            proj(psq, 0)
            psk = M.psb(2, 2)
            proj(psk, D)
            psv = M.psb(4, 2)
            proj(psv, 2 * D)
            M.cp(k32, psk, en="act")
            M.cp(v32, psv, en="act")
            if m == "p":
                M.dma(O["sb_k_p"][c * 128:(c + 1) * 128, :], k32)
                M.dma(O["sb_v_p"][c * 128:(c + 1) * 128, :], v32)
            else:
                M.dma(O["sb_k_s"], k32)
                M.dma(O["sb_v_s"], v32)
            if DBG < 25:
                continue
            M.cp(vbf, psv)
            M.ts(q_bf, psq, 0.125, None, ALU.mult)
            M.cp(k_bf, psk)
            if DBG < 26:
                continue
            psTq = M.psb(6).bitcast(BF16)
            psTk = M.psb(7).bitcast(BF16)
            for pr in range(8):
                bl = slice(pr * 128, (pr + 1) * 128)
                M.tp(psTq[:, bl], q_bf[:, bl], self.cb("ID"))
                M.tp(psTk[:, bl], k_bf[:, bl], self.cb("ID"))
            M.cp(qT, psTq, en="act")
            M.cp(kTc, psTk)
            if DBG < 27:
                continue
            if m == "p":
                M.dma(self.VBS[c], vbf)
                M.dma(self.KTS[c], kTc)
                psO = M.psb(4, 2)
                ZB = [M.psb(0, 2), M.psb(2, 2)]
                colz = lambda hl: (hl % 2) * 512 + (hl // 2) * 128
                str8 = STRb.us(1).bc([128, 8, 128])
                for j in range(c, -1, -1):
                    diag = (j == c)
                    if diag:
                        Kb_, Vb_ = kTc, vbf
                    else:
                        Kb_, Vb_ = Kb[nk % 2], Vb[nk % 2]
                        nk += 1
                        M.dma(Kb_, self.KTS[j])
                        M.dma(Vb_, self.VBS[j])
                    for hh in range(2):
                        for hl in range(8):
                            h = hh * 8 + hl
                            pr, pb = h // 2, (h % 2) * 64
                            ob = ZB[hh][:, colz(hl):colz(hl) + 128]
                            M.mm(ob, Kb_[pb:pb + 64, pr * 128:(pr + 1) * 128], qT[pb:pb + 64, pr * 128:(pr + 1) * 128],
                                 start=(hl < 2), stop=False)
                            M.mm(ob, onesb[pb:pb + 1, 0:128], brow[pb:pb + 1, h * 128:(h + 1) * 128],
                                 start=False, stop=(hl >= 6))
                    for hh in range(2):
                        M.act(eb[hh], ZB[hh], AF.Exp)
                        M.act(spb[hh], eb[hh], AF.Ln, bias=1.0)
                        if diag:
                            M.tt(vh8(spb[hh]), vh8(spb[hh]), str8, ALU.mult)
                    for hh in range(2):
                        for bk in range(2):
                            bl = slice(bk * 512, (bk + 1) * 512)
                            M.mm(ZB[hh][:, bl], NUPI, spb[hh][:, bl], start=False, stop=diag, nogrp=True)
                            if not diag:
                                M.mm(ZB[hh][:, bl], NONES, Csb[hh][:, bl], start=False, stop=True, nogrp=True)
                    for hh in range(2):
                        M.act(wb[hh], ZB[hh], AF.Exp)
                        if diag:
                            M.tt(vh8(wb[hh]), vh8(wb[hh]), str8, ALU.mult)
                    for hh in range(2):
                        for hl in range(8):
                            h = hh * 8 + hl
                            M.mm(psO[:, h * 64:(h + 1) * 64], wb[hh][:, colz(hl):colz(hl) + 128],
                                 Vb_[:, h * 64:(h + 1) * 64], start=(diag and hl == 0),
                                 stop=(j == 0 and hl == 7))
                    if j > 0:
                        for hh in range(2):
                            if diag:
                                M.cp(Cs32[hh], spb[hh])
                            else:
                                M.tt(Cs32[hh], Cs32[hh], spb[hh], ALU.add)
                            M.cp(Csb[hh], Cs32[hh], en="pool")
                M.cp(o32, psO, en="act")
                M.dma(self.OS[c], o32)
            elif DBG < 50:
                pass
            else:
                brs = M.sb([65, 128], BF16, "brs")
                for pb in (0, 64):
                    M.cp(brs[pb:pb + 1, :].rr("p (h t) -> p h t", h=16), b16[pb:pb + 1, :].us(2).bc([1, 16, TS]))
                mbA = self.cb("MASKB")
                v2 = lambda t: t.rr("p (a x) -> p a x", a=2)
                v16 = lambda t: t.rr("p (h t) -> p h t", h=16)
                pz = lambda ps: ps.rr("p (a x) -> p a x", a=2)[:, :, 0:64]
                cc = lambda h: (h % 2) * 64 + (h // 2) * TS
                cp_ = lambda h: (h % 2) * 512 + (h // 2) * TS
                c32, cbf = Cs32[0][:, 0:128], Csb[0][:, 0:128]
                ZS = [M.psb(0, 2), M.psb(2, 2)]
                psTs = [M.psb(6).bitcast(BF16), M.psb(7).bitcast(BF16)]
                e2 = [M.sb([128, 128], F32, "e2") for _ in range(2)]
                sp2 = [M.sb([128, 128], BF16, "sp2") for _ in range(2)]
                w2 = [M.sb([128, 128], BF16, "w2") for _ in range(2)]
                kpb2 = [kpb, M.sb([128, D], BF16, "kpb2")]
                ob32 = o32[0:TS, :]
                psOb = M.psb(4, 2)
                blocks = ["new"] + list(range(NPG - 1, -1, -1))
                units = [(b, bi) for b in range(NB) for bi in range(len(blocks))]
                kv = {}

                def prep_qk(u):
                    b, bi = units[u]
                    blk = blocks[bi]
                    p = u % 2
                    if blk == "new":
                        Kb_, Vb_ = kTc, vbf
                    else:
                        Kb_, Vb_ = Kb[p], Vb[p]
                        self.page_dma(kp32[p], I["ck"], b * NPG + blk)
                        self.page_dma(vp32[p], I["cv"], b * NPG + blk)
                        M.cp(kpb2[p], kp32[p])
                        for pr in range(8):
                            bl = slice(pr * 128, (pr + 1) * 128)
                            M.tp(psTs[p][:, bl], kpb2[p][:, bl], self.cb("ID"))
                        M.cp(Kb_, psTs[p], en="act")
                        M.cp(Vb_, vp32[p])
                    kv[u] = Vb_
                    for h in range(16):
                        pr, pb = h // 2, (h % 2) * 64
                        ob = ZS[p][:, cp_(h):cp_(h) + TS]
                        M.mm(ob, Kb_[pb:pb + 64, pr * 128:(pr + 1) * 128],
                             qT[pb:pb + 64, pr * 128 + b * TS:pr * 128 + (b + 1) * TS], start=(h < 2), stop=False)
                        M.mm(ob, onesb[pb:pb + 1, 0:128], brs[pb:pb + 1, h * TS:(h + 1) * TS],
                             start=False, stop=(h >= 14))

                def mask(t, u):
                    b, bi = units[u]
                    if bi == 0:
                        mb = mbA[:, b * TS:(b + 1) * TS].us(1).bc([128, 16, TS])
                        M.tt(v16(t), v16(t), mb, ALU.mult)

                def act1(u):
                    p = u % 2
                    M.act(v2(e2[p]), pz(ZS[p]), AF.Exp)
                    M.act(sp2[p], e2[p], AF.Ln, bias=1.0)
                    mask(sp2[p], u)

                def tail(u):
                    p = u % 2
                    new = units[u][1] == 0
                    for a in range(2):
                        M.mm(ZS[p][:, a * 512:a * 512 + 64], NUPI, sp2[p][:, a * 64:(a + 1) * 64],
                             start=False, stop=new, nogrp=True)
                        if not new:
                            M.mm(ZS[p][:, a * 512:a * 512 + 64], NONES, cbf[:, a * 64:(a + 1) * 64],
                                 start=False, stop=True, nogrp=True)

                def act2(u):
                    p = u % 2
                    M.act(v2(w2[p]), pz(ZS[p]), AF.Exp)
                    mask(w2[p], u)

                def csum(u):
                    p = u % 2
                    bi = units[u][1]
                    if bi == len(blocks) - 1:
                        return
                    if bi == 0:
                        M.cp(c32, sp2[p])
                    else:
                        M.tt(c32, c32, sp2[p], ALU.add)
                    M.cp(cbf, c32)

                def pv(u):
                    p = u % 2
                    b, bi = units[u]
                    new, lastb = (bi == 0), (bi == len(blocks) - 1)
                    Vb_ = kv.pop(u)
                    for h in range(16):
                        M.mm(psOb[0:TS, h * 64:(h + 1) * 64], w2[p][:, cc(h):cc(h) + TS],
                             Vb_[:, h * 64:(h + 1) * 64], start=(new and h % 8 == 0),
                             stop=(lastb and h % 8 == 7))
                    if lastb:
                        M.cp(ob32, psOb[0:TS, :], en="act")
                        M.dma(V(self.os_ap[NT + b * TS:NT + (b + 1) * TS, :], self.OS[NCH].trs), ob32)

                n = len(units)
                prep_qk(0)
                for u in range(n):
                    act1(u)
                    if u >= 1:
                        pv(u - 1)
                    if u + 1 < n:
                        prep_qk(u + 1)
                    tail(u)
                    act2(u)
                    csum(u)
                pv(n - 1)
        M.phase_end()

    def page_dma(self, out, pool_ap, slot):
        M = self.M
        e = M.E["pool"]
        tr0 = out.trs[0]
        if tr0.dsem is None:
            tr0.dsem = self.nc.alloc_semaphore("gsem_%d" % M.nsem)
            M.nsem += 1
            M.all_dsems.append(tr0)
            M.dsem_map[tr0.dsem.num] = tr0
        for tr in self.rows.trs:
            M._wait(e, tr.w)
        for tr in out.trs:
            if tr.w is not None and tr.w[0] is not tr0.dsem:
                M._wait(e, tr.w)
            for ev in tr.r:
                M._wait(e, ev)
        inst = e.h.indirect_dma_start(
            out=out.ap, out_offset=None, in_=pool_ap.rearrange("n p f -> (n p) f"),
            in_offset=bass.IndirectOffsetOnAxis(ap=self.rows.ap[:, slot:slot + 1], axis=0))
        tr0.dcnt += 16
        inst.then_inc(tr0.dsem, 16)
        M._commit((tr0.dsem, tr0.dcnt), [self.rows], [out])

    def build(self):
        self.setup()
        for i in self.LAYERS:
            self.layer_setup(i)
            kind = i % 4
            if kind == 0:
                self.rw_phase1(i)
            elif kind == 2:
                self.sb_phase1(i)
            else:
                self.la_phase1(i)
            self.phase2(i)
        self.M.finish()


def make_in_maps(inp, cfg):
    NCH, NPG = cfg["NCH"], cfg["NPG"]
    f = lambda a: np.ascontiguousarray(np.asarray(a, dtype=np.float32))
    shared = {}
    for n in ["norm_g", "ada_w", "ada_b"]:
        shared[n] = f(inp[n])
    shared["final_g"] = f(inp["final_g"]).reshape(1, D)
    shared["hg_lower"] = f(inp["hg_lower"])
    for n in ["rw_mix", "rw_w_rkvg", "rw_w0", "rw_w1", "rw_w2", "rw_a0", "rw_a1", "rw_a2", "rw_k_k", "rw_k_a",
              "rw_r_k", "rw_gn_w", "rw_gn_b", "rw_w_o", "gla_w_in", "gla_w_gk2", "gla_b_gk2", "gla_gn_w",
              "gla_w_o", "sb_w_in", "sb_bias", "sb_w_o", "hg_w_in", "hg_gn_w", "hg_w_o"]:
        a = f(inp[n])[0]
        if a.ndim == 1:
            a = a.reshape(1, -1)
        shared[n] = np.ascontiguousarray(a)
    shared["constA"] = CONST_A
    shared["constB"] = CONST_B
    ck = f(inp["cache_sb_k"])[0].reshape(-1, 128, D)
    cv = f(inp["cache_sb_v"])[0].reshape(-1, 128, D)
    shared["ck"] = ck[:cfg["NPOOL"]]
    shared["cv"] = cv[:cfg["NPOOL"]]
    maps = []
    for c in range(NCORE):
        sl = slice(c * NB, (c + 1) * NB)
        m = dict(shared)
        m["xp"] = f(inp["x_prompt"][c // 2]).reshape(NCH * 128, D)
        m["xs"] = f(inp["x_sample"][sl]).reshape(NB * TS, D)
        m["c17"] = np.concatenate([f(inp["c_prompt"][c // 2]).reshape(1, D), f(inp["c_sample"][sl])], axis=0)
        s = f(inp["state_rwkv"][0, sl])
        m["st_rw"] = np.ascontiguousarray(s.transpose(0, 1, 3, 2).reshape(NB, 8, 128, 64))
        m["sh_rw"] = f(inp["cache_rwkv_shift"][0, sl])
        m["st_gla"] = f(inp["state_gla"][0, sl])
        m["st_hg"] = f(inp["state_hgrn"][0, sl])
        m["ptab"] = np.ascontiguousarray(np.asarray(inp["page_table"], dtype=np.int32)[sl].reshape(1, NB * NPG))
        maps.append(m)
    return maps


def assemble(res, cfg, nseq_prompt):
    NCH = cfg["NCH"]
    T = NCH * 128
    ev = [res[2 * s] for s in range(nseq_prompt)]
    y_p = np.stack([r["y_p"].reshape(T, D) for r in ev])
    y_s = np.concatenate([r["y_s"].reshape(NB, TS, D) for r in res], axis=0)

    def rwst(a):
        sh = a.shape[:-3]
        a = a.reshape(*sh, 8, 2, 64, 64).reshape(*sh, 16, 64, 64)
        return np.ascontiguousarray(np.swapaxes(a, -1, -2))
    rw_state_p = np.stack([rwst(r["rw_state_p"]) for r in ev])[None]
    rw_shift_p = np.stack([r["rw_shift"][0] for r in ev])[None]
    gla_state_p = np.stack([r["gla_state_p"] for r in ev])[None]
    sb_k_p = np.stack([r["sb_k_p"].reshape(T, 16, 64) for r in ev])[None]
    sb_v_p = np.stack([r["sb_v_p"].reshape(T, 16, 64) for r in ev])[None]
    hg_state_p = np.stack([r["hg_state_p"] for r in ev])[None]
    rw_state_s = np.concatenate([rwst(r["rw_state_s"]) for r in res], axis=0)[None]
    rw_shift_s = np.concatenate([r["rw_shift"][1:17] for r in res], axis=0)[None]
    gla_state_s = np.concatenate([r["gla_state_s"] for r in res], axis=0)[None]
    sb_k_s = np.concatenate([r["sb_k_s"].reshape(NB, TS, 16, 64) for r in res], axis=0)[None]
    sb_v_s = np.concatenate([r["sb_v_s"].reshape(NB, TS, 16, 64) for r in res], axis=0)[None]
    hg_state_s = np.concatenate([r["hg_state_s"] for r in res], axis=0)[None]
    outs = (y_p, y_s, rw_state_p, rw_shift_p, gla_state_p, sb_k_p, sb_v_p, hg_state_p,
            rw_state_s, rw_shift_s, gla_state_s, sb_k_s, sb_v_s, hg_state_s)
    return tuple(np.ascontiguousarray(o, dtype=np.float32) for o in outs)


def run(inp, cfg):
    prog = Prog(cfg)
    maps = make_in_maps(inp, cfg)
    r = run_bass_kernel_spmd(prog.nc, maps, core_ids=list(range(NCORE)))
    return assemble(r.results, cfg, np.asarray(inp["x_prompt"]).shape[0])


def kernel(**inputs):
    T = np.asarray(inputs["x_prompt"]).shape[1]
    npg = np.asarray(inputs["page_table"]).shape[1]
    npool = np.asarray(inputs["cache_sb_k"]).shape[1]
    cfg = {"NCH": T // 128, "NPG": npg, "NPOOL": npool, "LAYERS": (0, 1, 2, 3)}
    return run(inputs, cfg)
```

```python
import numpy as np
import concourse.bass as bass
import concourse.mybir as mybir
from concourse.bass_utils import run_bass_kernel_spmd

F32 = mybir.dt.float32
BF16 = mybir.dt.bfloat16
I32 = mybir.dt.int32
AF = mybir.ActivationFunctionType
ALU = mybir.AluOpType
AX = mybir.AxisListType

import os
DBG = float(os.environ.get("RWDBG", "99"))
D = 1024
KC = 8
NCORE = 8
NB = 16
TS = 8
EPS = 1e-6
GN_EPS = 64e-5
CDEC = -0.6065306597126334


class Tr:
    __slots__ = ("w", "r", "dsem", "dcnt", "x")

    def __init__(self, x=False):
        self.x = x
        self.w = None
        self.r = []
        self.dsem = None
        self.dcnt = 0


class V:
    def __init__(self, ap, trs):
        self.ap = ap
        self.trs = trs

    def __getitem__(self, k):
        return V(self.ap[k], self.trs)

    def rr(self, pat, **kw):
        return V(self.ap.rearrange(pat, **kw), self.trs)

    def bc(self, shape):
        return V(self.ap.broadcast_to(list(shape)), self.trs)

    def us(self, ax):
        return V(self.ap.unsqueeze(ax), self.trs)

    def bitcast(self, dt):
        return V(self.ap.bitcast(dt), self.trs)

    @property
    def shape(self):
        return tuple(self.ap.shape)


class Eng:
    def __init__(self, name, h, sem):
        self.name = name
        self.h = h
        self.sem = sem
        self.cnt = 0
        self.waited = {}


class MK:
    def __init__(self, nc):
        self.nc = nc
        self.E = {}
        for name, h in (("pe", nc.tensor), ("dve", nc.vector), ("act", nc.scalar),
                        ("pool", nc.gpsimd), ("sp", nc.sync)):
            self.E[name] = Eng(name, h, nc.alloc_semaphore("sem_" + name))
        self.nsem = 5
        self.uid = 0
        self.dram_trs = []
        self.all_dsems = []
        self.dsem_map = {}
        self.free_dsems = []
        self.stack = None
        self.phase_trs = []
        self.psum = nc.alloc_psum_tensor("psum", [128, 4096], F32)
        self.ps_tr = [Tr(True) for _ in range(8)]
        self.ps_ptr = 0

    def sb(self, shape, dt, name=None):
        self.uid += 1
        nm = "%s_%d" % (name or "t", self.uid)
        if self.stack is not None:
            t = self.stack.enter_context(self.nc.sbuf_tensor(nm, list(shape), dt))
        else:
            t = self.nc.alloc_sbuf_tensor(nm, list(shape), dt)
        tr = Tr()
        if self.stack is not None:
            self.phase_trs.append(tr)
        return V(t[tuple(slice(None) for _ in shape)], (tr,))

    def phase_begin(self):
        import contextlib
        assert self.stack is None
        self.stack = contextlib.ExitStack()
        self.phase_trs = []

    def phase_end(self):
        self.finish()
        for tr in self.phase_trs:
            if tr.dsem is not None:
                if not tr.dsem.name.startswith("gsem"):
                    self.free_dsems.append((tr.dsem, tr.dcnt))
                self.all_dsems.remove(tr)
                del self.dsem_map[tr.dsem.num]
                tr.dsem = None
        for tr in self.ps_tr:
            tr.w = None
            tr.r = []
        self.stack.close()
        self.stack = None
        self.phase_trs = []

    def dram(self, name, shape, dt, kind="Internal"):
        t = self.nc.dram_tensor(name, list(shape), dt, kind=kind)
        return t.ap()

    def dv(self, ap):
        return V(ap, (Tr(),))

    def psb(self, b, n=1):
        assert b + n <= 8
        return V(self.psum[:, b * 512:(b + n) * 512], tuple(self.ps_tr[b:b + n]))

    def _wait(self, e, ev):
        if ev is None:
            return
        sem, val = ev
        trd = self.dsem_map.get(sem.num)
        if trd is not None:
            val = trd.dcnt
        if e.waited.get(sem.num, 0) >= val:
            return
        e.h.wait_ge(sem, val)
        e.waited[sem.num] = val

    def _deps(self, e, reads, writes):
        for v in reads:
            for tr in v.trs:
                if tr.w is not None and not (e.name == "pe" and tr.w[0] is e.sem):
                    self._wait(e, tr.w)
                if tr.x:
                    for ev in tr.r:
                        if ev[0] is not e.sem:
                            self._wait(e, ev)
        pe = e.name == "pe"
        for v in writes:
            for tr in v.trs:
                if tr.w is not None and not (pe and tr.w[0] is e.sem):
                    self._wait(e, tr.w)
                for ev in tr.r:
                    if not (pe and ev[0] is e.sem):
                        self._wait(e, ev)

    def _commit(self, ev, reads, writes):
        for v in writes:
            for tr in v.trs:
                tr.w = ev
                tr.r = []
        wset = set(id(tr) for v in writes for tr in v.trs)
        for v in reads:
            for tr in v.trs:
                if id(tr) not in wset:
                    tr.r.append(ev)
                    if len(tr.r) > 24:
                        best = {}
                        for s, val in tr.r:
                            if s.num not in best or best[s.num][1] < val:
                                best[s.num] = (s, val)
                        tr.r = list(best.values())

    def op(self, en, fn, reads, writes):
        e = self.E[en]
        reads = [v for v in reads if isinstance(v, V)]
        self._deps(e, reads, writes)
        inst = fn(e.h)
        e.cnt += 1
        inst.then_inc(e.sem, 1)
        self._commit((e.sem, e.cnt), reads, writes)

    def dma(self, out, in_, q="sp"):
        q = "sp"
        e = self.E[q]
        cand = [v for v in (out, in_) if isinstance(v, V) and not self._is_dram(v)]
        assert cand, "dma needs one SBUF side"
        tr0 = cand[0].trs[0]
        if tr0.dsem is None:
            if self.free_dsems:
                tr0.dsem, tr0.dcnt = self.free_dsems.pop()
            else:
                tr0.dsem = self.nc.alloc_semaphore("dsem_%d" % self.nsem)
                self.nsem += 1
            self.all_dsems.append(tr0)
            self.dsem_map[tr0.dsem.num] = tr0
        reads = [in_] if isinstance(in_, V) else []
        writes = [out] if isinstance(out, V) else []
        for v in reads:
            for tr in v.trs:
                if tr.w is not None:
                    self._wait(e, tr.w)
        for v in writes:
            for tr in v.trs:
                if tr.w is not None and tr.w[0] is not tr0.dsem:
                    self._wait(e, tr.w)
                for ev in tr.r:
                    self._wait(e, ev)
        oap = out.ap if isinstance(out, V) else out
        iap = in_.ap if isinstance(in_, V) else in_
        inst = e.h.dma_start(out=oap, in_=iap)
        tr0.dcnt += 16
        inst.then_inc(tr0.dsem, 16)
        self._commit((tr0.dsem, tr0.dcnt), reads, writes)

    @staticmethod
    def _is_dram(v):
        return str(v.ap.space) == "DRAM"

    def finish(self):
        evs = [(e.sem, e.cnt) for e in self.E.values() if e.cnt]
        evs += [(tr.dsem, tr.dcnt) for tr in self.all_dsems]
        for e in self.E.values():
            for ev in evs:
                if ev[0] is not e.sem:
                    self._wait(e, ev)

    def mm(self, out, lhsT, rhs, start=True, stop=True, nogrp=False):
        kw = {"skip_group_check": True} if nogrp else {}
        self.op("pe", lambda h: h.matmul(out.ap, lhsT.ap, rhs.ap, start=start, stop=stop, **kw),
                [lhsT, rhs] + ([] if start else [out]), [out])

    def tp(self, out, in_, ident):
        self.op("pe", lambda h: h.transpose(out.ap, in_.ap, ident.ap), [in_, ident], [out])

    def act(self, out, in_, func, bias=None, scale=None, accum=None, en="act"):
        kw = {}
        rd = [in_]
        if bias is not None:
            kw["bias"] = bias.ap if isinstance(bias, V) else bias
            rd.append(bias)
        if scale is not None:
            kw["scale"] = scale.ap if isinstance(scale, V) else scale
            rd.append(scale)
        wr = [out]
        if accum is not None:
            kw["accum_out"] = accum.ap
            wr.append(accum)
        self.op(en, lambda h: h.activation(out.ap, in_.ap, func, **kw), rd, wr)

    def tt(self, out, a, b, op, en="dve"):
        self.op(en, lambda h: h.tensor_tensor(out.ap, a.ap, b.ap, op), [a, b], [out])

    def ts(self, out, a, s1, s2, op0, op1=None, en="dve"):
        g = lambda s: s.ap if isinstance(s, V) else s
        if op1 is None:
            self.op(en, lambda h: h.tensor_scalar(out.ap, a.ap, g(s1), 0.0, op0, ALU.add), [a, s1], [out])
        else:
            self.op(en, lambda h: h.tensor_scalar(out.ap, a.ap, g(s1), g(s2), op0, op1),
                    [a, s1, s2], [out])

    def stt(self, out, a, s, b, op0, op1, en="dve"):
        g = lambda x: x.ap if isinstance(x, V) else x
        self.op(en, lambda h: h.scalar_tensor_tensor(out.ap, a.ap, g(s), b.ap, op0, op1),
                [a, s, b], [out])

    def cp(self, out, a, en="dve"):
        if en == "act":
            self.op(en, lambda h: h.activation(out.ap, a.ap, AF.Copy), [a], [out])
        else:
            self.op(en, lambda h: h.tensor_copy(out.ap, a.ap), [a], [out])

    def red(self, out, a, op=ALU.add, en="dve"):
        self.op(en, lambda h: h.tensor_reduce(out.ap, a.ap, AX.X, op), [a], [out])

    def rsqrt(self, out, a, scale, eps):
        self.act(out, a, AF.Sqrt, bias=eps, scale=scale)
        self.op("dve", lambda h: h.reciprocal(out.ap, out.ap), [out], [out])

    def recip(self, out, a):
        self.op("dve", lambda h: h.reciprocal(out.ap, a.ap), [a], [out])

    def memset(self, out, val, en="dve"):
        self.op(en, lambda h: h.memset(out.ap, val), [], [out])


def build_consts():
    p = np.arange(128)
    f = np.arange(128)
    same = (p[:, None] // TS) == (f[None, :] // TS)
    c = {}
    c["ID"] = (p[:, None] == f[None, :])
    c["INCL_P"] = (p[:, None] <= f[None, :])
    c["STRICT_P"] = (p[:, None] < f[None, :])
    c["REM_P"] = (p[:, None] > f[None, :])
    c["UPI"] = -1.0 * (p[:, None] >= f[None, :])
    c["NONES"] = -1.0 * np.ones((128, 128))
    c["ONES"] = np.ones((128, 128))
    c["INCL_S"] = c["INCL_P"] & same
    c["STRICT_S"] = c["STRICT_P"] & same
    c["REM_S"] = c["REM_P"] & same
    c["SEG_S"] = (p[:, None] // TS) == np.arange(NB)[None, :]
    sel = np.zeros((128, 256))
    sel[0, 0:128] = 1.0
    for b in range(NB):
        sel[1 + b, 128 + b * TS:128 + (b + 1) * TS] = 1.0
    c["SEL17"] = sel
    mb = np.zeros((128, NB, TS))
    for s in range(128):
        for t in range(TS):
            if (s % TS) < t:
                mb[s, s // TS, t] = 1.0
    c["MASKB"] = mb.reshape(128, NB * TS)
    c["IOTA"] = p[:, None].astype(np.float32)
    off = {}
    cols = []
    o = 0
    for k, v in c.items():
        v = np.asarray(v, np.float32)
        off[k] = (o, v.shape[1])
        cols.append(v)
        o += v.shape[1]
    A = np.concatenate(cols, axis=1).astype(np.float32)
    bm = np.zeros((128, NB, 128), np.float32)
    for b in range(NB):
        bm[:, b, b * TS:(b + 1) * TS] = 1.0
    return A, off, bm.reshape(128, NB * 128)


CONST_A, COFF, CONST_B = build_consts()
NCA = CONST_A.shape[1]


class Prog:
    def __init__(self, cfg):
        self.NCH = NCH = cfg["NCH"]
        self.NPG = NPG = cfg["NPG"]
        self.NPOOL = NPOOL = cfg["NPOOL"]
        self.LAYERS = tuple(cfg["LAYERS"])
        nc = self.nc = bass.Bass("TRN2", target_bir_lowering=False)
        M = self.M = MK(nc)
        NT = NCH * 128
        self.I = I = {}
        self.O = O = {}

        def di(n, sh, dt=F32):
            I[n] = nc.dram_tensor(n, list(sh), dt, kind="ExternalInput").ap()

        def do(n, sh, dt=F32):
            O[n] = nc.dram_tensor(n, list(sh), dt, kind="ExternalOutput").ap()

        for n, sh in [("xp", [NT, D]), ("xs", [128, D]), ("c17", [17, D]), ("st_rw", [NB, 8, 128, 64]),
                      ("sh_rw", [NB, D]), ("st_gla", [NB, 4, 128, 256]), ("st_hg", [NB, 8, 128, 128]),
                      ("ck", [NPOOL, 128, D]), ("cv", [NPOOL, 128, D]),
                      ("norm_g", [4, D]), ("ada_w", [4, D, 3 * D]), ("ada_b", [4, 3 * D]), ("final_g", [1, D]),
                      ("rw_mix", [6, D]), ("rw_w_rkvg", [4, D, D]), ("rw_w0", [1, D]), ("rw_w1", [D, 64]),
                      ("rw_w2", [64, D]), ("rw_a0", [1, D]), ("rw_a1", [D, 64]), ("rw_a2", [64, D]),
                      ("rw_k_k", [1, D]), ("rw_k_a", [1, D]), ("rw_r_k", [1, D]), ("rw_gn_w", [1, D]),
                      ("rw_gn_b", [1, D]), ("rw_w_o", [D, D]),
                      ("gla_w_in", [D, 3088]), ("gla_w_gk2", [16, 512]), ("gla_b_gk2", [1, 512]),
                      ("gla_gn_w", [1, 256]), ("gla_w_o", [D, D]),
                      ("sb_w_in", [D, 4 * D]), ("sb_bias", [1, 16]), ("sb_w_o", [D, D]),
                      ("hg_w_in", [D, 4 * D]), ("hg_lower", [4, D]), ("hg_gn_w", [1, 128]), ("hg_w_o", [D, D]),
                      ("constA", [128, NCA]), ("constB", [128, NB * 128])]:
            di(n, sh)
        di("ptab", [1, NB * NPG], I32)
        for n, sh in [("y_p", [NT, D]), ("y_s", [128, D]), ("rw_state_p", [8, 128, 64]), ("rw_shift", [17, D]),
                      ("gla_state_p", [4, 128, 256]), ("sb_k_p", [NT, D]), ("sb_v_p", [NT, D]),
                      ("hg_state_p", [8, 128, 128]), ("rw_state_s", [NB, 8, 128, 64]),
                      ("gla_state_s", [NB, 4, 128, 256]), ("sb_k_s", [128, D]), ("sb_v_s", [128, D]),
                      ("hg_state_s", [NB, 8, 128, 128])]:
            do(n, sh)
        xs_ = M.dram("XS", [NT + 128, D], F32)
        os_ = M.dram("OS", [NT + 128, D], F32)
        kt_ = M.dram("KTS", [NCH, 128, D], BF16)
        vb_ = M.dram("VBS", [NCH, 128, D], BF16)
        self.XS = [M.dv(xs_[c * 128:(c + 1) * 128, :]) for c in range(NCH + 1)]
        self.OS = [M.dv(os_[c * 128:(c + 1) * 128, :]) for c in range(NCH + 1)]
        self.KTS = [M.dv(kt_[c]) for c in range(NCH)]
        self.VBS = [M.dv(vb_[c]) for c in range(NCH)]
        self.os_ap = os_
        self.CA = M.sb([128, NCA], F32, "CA")
        self.CAb = M.sb([128, NCA], BF16, "CAb")
        self.BMb = M.sb([128, NB * 128], BF16, "BMb")
        self.scT = M.sb([128, 8, 17], F32, "scT")
        self.AT = {m: M.sb([128, 8, 16], F32, "AT" + m) for m in "ps"}
        self.ST = {m: M.sb([128, 8, 16], F32, "ST" + m) for m in "ps"}
        self.gbc = {m: M.sb([128, D], F32, "gbc" + m) for m in "ps"}
        self.HL = M.sb([128, 8, 17], F32, "HL")
        self.xbuf = [M.sb([128, D], F32, "xbuf") for _ in range(2)]
        self.xi = 0
        self.pref = {}
        self.xn = M.sb([128, D], F32, "xn")
        self.htmp = M.sb([128, D], F32, "htmp")
        self.junk = M.sb([128, D], BF16, "junk")
        self.ssv = M.sb([128, 4], F32, "ssv")
        self.first_layer = self.LAYERS[0]
        self.last_layer = self.LAYERS[-1]
        self.build()

    def c(self, name, rows=128):
        o, w = COFF[name]
        return self.CA[0:rows, o:o + w]

    def cb(self, name, rows=128):
        o, w = COFF[name]
        return self.CAb[0:rows, o:o + w]

    def chunks(self):
        return list(range(self.NCH + 1))

    def mode(self, c):
        return "s" if c == self.NCH else "p"

    def load_w(self, dst, src, stage):
        M = self.M
        ncols = src.shape[1]
        k = 0
        for kc in range(KC):
            for c0 in range(0, ncols, 1024):
                w = min(1024, ncols - c0)
                st = stage[k % len(stage)]
                M.dma(st[:, 0:w], src[kc * 128:(kc + 1) * 128, c0:c0 + w])
                M.cp(dst[:, kc, c0:c0 + w], st[:, 0:w], en=("dve" if k % 2 == 0 else "pool"))
                k += 1

    def bcast_load(self, dst, src_row):
        self.M.dma(dst, src_row.partition_broadcast(dst.shape[0]))

    def setup(self):
        M, I = self.M, self.I
        M.phase_begin()
        M.dma(self.CA, I["constA"])
        M.cp(self.CAb, self.CA)
        st = M.sb([128, NB * 128], F32, "stB")
        M.dma(st, I["constB"])
        M.cp(self.BMb, st)
        c17 = M.sb([17, D], F32, "c17")
        M.dma(c17, I["c17"])
        ps = M.psb(0)
        for kc in range(KC):
            M.tp(ps[:, kc * 17:(kc + 1) * 17], c17[0:17, kc * 128:(kc + 1) * 128], self.c("ID", 17)[:, 0:17])
        M.act(self.scT.rr("p k b -> p (k b)"), ps[:, 0:KC * 17], AF.Silu)
        M.memset(self.HL, 0.0)
        M.phase_end()

    def layer_setup(self, i):
        M, I = self.M, self.I
        M.phase_begin()
        wb = [M.sb([128, D], F32, "adaw") for _ in range(3)]
        psm = [M.psb(b) for b in range(6)]
        k = 0
        for kc in range(KC):
            for cb in range(3):
                wt = wb[k % 3]
                k += 1
                M.dma(wt, I["ada_w"][i, kc * 128:(kc + 1) * 128, cb * 1024:(cb + 1) * 1024])
                for j in range(2):
                    M.mm(psm[cb * 2 + j][0:17, :], self.scT[:, kc, :], wt[:, j * 512:(j + 1) * 512],
                         start=(kc == 0), stop=(kc == KC - 1))
        MT = M.sb([18, 3 * D], F32, "MT")
        bb = M.sb([17, 3 * D], F32, "adab")
        M.memset(MT, 0.0)
        self.bcast_load(bb, I["ada_b"][i])
        for cbk in range(6):
            M.tt(MT[0:17, cbk * 512:(cbk + 1) * 512], psm[cbk][0:17, :], bb[:, cbk * 512:(cbk + 1) * 512], ALU.add)
        M.dma(MT[17:18, D:2 * D], I["norm_g"][i:i + 1, :])
        o, _ = COFF["SEL17"]
        for m, so in (("p", o), ("s", o + 128)):
            for cb in range(2):
                ps = M.psb(6 + cb)
                M.mm(ps, self.CA[0:17, so:so + 128], MT[0:17, 2 * D + cb * 512:2 * D + (cb + 1) * 512])
                M.cp(self.gbc[m][:, cb * 512:(cb + 1) * 512], ps, en="act")
        ps = M.psb(0)
        for blk in range(16):
            M.tp(ps[:, blk * 18:(blk + 1) * 18], MT[0:18, blk * 128:(blk + 1) * 128], self.c("ID", 18)[:, 0:18])
        mt = M.sb([128, 16, 18], F32, "mtT")
        M.cp(mt.rr("p a b -> p (a b)"), ps[:, 0:16 * 18])
        M.cp(self.ST["p"], mt[:, 0:8, 0:1].bc([128, 8, 16]))
        M.cp(self.ST["s"], mt[:, 0:8, 1:17])
        M.stt(self.AT["p"], mt[:, 8:16, 0:1].bc([128, 8, 16]), 1.0, mt[:, 8:16, 17:18].bc([128, 8, 16]),
              ALU.add, ALU.mult)
        M.stt(self.AT["s"], mt[:, 8:16, 1:17], 1.0, mt[:, 8:16, 17:18].bc([128, 8, 16]), ALU.add, ALU.mult)
        M.phase_end()

    def prefetch_x(self, i, c):
        if c > self.NCH or (i, c) in self.pref:
            return
        M, I = self.M, self.I
        xt = self.xbuf[self.xi % 2]
        self.xi += 1
        if i == self.first_layer:
            src = I["xp"][c * 128:(c + 1) * 128, :] if self.mode(c) == "p" else I["xs"]
        else:
            src = self.XS[c]
        M.dma(xt, src)
        self.pref[(i, c)] = xt

    def front(self, i, c, out, bufs=None):
        M, I = self.M, self.I
        m = self.mode(c)
        self.prefetch_x(i, c)
        xt = self.pref.pop((i, c))
        self.prefetch_x(i, c + 1)
        xn, htmp, junk, ssv = bufs if bufs is not None else (self.xn, self.htmp, self.junk, self.ssv)
        ss = ssv[:, 0:1]
        rstd = ssv[:, 1:2]
        M.act(junk, xt, AF.Square, accum=ss)
        M.rsqrt(rstd, ss, 1.0 / D, EPS)
        M.act(xn, xt, AF.Copy, scale=rstd)
        pst = M.psb(6, 2)
        for kc in range(KC):
            M.tp(pst[:, kc * 128:(kc + 1) * 128], xn[:, kc * 128:(kc + 1) * 128], self.c("ID"))
        v4 = lambda t: t.rr("p (k b t) -> p k b t", k=8, b=16)
        M.tt(v4(htmp), v4(pst), self.AT[m].us(3).bc([128, 8, 16, 8]), ALU.mult)
        M.tt(v4(out), v4(htmp), self.ST[m].us(3).bc([128, 8, 16, 8]), ALU.add)
        return xt

    def rw_prep(self):
        M, I = self.M, self.I
        r = {}
        mx = M.sb([6, D], F32, "mx")
        M.dma(mx, I["rw_mix"])
        sc = M.sb([NB, D], F32, "shc")
        M.dma(sc, I["sh_rw"])
        ps = M.psb(0)
        for kc in range(KC):
            M.tp(ps[:, kc * 6:(kc + 1) * 6], mx[0:6, kc * 128:(kc + 1) * 128], self.c("ID", 6)[:, 0:6])
            M.tp(ps[:, 64 + kc * 16:64 + (kc + 1) * 16], sc[0:NB, kc * 128:(kc + 1) * 128],
                 self.c("ID", NB)[:, 0:NB])
        r["mixT"] = M.sb([128, 8, 6], F32, "mixT")
        r["cacheT"] = M.sb([128, 8, 16], F32, "cacheT")
        M.cp(r["mixT"].rr("p k n -> p (k n)"), ps[:, 0:48])
        M.cp(r["cacheT"].rr("p k n -> p (k n)"), ps[:, 64:64 + 128])
        r["carry"] = M.sb([128, 8, 1], F32, "carry")
        M.memset(r["carry"], 0.0)
        r["xx"] = M.sb([128, D], F32, "xx")
        r["mtmp"] = M.sb([128, D], F32, "mtmp")
        return r

    def rw_xx(self, c, h32, r):
        M = self.M
        xx = r["xx"]
        if self.mode(c) == "p":
            v3 = lambda t: t.rr("p (k t) -> p k t", k=8)
            M.tt(v3(xx)[:, :, 1:128], v3(h32)[:, :, 0:127], v3(h32)[:, :, 1:128], ALU.subtract)
            M.tt(v3(xx)[:, :, 0:1], r["carry"], v3(h32)[:, :, 0:1], ALU.subtract)
            M.cp(r["carry"], v3(h32)[:, :, 127:128], en="pool")
        else:
            v4 = lambda t: t.rr("p (k b t) -> p k b t", k=8, b=16)
            M.tt(v4(xx)[:, :, :, 1:8], v4(h32)[:, :, :, 0:7], v4(h32)[:, :, :, 1:8], ALU.subtract)
            M.tt(v4(xx)[:, :, :, 0:1], r["cacheT"].us(3), v4(h32)[:, :, :, 0:1], ALU.subtract)

    def rw_mixed(self, out, h32, r, n, en="dve"):
        M = self.M
        v3 = lambda t: t.rr("p (k t) -> p k t", k=8)
        M.tt(v3(r["mtmp"]), v3(r["xx"]), r["mixT"][:, :, n:n + 1].bc([128, 8, 128]), ALU.mult, en=en)
        M.tt(out, r["mtmp"], h32, ALU.add, en=en)

    def phase2(self, i):
        M, I, O = self.M, self.I, self.O
        kind = i % 4
        M.phase_begin()
        Wg = M.sb([128, 8, D], BF16, "Wg")
        Wo = M.sb([128, 8, D], BF16, "Wo")
        stage = [M.sb([128, D], F32, "stage") for _ in range(2)]
        gsrc = {0: I["rw_w_rkvg"][3], 1: I["gla_w_in"][:, 2048:3072], 2: I["sb_w_in"][:, 3072:4096],
                3: I["hg_w_in"][:, 3072:4096]}[kind]
        osrc = {0: I["rw_w_o"], 1: I["gla_w_o"], 2: I["sb_w_o"], 3: I["hg_w_o"]}[kind]
        self.load_w(Wg, gsrc, stage)
        self.load_w(Wo, osrc, stage)
        NBUF = 2
        hs_ = [M.sb([128, D], F32 if kind == 0 else BF16, "h") for _ in range(NBUF)]
        sgs = [M.sb([128, D], F32, "sg") for _ in range(NBUF)]
        ops = [M.sb([128, D], F32, "op") for _ in range(NBUF)]
        ogTs = [M.sb([128, D], BF16, "ogT") for _ in range(NBUF)]
        t3s = [stage[0], M.sb([128, D], F32, "t3b")]
        fbufs = [(self.xn, self.htmp, self.junk, self.ssv),
                 (M.sb([128, D], F32, "xn2"), M.sb([128, D], F32, "htmp2"), M.sb([128, D], BF16, "junk2"),
                  M.sb([128, 4], F32, "ssv2"))]
        last = (i == self.last_layer)
        if last:
            fg = M.sb([128, D], F32, "fg")
            self.bcast_load(fg, I["final_g"][0])
            t4s = [stage[1], M.sb([128, D], F32, "t4b")]
        if kind == 0:
            r = self.rw_prep()
            xgs = [M.sb([128, D], BF16, "xg") for _ in range(NBUF)]
        M.dma(ops[0], self.OS[0])
        for ci, c in enumerate(self.chunks()):
            p = ci % NBUF
            m = self.mode(c)
            h, sg, op, ogT, t3 = hs_[p], sgs[p], ops[p], ogTs[p], t3s[p]
            ssv = fbufs[p][3]
            xt = self.front(i, c, h, bufs=fbufs[p])
            if c + 1 <= self.NCH:
                M.dma(ops[(ci + 1) % NBUF], self.OS[c + 1])
            gin = h
            if kind == 0:
                self.rw_xx(c, h, r)
                self.rw_mixed(xgs[p], h, r, 5)
                gin = xgs[p]
            psg = M.psb(0, 2)
            for kc in range(KC):
                for j in range(2):
                    M.mm(psg[:, j * 512:(j + 1) * 512], gin[:, kc * 128:(kc + 1) * 128],
                         Wg[:, kc, j * 512:(j + 1) * 512], start=(kc == 0), stop=(kc == KC - 1))
            M.act(sg, psg, AF.Silu)
            M.tt(sg, sg, op, ALU.mult)
            pst = M.psb(2, 2)
            for kc in range(KC):
                M.tp(pst[:, kc * 128:(kc + 1) * 128], sg[:, kc * 128:(kc + 1) * 128], self.c("ID"))
            M.cp(ogT, pst, en="act")
            pso = M.psb(4, 2)
            for kc in range(KC):
                for j in range(2):
                    M.mm(pso[:, j * 512:(j + 1) * 512], ogT[:, kc * 128:(kc + 1) * 128],
                         Wo[:, kc, j * 512:(j + 1) * 512], start=(kc == 0), stop=(kc == KC - 1))
            M.tt(t3, pso, self.gbc[m], ALU.mult)
            M.tt(t3, t3, xt, ALU.add)
            if last:
                t4 = t4s[p]
                ss = ssv[:, 2:3]
                rs = ssv[:, 3:4]
                M.act(fbufs[p][2], t3, AF.Square, accum=ss)
                M.rsqrt(rs, ss, 1.0 / D, EPS)
                M.stt(t4, t3, rs, fg, ALU.mult, ALU.mult)
                dst = O["y_p"][c * 128:(c + 1) * 128, :] if m == "p" else O["y_s"]
                M.dma(dst, t4, q="pool")
            else:
                M.dma(self.XS[c], t3, q="pool")
        M.phase_end()

    def la_phase1(self, i):
        M, I, O = self.M, self.I, self.O
        kind = i % 4
        H, K, Vd = (4, 128, 256) if kind == 1 else (8, 128, 128)
        HK, HV = H * K, H * Vd
        M.phase_begin()
        stage = [M.sb([128, D], F32, "stage") for _ in range(2)]
        if kind == 1:
            W = M.sb([128, 8, 2064], BF16, "W")
            self.load_w(W[:, :, 0:2048], I["gla_w_in"][:, 0:2048], stage)
            self.load_w(W[:, :, 2048:2064], I["gla_w_in"][:, 3072:3088], stage)
            wgk2 = M.sb([16, 512], F32, "wgk2")
            M.dma(wgk2, I["gla_w_gk2"])
            bgk2 = M.sb([1, 512], F32, "bgk2")
            M.dma(bgk2, I["gla_b_gk2"])
            glT = M.sb([16, 128], F32, "glT")
            gnw = M.sb([128, Vd], F32, "gnw")
            self.bcast_load(gnw, I["gla_gn_w"][0])
            coef, qscale = -1.0 / 16.0, K ** -0.5
            st_in, st_out, st_p = I["st_gla"], O["gla_state_s"], O["gla_state_p"]
        else:
            W = M.sb([128, 8, 3 * D], BF16, "W")
            self.load_w(W, I["hg_w_in"][:, 0:3 * D], stage)
            gnw = M.sb([128, Vd], F32, "gnw")
            self.bcast_load(gnw, I["hg_gn_w"][0])
            LB = M.sb([128, D], F32, "LB")
            OMLB = M.sb([128, D], F32, "OMLB")
            et = M.sb([128, D], F32, "et")
            for j in range(4):
                self.bcast_load(stage[0], I["hg_lower"][j])
                M.act(et, stage[0], AF.Exp)
                if j == 0:
                    M.cp(OMLB, et)
                    M.memset(LB, 0.0)
                else:
                    M.tt(OMLB, OMLB, et, ALU.add)
                    if j <= i:
                        M.tt(LB, LB, et, ALU.add)
            M.recip(OMLB, OMLB)
            M.tt(LB, LB, OMLB, ALU.mult)
            M.ts(OMLB, LB, -1.0, 1.0, ALU.mult, ALU.add)
            coef, qscale = 1.0, K ** -0.5
            st_in, st_out, st_p = I["st_hg"], O["hg_state_s"], O["hg_state_p"]
            q32 = M.sb([128, D], F32, "q32")
            k32 = M.sb([128, D], F32, "k32")
        hT = M.sb([128, D], BF16, "hT")
        L = M.sb([128, HK], F32, "L")
        E1 = M.sb([128, HK], F32, "E1")
        E2 = M.sb([128, HK], F32, "E2")
        qt = M.sb([128, HK], BF16, "qt")
        kt = M.sb([128, HK], BF16, "kt")
        kh = M.sb([128, HK], BF16, "kh")
        vbf = M.sb([128, HV], BF16, "vbf")
        qkT = M.sb([128, 2 * H * 128], BF16, "qkT")
        attn = M.sb([128, H * 128], BF16, "attn")
        dcol = M.sb([128, H * 16], F32, "dcol")
        o32 = M.sb([128, HV], F32, "o32")
        sq = M.sb([128, HV], F32, "sq")
        rs = M.sb([128, 2 * H], F32, "rs")
        S32 = M.sb([128, HV], F32, "S32")
        Sbf = M.sb([128, HV], BF16, "Sbf")
        M.memset(S32, 0.0)
        M.memset(Sbf, 0.0)
        qx = [M.sb([128, NB * 128], BF16, "qx") for _ in range(2)]
        kx = [M.sb([128, NB * 128], BF16, "kx") for _ in range(2)]
        ngrp = 2 if Vd == 256 else 1
        nbg = NB // ngrp
        Sf = [M.sb([128, nbg * Vd], F32, "Sf") for _ in range(2)]
        Sb_ = [M.sb([128, nbg * Vd], BF16, "Sb") for _ in range(2)]
        ones1 = self.c("ONES", 1)
        vh = lambda t, n: t.rr("p (h x) -> p h x", h=n)
        for c in self.chunks():
            m = self.mode(c)
            nseg = 1 if m == "p" else NB
            INCL = self.c("INCL_P" if m == "p" else "INCL_S")
            REM = self.c("REM_P" if m == "p" else "REM_S")
            SEG = self.c("ONES")[:, 0:1] if m == "p" else self.c("SEG_S")
            self.front(i, c, hT)

            def proj(ps, c0, n):
                for kc in range(KC):
                    for j in range(n // 512):
                        M.mm(ps[:, j * 512:(j + 1) * 512], hT[:, kc * 128:(kc + 1) * 128],
                             W[:, kc, c0 + j * 512:c0 + (j + 1) * 512], start=(kc == 0), stop=(kc == KC - 1))
            if kind == 1:
                psqk = M.psb(0, 2)
                proj(psqk, 0, 1024)
                psv = M.psb(2, 2)
                proj(psv, 1024, 1024)
                psl = M.psb(4)
                for kc in range(KC):
                    M.mm(psl[0:16, 0:128], W[:, kc, 2048:2064], hT[:, kc * 128:(kc + 1) * 128],
                         start=(kc == 0), stop=(kc == KC - 1))
                M.cp(glT, psl[0:16, 0:128], en="act")
                psg = M.psb(5)
                M.mm(psg, glT, wgk2, start=True, stop=False)
                M.mm(psg, ones1, bgk2, start=False, stop=True)
                M.act(E1, psg, AF.Exp, scale=-1.0)
                M.act(L, E1, AF.Ln, bias=1.0)
                q_src, k_src, v_src = psqk[:, 0:512], psqk[:, 512:1024], psv
                bB, bR, bD, bT, bA, bO, bS = 6, 7, 4, 5, 6, 0, 2
            else:
                psq = M.psb(0, 2)
                proj(psq, 0, 1024)
                psf = M.psb(2, 2)
                proj(psf, 1024, 1024)
                psi = M.psb(4, 2)
                proj(psi, 2048, 1024)
                M.act(q32, psq, AF.Silu)
                M.act(E1, psf, AF.Sigmoid)
                M.tt(E1, E1, OMLB, ALU.mult)
                M.tt(E1, E1, LB, ALU.add)
                M.act(L, E1, AF.Ln)
                M.ts(k32, E1, -1.0, 1.0, ALU.mult, ALU.add)
                M.cp(vbf, psi, en="act")
                q_src, k_src, v_src = q32, k32, None
                bB, bR, bD, bT, bA, bO, bS = 0, 2, 6, 4, 0, 2, 4
            nbk = HK // 512
            psB = M.psb(bB, nbk)
            psR = M.psb(bR, nbk)
            for j in range(nbk):
                M.mm(psB[:, j * 512:(j + 1) * 512], INCL, L[:, j * 512:(j + 1) * 512])
                M.mm(psR[:, j * 512:(j + 1) * 512], REM, L[:, j * 512:(j + 1) * 512])
            M.act(E1, psB, AF.Exp, scale=coef)
            M.stt(qt, q_src, qscale, E1, ALU.mult, ALU.mult)
            M.act(E2, psB, AF.Exp, scale=-coef)
            M.tt(kt, k_src, E2, ALU.mult)
            M.act(E1, psR, AF.Exp, scale=coef)
            M.tt(kh, k_src, E1, ALU.mult)
            if v_src is not None:
                M.cp(vbf, v_src, en="act")
            psD = M.psb(bD)
            for h in range(H):
                M.mm(psD[:, h * nseg:(h + 1) * nseg], L[:, h * K:(h + 1) * K], SEG)
            M.act(dcol[:, 0:H * nseg], psD[:, 0:H * nseg], AF.Exp, scale=coef)
            nTb = (2 * H * 128) // 1024
            psT = M.psb(bT, nTb).bitcast(BF16)
            for h in range(H):
                M.tp(psT[:, h * 128:(h + 1) * 128], qt[:, h * K:(h + 1) * K], self.cb("ID"))
                M.tp(psT[:, (H + h) * 128:(H + h + 1) * 128], kt[:, h * K:(h + 1) * K], self.cb("ID"))
            M.cp(qkT, psT)
            psA = M.psb(bA, (H * 128) // 512)
            for h in range(H):
                M.mm(psA[:, h * 128:(h + 1) * 128], qkT[:, (H + h) * 128:(H + h + 1) * 128],
                     qkT[:, h * 128:(h + 1) * 128])
            M.tt(vh(attn, H), vh(psA, H), INCL.us(1).bc([128, H, 128]), ALU.mult)
            psO = M.psb(bO, 2)
            if m == "p":
                for h in range(H):
                    M.mm(psO[:, h * Vd:(h + 1) * Vd], attn[:, h * 128:(h + 1) * 128], vbf[:, h * Vd:(h + 1) * Vd],
                         start=True, stop=False)
                    M.mm(psO[:, h * Vd:(h + 1) * Vd], qkT[:, h * 128:(h + 1) * 128], Sbf[:, h * Vd:(h + 1) * Vd],
                         start=False, stop=True)
                psS = M.psb(bS, 2)
                for h in range(H):
                    M.mm(psS[:, h * Vd:(h + 1) * Vd], kh[:, h * K:(h + 1) * K], vbf[:, h * Vd:(h + 1) * Vd])
                M.tt(vh(S32, H), vh(S32, H), dcol[:, 0:H].us(2).bc([128, H, Vd]), ALU.mult)
                M.tt(S32, S32, psS, ALU.add)
                M.cp(Sbf, S32, en="pool")
            else:
                k = 0
                for h in range(H):
                    M.mm(psO[:, h * Vd:(h + 1) * Vd], attn[:, h * 128:(h + 1) * 128], vbf[:, h * Vd:(h + 1) * Vd],
                         start=True, stop=False)
                    qx_, kx_ = qx[h % 2], kx[h % 2]
                    v3 = lambda t: t.rr("p (b x) -> p b x", b=NB)
                    M.tt(v3(qx_), qkT[:, h * 128:(h + 1) * 128].us(1).bc([128, NB, 128]), v3(self.BMb),
                         ALU.mult, en="pool")
                    M.tt(v3(kx_), kh[:, h * K:(h + 1) * K].us(1).bc([128, NB, 128]),
                         self.cb("SEG_S").us(2).bc([128, NB, 128]), ALU.mult, en="pool")
                    for g in range(ngrp):
                        b0 = g * nbg
                        sf, sb_ = Sf[k % 2], Sb_[k % 2]
                        k += 1
                        vb = lambda t: t.rr("p (b x) -> p b x", b=nbg)
                        M.dma(vb(sf), st_in[b0:b0 + nbg, h].rearrange("b k v -> k b v"))
                        M.cp(sb_, sf, en="act")
                        for bb in range(nbg):
                            b = b0 + bb
                            M.mm(psO[:, h * Vd:(h + 1) * Vd], qx_[:, b * 128:(b + 1) * 128],
                                 sb_[:, bb * Vd:(bb + 1) * Vd], start=False, stop=(b == NB - 1))
                        psS = M.psb(4, 4)
                        for bb in range(nbg):
                            b = b0 + bb
                            M.mm(psS[:, bb * Vd:(bb + 1) * Vd], kx_[:, b * 128:(b + 1) * 128],
                                 vbf[:, h * Vd:(h + 1) * Vd])
                        M.tt(vb(sf), vb(sf), dcol[:, h * NB + b0:h * NB + b0 + nbg].us(2).bc([128, nbg, Vd]),
                             ALU.mult)
                        M.tt(sf, sf, psS[:, 0:nbg * Vd], ALU.add)
                        M.dma(st_out[b0:b0 + nbg, h].rearrange("b k v -> k b v"), vb(sf), q="pool")
            M.cp(o32, psO, en="act")
            M.act(sq, psO, AF.Square)
            M.red(rs[:, 0:H], vh(sq, H))
            M.rsqrt(rs[:, 0:H], rs[:, 0:H], 1.0 / Vd, EPS)
            M.tt(vh(o32, H), vh(o32, H), rs[:, 0:H].us(2).bc([128, H, Vd]), ALU.mult)
            M.tt(vh(o32, H), vh(o32, H), gnw.us(1).bc([128, H, Vd]), ALU.mult)
            M.dma(self.OS[c], o32, q="pool")
        M.dma(st_p.rearrange("h k v -> k h v"), vh(S32, H), q="pool")
        M.phase_end()

    def rw_phase1(self, i):
        M, I, O = self.M, self.I, self.O
        NCH = self.NCH
        M.phase_begin()
        stage = [M.sb([128, D], F32, "stage") for _ in range(2)]
        W = M.sb([128, 8, 3 * D], BF16, "W")
        for n in range(3):
            self.load_w(W[:, :, n * D:(n + 1) * D], I["rw_w_rkvg"][n], stage)
        w1 = M.sb([128, 8, 64], BF16, "w1")
        a1 = M.sb([128, 8, 64], BF16, "a1")
        self.load_w(w1, I["rw_w1"], stage)
        self.load_w(a1, I["rw_a1"], stage)
        w2x = M.sb([65, D], BF16, "w2x")
        a2x = M.sb([65, D], BF16, "a2x")
        for dst, s2, s0 in ((w2x, "rw_w2", "rw_w0"), (a2x, "rw_a2", "rw_a0")):
            M.dma(stage[0][0:64, :], I[s2])
            M.dma(stage[0][64:65, :], I[s0])
            M.cp(dst, stage[0][0:65, :])
        bcs = {}
        for n in ("rw_k_k", "rw_k_a", "rw_r_k", "rw_gn_w", "rw_gn_b"):
            bcs[n] = M.sb([128, D], BF16, n)
            self.bcast_load(stage[1], I[n][0])
            M.cp(bcs[n], stage[1])
        tuT = M.sb([65, 128], BF16, "tuT")
        auT = M.sb([65, 128], BF16, "auT")
        M.memset(tuT[64:65, :], 1.0)
        M.memset(auT[64:65, :], 1.0)
        r = self.rw_prep()
        E, t1 = stage
        o32, sq = r["xx"], r["mtmp"]
        h32 = M.sb([128, D], F32, "h32")
        kk = h32
        r32 = M.sb([128, D], F32, "r32")
        k32 = M.sb([128, D], F32, "k32")
        lw = M.sb([128, D], F32, "lw")
        a32 = M.sb([128, D], F32, "a32")
        xT = [M.sb([128, D], BF16, "xT") for _ in range(2)]
        X1, U = xT
        rt = M.sb([128, D], BF16, "rt")
        kt_ = M.sb([128, D], BF16, "kt")
        bt = M.sb([128, D], BF16, "bt")
        at = M.sb([128, D], BF16, "at")
        kh, bh = rt, kt_
        vbf = M.sb([128, D], BF16, "vbf")
        ARt = M.sb([128, 2 * D], BF16, "ARt")
        KBt = M.sb([128, 2 * D], BF16, "KBt")
        AK = M.sb([128, 16, 128], BF16, "AK")
        RK = M.sb([128, 16, 128], BF16, "RK")
        RB = M.sb([128, 16, 128], BF16, "RB")
        TT = M.sb([128, 16, 128], BF16, "TT")
        NS = [[M.sb([128, 4, 128], BF16, "nm") for _ in range(3)] for _ in range(2)]
        S32 = M.sb([128, 512], F32, "S32")
        Sbf = M.sb([128, 512], BF16, "Sbf")
        M.memset(S32, 0.0)
        M.memset(Sbf, 0.0)
        dcol = M.sb([128, 8 * NB], F32, "dcol")
        st = M.sb([128, 80], F32, "st")
        XP = [E.bitcast(BF16), t1.bitcast(BF16), r32.bitcast(BF16), k32.bitcast(BF16)]
        Sf = lw
        Sb_ = at
        hlo = a32[0:17, :]
        vh = lambda t: t.rr("p (h x) -> p h x", h=16)
        v3k = lambda t: t.rr("p (k t) -> p k t", k=8)
        v4k = lambda t: t.rr("p (k b t) -> p k b t", k=8, b=16)
        vbx = lambda t: t.rr("p (b x) -> p b x", b=NB)
        IDb = self.cb("ID")

        def projx(ps, x_, c0):
            for kc in range(KC):
                for j in range(2):
                    M.mm(ps[:, j * 512:(j + 1) * 512], x_[:, kc * 128:(kc + 1) * 128],
                         W[:, kc, c0 + j * 512:c0 + (j + 1) * 512], start=(kc == 0), stop=(kc == KC - 1))

        for c in self.chunks():
            m = self.mode(c)
            nseg = 1 if m == "p" else NB
            INCL = self.c("INCL_P" if m == "p" else "INCL_S")
            STRICT = self.c("STRICT_P" if m == "p" else "STRICT_S")
            REM = self.c("REM_P" if m == "p" else "REM_S")
            SEG = self.c("ONES")[:, 0:1] if m == "p" else self.c("SEG_S")
            nlev = 6 if m == "p" else 2
            self.front(i, c, h32)
            self.rw_xx(c, h32, r)
            if m == "p" and c == NCH - 1:
                M.cp(self.HL[:, :, 0:1], v3k(h32)[:, :, 127:128], en="pool")
            if m == "s":
                M.cp(self.HL[:, :, 1:17], v4k(h32)[:, :, :, 7], en="pool")
            self.rw_mixed(xT[0], h32, r, 0)
            psr = M.psb(0, 2)
            projx(psr, xT[0], 0)
            self.rw_mixed(xT[1], h32, r, 2)
            psk = M.psb(2, 2)
            projx(psk, xT[1], D)
            self.rw_mixed(xT[0], h32, r, 3)
            psv = M.psb(4, 2)
            projx(psv, xT[0], 2 * D)
            self.rw_mixed(xT[1], h32, r, 1)
            psu = M.psb(6)
            for kc in range(KC):
                M.mm(psu[0:64, 0:128], w1[:, kc, :], xT[1][:, kc * 128:(kc + 1) * 128],
                     start=(kc == 0), stop=(kc == KC - 1))
            M.act(tuT[0:64, :], psu[0:64, 0:128], AF.Tanh)
            self.rw_mixed(xT[0], h32, r, 4)
            psu2 = M.psb(7)
            for kc in range(KC):
                M.mm(psu2[0:64, 0:128], a1[:, kc, :], xT[0][:, kc * 128:(kc + 1) * 128],
                     start=(kc == 0), stop=(kc == KC - 1))
            M.cp(auT[0:64, :], psu2[0:64, 0:128], en="act")
            M.cp(r32, psr, en="act")
            M.cp(k32, psk, en="act")
            M.cp(vbf, psv, en="act")
            psw = M.psb(0, 2)
            psa = M.psb(2, 2)
            for j in range(2):
                M.mm(psw[:, j * 512:(j + 1) * 512], tuT[0:65, :], w2x[0:65, j * 512:(j + 1) * 512])
                M.mm(psa[:, j * 512:(j + 1) * 512], auT[0:65, :], a2x[0:65, j * 512:(j + 1) * 512])
            if DBG < 1:
                continue
            M.act(lw, psw, AF.Sigmoid)
            M.act(a32, psa, AF.Sigmoid)
            M.tt(kk, k32, bcs["rw_k_k"], ALU.mult)
            M.tt(t1, kk, kk, ALU.mult)
            M.red(st[:, 0:16], vh(t1))
            M.act(st[:, 0:16], st[:, 0:16], AF.Sqrt)
            M.ts(st[:, 0:16], st[:, 0:16], 1e-12, None, ALU.max)
            M.recip(st[:, 0:16], st[:, 0:16])
            M.tt(vh(kk), vh(kk), st[:, 0:16].us(2).bc([128, 16, 64]), ALU.mult)
            M.stt(t1, a32, -1.0, bcs["rw_k_a"], ALU.add, ALU.mult)
            M.stt(k32, t1, 1.0, k32, ALU.add, ALU.mult)
            M.tt(a32, kk, a32, ALU.mult)
            M.tt(t1, r32, k32, ALU.mult)
            M.tt(t1, t1, bcs["rw_r_k"], ALU.mult)
            M.red(st[:, 16:32], vh(t1))
            psB = M.psb(4, 2)
            psX = M.psb(6, 2)
            psR = M.psb(0, 2)
            for j in range(2):
                sl = slice(j * 512, (j + 1) * 512)
                M.mm(psB[:, sl], INCL, lw[:, sl])
                M.mm(psX[:, sl], STRICT, lw[:, sl])
                M.mm(psR[:, sl], REM, lw[:, sl])
            M.act(E, psB, AF.Exp, scale=CDEC)
            M.tt(rt, r32, E, ALU.mult)
            M.act(E, psB, AF.Exp, scale=-CDEC)
            M.tt(kt_, k32, E, ALU.mult)
            M.tt(bt, a32, E, ALU.mult)
            M.act(E, psX, AF.Exp, scale=CDEC)
            M.stt(at, kk, -1.0, E, ALU.mult, ALU.mult)
            psT = M.psb(2, 2).bitcast(BF16)
            psT2 = M.psb(4, 2).bitcast(BF16)
            for pr in range(8):
                bl = slice(pr * 128, (pr + 1) * 128)
                M.tp(psT[:, (pr * 2) * 128:(pr * 2 + 1) * 128], at[:, bl], IDb)
                M.tp(psT[:, (pr * 2 + 1) * 128:(pr * 2 + 2) * 128], rt[:, bl], IDb)
                M.tp(psT2[:, (pr * 2) * 128:(pr * 2 + 1) * 128], kt_[:, bl], IDb)
                M.tp(psT2[:, (pr * 2 + 1) * 128:(pr * 2 + 2) * 128], bt[:, bl], IDb)
            M.cp(ARt, psT, en="act")
            M.cp(KBt, psT2)
            M.act(E, psR, AF.Exp, scale=CDEC)
            M.tt(kh, k32, E, ALU.mult)
            M.tt(bh, a32, E, ALU.mult)
            psD = M.psb(6)
            for pr in range(8):
                M.mm(psD[:, pr * nseg:(pr + 1) * nseg], lw[:, pr * 128:(pr + 1) * 128], SEG)
            M.act(dcol[:, 0:8 * nseg], psD[:, 0:8 * nseg], AF.Exp, scale=CDEC)
            if DBG < 2.1:
                continue
            for hg in range(4):
                psA1 = M.psb(0, 2)
                psA2 = M.psb(2, 2)
                psM = M.psb(4, 2)
                for hh in range(4):
                    h = hg * 4 + hh
                    pr, pb = h // 2, (h % 2) * 64
                    KT_h = KBt[pb:pb + 64, (pr * 2) * 128:(pr * 2 + 1) * 128]
                    BT_h = KBt[pb:pb + 64, (pr * 2 + 1) * 128:(pr * 2 + 2) * 128]
                    AR_h = ARt[pb:pb + 64, pr * 256:(pr + 1) * 256]
                    AT_h = ARt[pb:pb + 64, (pr * 2) * 128:(pr * 2 + 1) * 128]
                    ca = (hh % 2) * 512 + (hh // 2) * 256
                    cm = (hh % 2) * 512 + (hh // 2) * 128
                    M.mm(psA1[:, ca:ca + 256], KT_h, AR_h)
                    M.mm(psA2[:, ca:ca + 256], BT_h, AR_h)
                    M.mm(psM[:, cm:cm + 128], AT_h, BT_h)
                if DBG < 2.2:
                    continue
                va = lambda t, a: t.rr("p (h2 hp a t) -> p h2 hp a t", h2=2, hp=2, a=2)[:, :, :, a, :]
                vo = lambda t: t.rr("p (hp h2) t -> p h2 hp t", h2=2)
                hs = slice(hg * 4, (hg + 1) * 4)
                sb4 = STRICT.us(1).us(1).bc([128, 2, 2, 128])
                ib4 = INCL.us(1).us(1).bc([128, 2, 2, 128])
                P, PT, X = NS[0]
                M.tt(vo(AK[:, hs, :]), va(psA1, 0), sb4, ALU.mult)
                M.tt(vo(RK[:, hs, :]), va(psA1, 1), ib4, ALU.mult)
                M.tt(vo(P), va(psA2, 0), sb4, ALU.mult)
                M.tt(vo(RB[:, hs, :]), va(psA2, 1), ib4, ALU.mult)
                vm = psM.rr("p (h2 x) -> p h2 x", h2=2)[:, :, 0:256].rr("p h2 (hp t) -> p h2 hp t", hp=2)
                M.tt(vo(PT), vm, REM.us(1).us(1).bc([128, 2, 2, 128]), ALU.mult)
                if DBG < 2.4:
                    continue
                M.tt(X, P, IDb.us(1).bc([128, 4, 128]), ALU.add, en="pool")
                if DBG < 2.6:
                    continue
                cur = 0
                for lev in range(1, nlev + 1):
                    lastl = (lev == nlev)
                    Pn, PTn, Xn = NS[1 - cur]
                    psP = M.psb(6)
                    psPT = M.psb(7)
                    psXn = M.psb(0)
                    for hh in range(4):
                        bl = slice(hh * 128, (hh + 1) * 128)
                        if not lastl:
                            M.mm(psP[:, bl], PT[:, hh, :], P[:, hh, :])
                        M.mm(psPT[:, bl], P[:, hh, :], PT[:, hh, :])
                    if not lastl:
                        M.cp(Pn.rr("p h t -> p (h t)"), psP, en="act")
                    M.cp(PTn.rr("p h t -> p (h t)"), psPT)
                    for hh in range(4):
                        bl = slice(hh * 128, (hh + 1) * 128)
                        M.mm(psXn[:, bl], PTn[:, hh, :], X[:, hh, :])
                    M.tt(Xn.rr("p h t -> p (h t)"), psXn, X.rr("p h t -> p (h t)"), ALU.add)
                    P, PT, X = Pn, PTn, Xn
                    cur = 1 - cur
                M.cp(TT[:, hs, :], X, en="pool")
            if DBG < 3:
                continue
            psX1 = M.psb(0, 2)
            psU = M.psb(2, 2)
            psO = M.psb(4, 2)

            def scan(prs, smp):
                heads = [2 * pr + h2 for pr in prs for h2 in (0, 1)]
                cs = slice(heads[0] * 64, (heads[-1] + 1) * 64)
                for h in heads:
                    pr, pb = h // 2, (h % 2) * 64
                    hb = slice(h * 64, (h + 1) * 64)
                    if not smp:
                        M.mm(psX1[:, hb], ARt[pb:pb + 64, (pr * 2) * 128:(pr * 2 + 1) * 128],
                             Sbf[pb:pb + 64, pr * 64:(pr + 1) * 64], start=True, stop=False)
                    else:
                        for b in range(NB):
                            M.mm(psX1[:, hb], XP[0][pb:pb + 64, b * 128:(b + 1) * 128],
                                 Sb_[pb:pb + 64, b * 64:(b + 1) * 64], start=(b == 0), stop=False)
                    M.mm(psX1[:, hb], AK[:, h, :], vbf[:, hb], start=False, stop=True)
                M.cp(X1[:, cs], psX1[:, cs], en="act")
                for h in heads:
                    hb = slice(h * 64, (h + 1) * 64)
                    M.mm(psU[:, hb], TT[:, h, :], X1[:, hb])
                M.cp(U[:, cs], psU[:, cs])
                for h in heads:
                    pr, pb = h // 2, (h % 2) * 64
                    hb = slice(h * 64, (h + 1) * 64)
                    if not smp:
                        M.mm(psO[:, hb], ARt[pb:pb + 64, (pr * 2 + 1) * 128:(pr * 2 + 2) * 128],
                             Sbf[pb:pb + 64, pr * 64:(pr + 1) * 64], start=True, stop=False)
                    else:
                        for b in range(NB):
                            M.mm(psO[:, hb], XP[1][pb:pb + 64, b * 128:(b + 1) * 128],
                                 Sb_[pb:pb + 64, b * 64:(b + 1) * 64], start=(b == 0), stop=False)
                    M.mm(psO[:, hb], RB[:, h, :], U[:, hb], start=False, stop=False)
                    M.mm(psO[:, hb], RK[:, h, :], vbf[:, hb], start=False, stop=True)
                if not smp:
                    psS = M.psb(6)
                    for h in heads:
                        pr, pb = h // 2, (h % 2) * 64
                        hb = slice(h * 64, (h + 1) * 64)
                        ob = psS[pb:pb + 64, pr * 64:(pr + 1) * 64]
                        M.mm(ob, bh[:, hb], U[:, hb], start=True, stop=False)
                        M.mm(ob, kh[:, hb], vbf[:, hb], start=False, stop=True)
                    v8 = lambda t: t.rr("p (a x) -> p a x", a=8)
                    M.tt(v8(S32), v8(S32), dcol[:, 0:8].us(2).bc([128, 8, 64]), ALU.mult)
                    M.tt(S32, S32, psS, ALU.add)
                    M.cp(Sbf, S32, en="pool")
                else:
                    pr = prs[0]
                    psS = M.psb(6, 2)
                    for h2 in (0, 1):
                        h = 2 * pr + h2
                        pb = h2 * 64
                        hb = slice(h * 64, (h + 1) * 64)
                        for b in range(NB):
                            ob = psS[pb:pb + 64, b * 64:(b + 1) * 64]
                            M.mm(ob, XP[2][:, b * 128 + pb:b * 128 + pb + 64], U[:, hb], start=True, stop=False)
                            M.mm(ob, XP[3][:, b * 128 + pb:b * 128 + pb + 64], vbf[:, hb], start=False, stop=True)
                    v16 = lambda t: t.rr("p (b x) -> p b x", b=NB)
                    M.tt(v16(Sf), v16(Sf), dcol[:, pr * NB:(pr + 1) * NB].us(2).bc([128, NB, 64]), ALU.mult)
                    M.tt(Sf, Sf, psS, ALU.add)
                    M.dma(O["rw_state_s"][:, pr].rearrange("b p v -> p b v"), v16(Sf))

            if m == "p":
                scan(list(range(8)), False)
            else:
                segb = self.cb("SEG_S").us(2).bc([128, NB, 128])
                for pr in range(8):
                    M.dma(vbx(Sf), I["st_rw"][:, pr].rearrange("b p v -> p b v"))
                    M.cp(Sb_, Sf, en="act")
                    M.tt(vbx(XP[0]), ARt[:, (pr * 2) * 128:(pr * 2 + 1) * 128].us(1).bc([128, NB, 128]),
                         vbx(self.BMb), ALU.mult, en="pool")
                    M.tt(vbx(XP[1]), ARt[:, (pr * 2 + 1) * 128:(pr * 2 + 2) * 128].us(1).bc([128, NB, 128]),
                         vbx(self.BMb), ALU.mult, en="pool")
                    M.tt(vbx(XP[2]), bh[:, pr * 128:(pr + 1) * 128].us(1).bc([128, NB, 128]), segb, ALU.mult)
                    M.tt(vbx(XP[3]), kh[:, pr * 128:(pr + 1) * 128].us(1).bc([128, NB, 128]), segb, ALU.mult)
                    scan([pr], True)
            if DBG < 4:
                continue
            M.cp(o32, psO, en="act")
            M.act(sq, psO, AF.Square)
            s1, s2, mean, var = st[:, 32:48], st[:, 48:64], st[:, 64:80], st[:, 48:64]
            M.red(s1, vh(o32))
            M.red(s2, vh(sq))
            M.ts(mean, s1, 1.0 / 64, None, ALU.mult)
            M.tt(s1, mean, mean, ALU.mult)
            M.stt(var, s2, 1.0 / 64, s1, ALU.mult, ALU.subtract)
            M.rsqrt(var, var, 1.0, GN_EPS)
            M.tt(vh(o32), vh(o32), mean.us(2).bc([128, 16, 64]), ALU.subtract)
            M.tt(vh(o32), vh(o32), var.us(2).bc([128, 16, 64]), ALU.mult)
            M.tt(o32, o32, bcs["rw_gn_w"], ALU.mult)
            M.tt(o32, o32, bcs["rw_gn_b"], ALU.add)
            M.tt(vh(sq), vh(vbf), st[:, 16:32].us(2).bc([128, 16, 64]), ALU.mult)
            M.tt(o32, o32, sq, ALU.add)
            M.dma(self.OS[c], o32)
        M.dma(O["rw_state_p"].rearrange("a p v -> p a v"), S32.rr("p (a x) -> p a x", a=8))
        ps = M.psb(0, 2)
        for kc in range(KC):
            M.tp(ps[0:17, kc * 128:(kc + 1) * 128], self.HL[:, kc, :], self.c("ID"))
        M.cp(hlo, ps[0:17, :])
        M.dma(O["rw_shift"], hlo)
        M.phase_end()

    def sb_phase1(self, i):
        M, I, O = self.M, self.I, self.O
        NCH, NPG = self.NCH, self.NPG
        NT = NCH * 128
        M.phase_begin()
        stage = [M.sb([128, D], F32, "stage") for _ in range(2)]
        W = M.sb([128, 8, 3 * D], BF16, "W")
        self.load_w(W, I["sb_w_in"][:, 0:3 * D], stage)
        b16 = M.sb([65, 16], F32, "b16")
        M.dma(b16[0:1, :], I["sb_bias"])
        M.dma(b16[64:65, :], I["sb_bias"])
        brow = M.sb([65, 16 * 128], BF16, "brow")
        for pb in (0, 64):
            M.cp(brow[pb:pb + 1, :].rr("p (h t) -> p h t", h=16), b16[pb:pb + 1, :].us(2).bc([1, 16, 128]))
        onesb = self.cb("ONES")
        NUPI = self.cb("UPI")
        NONES = self.cb("NONES")
        STRb = self.cb("STRICT_P")
        hT = M.sb([128, D], BF16, "hT")
        q_bf = M.sb([128, D], BF16, "q_bf")
        k_bf = M.sb([128, D], BF16, "k_bf")
        k32, v32 = stage
        vbf = M.sb([128, D], BF16, "vbf")
        qT = M.sb([128, D], BF16, "qT")
        kTc = M.sb([128, D], BF16, "kTc")
        Kb = [M.sb([128, D], BF16, "Kb") for _ in range(2)]
        Vb = [M.sb([128, D], BF16, "Vb") for _ in range(2)]
        eb = [M.sb([128, D], F32, "e") for _ in range(2)]
        spb = [M.sb([128, D], BF16, "sp") for _ in range(2)]
        wb = [M.sb([128, D], BF16, "w") for _ in range(2)]
        e, sp, w = eb[0], spb[0], wb[0]
        Cs32 = [M.sb([128, D], F32, "Cs32") for _ in range(2)]
        Csb = [M.sb([128, D], BF16, "Csb") for _ in range(2)]
        o32 = M.sb([128, D], F32, "o32")
        kp32 = [M.sb([128, D], F32, "kp32") for _ in range(2)]
        vp32 = [M.sb([128, D], F32, "vp32") for _ in range(2)]
        kpb = M.sb([128, D], BF16, "kpb")
        vh8 = lambda t: t.rr("p (h t) -> p h t", h=8)
        if DBG > 45:
            pt_i = M.sb([128, NB * NPG], I32, "pt_i")
            M.dma(pt_i, I["ptab"][0].partition_broadcast(128))
            pt_f = M.sb([128, NB * NPG], F32, "pt_f")
            M.cp(pt_f, pt_i)
            M.ts(pt_f, pt_f, 128.0, self.c("IOTA"), ALU.mult, ALU.add)
            self.rows = M.sb([128, NB * NPG], I32, "rows")
            M.cp(self.rows, pt_f)

        def proj(ps, c0):
            for kc in range(KC):
                for j in range(2):
                    M.mm(ps[:, j * 512:(j + 1) * 512], hT[:, kc * 128:(kc + 1) * 128],
                         W[:, kc, c0 + j * 512:c0 + (j + 1) * 512], start=(kc == 0), stop=(kc == KC - 1))

        nk = 0
        for c in self.chunks():
            m = self.mode(c)
            self.front(i, c, hT)
            psq = M.psb(0, 2)
            proj(psq, 0)
            psk = M.psb(2, 2)
            proj(psk, D)
            psv = M.psb(4, 2)
            proj(psv, 2 * D)
            M.cp(k32, psk, en="act")
            M.cp(v32, psv, en="act")
            if m == "p":
                M.dma(O["sb_k_p"][c * 128:(c + 1) * 128, :], k32)
                M.dma(O["sb_v_p"][c * 128:(c + 1) * 128, :], v32)
            else:
                M.dma(O["sb_k_s"], k32)
                M.dma(O["sb_v_s"], v32)
            if DBG < 25:
                continue
            M.cp(vbf, psv)
            M.ts(q_bf, psq, 0.125, None, ALU.mult)
            M.cp(k_bf, psk)
            if DBG < 26:
                continue
            psTq = M.psb(6).bitcast(BF16)
            psTk = M.psb(7).bitcast(BF16)
            for pr in range(8):
                bl = slice(pr * 128, (pr + 1) * 128)
                M.tp(psTq[:, bl], q_bf[:, bl], self.cb("ID"))
                M.tp(psTk[:, bl], k_bf[:, bl], self.cb("ID"))
            M.cp(qT, psTq, en="act")
            M.cp(kTc, psTk)
            if DBG < 27:
                continue
            if m == "p":
                M.dma(self.VBS[c], vbf)
                M.dma(self.KTS[c], kTc)
                psO = M.psb(4, 2)
                ZB = [M.psb(0, 2), M.psb(2, 2)]
                colz = lambda hl: (hl % 2) * 512 + (hl // 2) * 128
                str8 = STRb.us(1).bc([128, 8, 128])
                for j in range(c, -1, -1):
                    diag = (j == c)
                    if diag:
                        Kb_, Vb_ = kTc, vbf
                    else:
                        Kb_, Vb_ = Kb[nk % 2], Vb[nk % 2]
                        nk += 1
                        M.dma(Kb_, self.KTS[j])
                        M.dma(Vb_, self.VBS[j])
                    for hh in range(2):
                        for hl in range(8):
                            h = hh * 8 + hl
                            pr, pb = h // 2, (h % 2) * 64
                            ob = ZB[hh][:, colz(hl):colz(hl) + 128]
                            M.mm(ob, Kb_[pb:pb + 64, pr * 128:(pr + 1) * 128], qT[pb:pb + 64, pr * 128:(pr + 1) * 128],
                                 start=(hl < 2), stop=False)
                            M.mm(ob, onesb[pb:pb + 1, 0:128], brow[pb:pb + 1, h * 128:(h + 1) * 128],
                                 start=False, stop=(hl >= 6))
                    for hh in range(2):
                        M.act(eb[hh], ZB[hh], AF.Exp)
                        M.act(spb[hh], eb[hh], AF.Ln, bias=1.0)
                        if diag:
                            M.tt(vh8(spb[hh]), vh8(spb[hh]), str8, ALU.mult)
                    for hh in range(2):
                        for bk in range(2):
                            bl = slice(bk * 512, (bk + 1) * 512)
                            M.mm(ZB[hh][:, bl], NUPI, spb[hh][:, bl], start=False, stop=diag, nogrp=True)
                            if not diag:
                                M.mm(ZB[hh][:, bl], NONES, Csb[hh][:, bl], start=False, stop=True, nogrp=True)
                    for hh in range(2):
                        M.act(wb[hh], ZB[hh], AF.Exp)
                        if diag:
                            M.tt(vh8(wb[hh]), vh8(wb[hh]), str8, ALU.mult)
                    for hh in range(2):
                        for hl in range(8):
                            h = hh * 8 + hl
                            M.mm(psO[:, h * 64:(h + 1) * 64], wb[hh][:, colz(hl):colz(hl) + 128],
                                 Vb_[:, h * 64:(h + 1) * 64], start=(diag and hl == 0),
                                 stop=(j == 0 and hl == 7))
                    if j > 0:
                        for hh in range(2):
                            if diag:
                                M.cp(Cs32[hh], spb[hh])
                            else:
                                M.tt(Cs32[hh], Cs32[hh], spb[hh], ALU.add)
                            M.cp(Csb[hh], Cs32[hh], en="pool")
                M.cp(o32, psO, en="act")
                M.dma(self.OS[c], o32)
            elif DBG < 50:
                pass
            else:
                brs = M.sb([65, 128], BF16, "brs")
                for pb in (0, 64):
                    M.cp(brs[pb:pb + 1, :].rr("p (h t) -> p h t", h=16), b16[pb:pb + 1, :].us(2).bc([1, 16, TS]))
                mbA = self.cb("MASKB")
                v2 = lambda t: t.rr("p (a x) -> p a x", a=2)
                v16 = lambda t: t.rr("p (h t) -> p h t", h=16)
                pz = lambda ps: ps.rr("p (a x) -> p a x", a=2)[:, :, 0:64]
                cc = lambda h: (h % 2) * 64 + (h // 2) * TS
                cp_ = lambda h: (h % 2) * 512 + (h // 2) * TS
                c32, cbf = Cs32[0][:, 0:128], Csb[0][:, 0:128]
                ZS = [M.psb(0, 2), M.psb(2, 2)]
                psTs = [M.psb(6).bitcast(BF16), M.psb(7).bitcast(BF16)]
                e2 = [M.sb([128, 128], F32, "e2") for _ in range(2)]
                sp2 = [M.sb([128, 128], BF16, "sp2") for _ in range(2)]
                w2 = [M.sb([128, 128], BF16, "w2") for _ in range(2)]
                kpb2 = [kpb, M.sb([128, D], BF16, "kpb2")]
                ob32 = o32[0:TS, :]
                psOb = M.psb(4, 2)
                blocks = ["new"] + list(range(NPG - 1, -1, -1))
                units = [(b, bi) for b in range(NB) for bi in range(len(blocks))]
                kv = {}

                def prep_qk(u):
                    b, bi = units[u]
                    blk = blocks[bi]
                    p = u % 2
                    if blk == "new":
                        Kb_, Vb_ = kTc, vbf
                    else:
                        Kb_, Vb_ = Kb[p], Vb[p]
                        self.page_dma(kp32[p], I["ck"], b * NPG + blk)
                        self.page_dma(vp32[p], I["cv"], b * NPG + blk)
                        M.cp(kpb2[p], kp32[p])
                        for pr in range(8):
                            bl = slice(pr * 128, (pr + 1) * 128)
                            M.tp(psTs[p][:, bl], kpb2[p][:, bl], self.cb("ID"))
                        M.cp(Kb_, psTs[p], en="act")
                        M.cp(Vb_, vp32[p])
                    kv[u] = Vb_
                    for h in range(16):
                        pr, pb = h // 2, (h % 2) * 64
                        ob = ZS[p][:, cp_(h):cp_(h) + TS]
                        M.mm(ob, Kb_[pb:pb + 64, pr * 128:(pr + 1) * 128],
                             qT[pb:pb + 64, pr * 128 + b * TS:pr * 128 + (b + 1) * TS], start=(h < 2), stop=False)
                        M.mm(ob, onesb[pb:pb + 1, 0:128], brs[pb:pb + 1, h * TS:(h + 1) * TS],
                             start=False, stop=(h >= 14))

                def mask(t, u):
                    b, bi = units[u]
                    if bi == 0:
                        mb = mbA[:, b * TS:(b + 1) * TS].us(1).bc([128, 16, TS])
                        M.tt(v16(t), v16(t), mb, ALU.mult)

                def act1(u):
                    p = u % 2
                    M.act(v2(e2[p]), pz(ZS[p]), AF.Exp)
                    M.act(sp2[p], e2[p], AF.Ln, bias=1.0)
                    mask(sp2[p], u)

                def tail(u):
                    p = u % 2
                    new = units[u][1] == 0
                    for a in range(2):
                        M.mm(ZS[p][:, a * 512:a * 512 + 64], NUPI, sp2[p][:, a * 64:(a + 1) * 64],
                             start=False, stop=new, nogrp=True)
                        if not new:
                            M.mm(ZS[p][:, a * 512:a * 512 + 64], NONES, cbf[:, a * 64:(a + 1) * 64],
                                 start=False, stop=True, nogrp=True)

                def act2(u):
                    p = u % 2
                    M.act(v2(w2[p]), pz(ZS[p]), AF.Exp)
                    mask(w2[p], u)

                def csum(u):
                    p = u % 2
                    bi = units[u][1]
                    if bi == len(blocks) - 1:
                        return
                    if bi == 0:
                        M.cp(c32, sp2[p])
                    else:
                        M.tt(c32, c32, sp2[p], ALU.add)
                    M.cp(cbf, c32)

                def pv(u):
                    p = u % 2
                    b, bi = units[u]
                    new, lastb = (bi == 0), (bi == len(blocks) - 1)
                    Vb_ = kv.pop(u)
                    for h in range(16):
                        M.mm(psOb[0:TS, h * 64:(h + 1) * 64], w2[p][:, cc(h):cc(h) + TS],
                             Vb_[:, h * 64:(h + 1) * 64], start=(new and h % 8 == 0),
                             stop=(lastb and h % 8 == 7))
                    if lastb:
                        M.cp(ob32, psOb[0:TS, :], en="act")
                        M.dma(V(self.os_ap[NT + b * TS:NT + (b + 1) * TS, :], self.OS[NCH].trs), ob32)

                n = len(units)
                prep_qk(0)
                for u in range(n):
                    act1(u)
                    if u >= 1:
                        pv(u - 1)
                    if u + 1 < n:
                        prep_qk(u + 1)
                    tail(u)
                    act2(u)
                    csum(u)
                pv(n - 1)
        M.phase_end()

    def page_dma(self, out, pool_ap, slot):
        M = self.M
        e = M.E["pool"]
        tr0 = out.trs[0]
        if tr0.dsem is None:
            tr0.dsem = self.nc.alloc_semaphore("gsem_%d" % M.nsem)
            M.nsem += 1
            M.all_dsems.append(tr0)
            M.dsem_map[tr0.dsem.num] = tr0
        for tr in self.rows.trs:
            M._wait(e, tr.w)
        for tr in out.trs:
            if tr.w is not None and tr.w[0] is not tr0.dsem:
                M._wait(e, tr.w)
            for ev in tr.r:
                M._wait(e, ev)
        inst = e.h.indirect_dma_start(
            out=out.ap, out_offset=None, in_=pool_ap.rearrange("n p f -> (n p) f"),
            in_offset=bass.IndirectOffsetOnAxis(ap=self.rows.ap[:, slot:slot + 1], axis=0))
        tr0.dcnt += 16
        inst.then_inc(tr0.dsem, 16)
        M._commit((tr0.dsem, tr0.dcnt), [self.rows], [out])

    def build(self):
        self.setup()
        for i in self.LAYERS:
            self.layer_setup(i)
            kind = i % 4
            if kind == 0:
                self.rw_phase1(i)
            elif kind == 2:
                self.sb_phase1(i)
            else:
                self.la_phase1(i)
            self.phase2(i)
        self.M.finish()


def make_in_maps(inp, cfg):
    NCH, NPG = cfg["NCH"], cfg["NPG"]
    f = lambda a: np.ascontiguousarray(np.asarray(a, dtype=np.float32))
    shared = {}
    for n in ["norm_g", "ada_w", "ada_b"]:
        shared[n] = f(inp[n])
    shared["final_g"] = f(inp["final_g"]).reshape(1, D)
    shared["hg_lower"] = f(inp["hg_lower"])
    for n in ["rw_mix", "rw_w_rkvg", "rw_w0", "rw_w1", "rw_w2", "rw_a0", "rw_a1", "rw_a2", "rw_k_k", "rw_k_a",
              "rw_r_k", "rw_gn_w", "rw_gn_b", "rw_w_o", "gla_w_in", "gla_w_gk2", "gla_b_gk2", "gla_gn_w",
              "gla_w_o", "sb_w_in", "sb_bias", "sb_w_o", "hg_w_in", "hg_gn_w", "hg_w_o"]:
        a = f(inp[n])[0]
        if a.ndim == 1:
            a = a.reshape(1, -1)
        shared[n] = np.ascontiguousarray(a)
    shared["constA"] = CONST_A
    shared["constB"] = CONST_B
    ck = f(inp["cache_sb_k"])[0].reshape(-1, 128, D)
    cv = f(inp["cache_sb_v"])[0].reshape(-1, 128, D)
    shared["ck"] = ck[:cfg["NPOOL"]]
    shared["cv"] = cv[:cfg["NPOOL"]]
    maps = []
    for c in range(NCORE):
        sl = slice(c * NB, (c + 1) * NB)
        m = dict(shared)
        m["xp"] = f(inp["x_prompt"][c // 2]).reshape(NCH * 128, D)
        m["xs"] = f(inp["x_sample"][sl]).reshape(NB * TS, D)
        m["c17"] = np.concatenate([f(inp["c_prompt"][c // 2]).reshape(1, D), f(inp["c_sample"][sl])], axis=0)
        s = f(inp["state_rwkv"][0, sl])
        m["st_rw"] = np.ascontiguousarray(s.transpose(0, 1, 3, 2).reshape(NB, 8, 128, 64))
        m["sh_rw"] = f(inp["cache_rwkv_shift"][0, sl])
        m["st_gla"] = f(inp["state_gla"][0, sl])
        m["st_hg"] = f(inp["state_hgrn"][0, sl])
        m["ptab"] = np.ascontiguousarray(np.asarray(inp["page_table"], dtype=np.int32)[sl].reshape(1, NB * NPG))
        maps.append(m)
    return maps


def assemble(res, cfg, nseq_prompt):
    NCH = cfg["NCH"]
    T = NCH * 128
    ev = [res[2 * s] for s in range(nseq_prompt)]
    y_p = np.stack([r["y_p"].reshape(T, D) for r in ev])
    y_s = np.concatenate([r["y_s"].reshape(NB, TS, D) for r in res], axis=0)

    def rwst(a):
        sh = a.shape[:-3]
        a = a.reshape(*sh, 8, 2, 64, 64).reshape(*sh, 16, 64, 64)
        return np.ascontiguousarray(np.swapaxes(a, -1, -2))
    rw_state_p = np.stack([rwst(r["rw_state_p"]) for r in ev])[None]
    rw_shift_p = np.stack([r["rw_shift"][0] for r in ev])[None]
    gla_state_p = np.stack([r["gla_state_p"] for r in ev])[None]
    sb_k_p = np.stack([r["sb_k_p"].reshape(T, 16, 64) for r in ev])[None]
    sb_v_p = np.stack([r["sb_v_p"].reshape(T, 16, 64) for r in ev])[None]
    hg_state_p = np.stack([r["hg_state_p"] for r in ev])[None]
    rw_state_s = np.concatenate([rwst(r["rw_state_s"]) for r in res], axis=0)[None]
    rw_shift_s = np.concatenate([r["rw_shift"][1:17] for r in res], axis=0)[None]
    gla_state_s = np.concatenate([r["gla_state_s"] for r in res], axis=0)[None]
    sb_k_s = np.concatenate([r["sb_k_s"].reshape(NB, TS, 16, 64) for r in res], axis=0)[None]
    sb_v_s = np.concatenate([r["sb_v_s"].reshape(NB, TS, 16, 64) for r in res], axis=0)[None]
    hg_state_s = np.concatenate([r["hg_state_s"] for r in res], axis=0)[None]
    outs = (y_p, y_s, rw_state_p, rw_shift_p, gla_state_p, sb_k_p, sb_v_p, hg_state_p,
            rw_state_s, rw_shift_s, gla_state_s, sb_k_s, sb_v_s, hg_state_s)
    return tuple(np.ascontiguousarray(o, dtype=np.float32) for o in outs)


def run(inp, cfg):
    prog = Prog(cfg)
    maps = make_in_maps(inp, cfg)
    r = run_bass_kernel_spmd(prog.nc, maps, core_ids=list(range(NCORE)))
    return assemble(r.results, cfg, np.asarray(inp["x_prompt"]).shape[0])


def kernel(**inputs):
    T = np.asarray(inputs["x_prompt"]).shape[1]
    npg = np.asarray(inputs["page_table"]).shape[1]
    npool = np.asarray(inputs["cache_sb_k"]).shape[1]
    cfg = {"NCH": T // 128, "NPG": npg, "NPOOL": npool, "LAYERS": (0, 1, 2, 3)}
    return run(inputs, cfg)
```

```python
import numpy as np
import concourse.bass as bass
import concourse.mybir as mybir
from concourse.bass_utils import run_bass_kernel_spmd

F32 = mybir.dt.float32
BF16 = mybir.dt.bfloat16
I32 = mybir.dt.int32
AF = mybir.ActivationFunctionType
ALU = mybir.AluOpType
AX = mybir.AxisListType

import os
DBG = float(os.environ.get("RWDBG", "99"))
D = 1024
KC = 8
NCORE = 8
NB = 16
TS = 8
EPS = 1e-6
GN_EPS = 64e-5
CDEC = -0.6065306597126334


class Tr:
    __slots__ = ("w", "r", "dsem", "dcnt", "x")

    def __init__(self, x=False):
        self.x = x
        self.w = None
        self.r = []
        self.dsem = None
        self.dcnt = 0


class V:
    def __init__(self, ap, trs):
        self.ap = ap
        self.trs = trs

    def __getitem__(self, k):
        return V(self.ap[k], self.trs)

    def rr(self, pat, **kw):
        return V(self.ap.rearrange(pat, **kw), self.trs)

    def bc(self, shape):
        return V(self.ap.broadcast_to(list(shape)), self.trs)

    def us(self, ax):
        return V(self.ap.unsqueeze(ax), self.trs)

    def bitcast(self, dt):
        return V(self.ap.bitcast(dt), self.trs)

    @property
    def shape(self):
        return tuple(self.ap.shape)


class Eng:
    def __init__(self, name, h, sem):
        self.name = name
        self.h = h
        self.sem = sem
        self.cnt = 0
        self.waited = {}


class MK:
    def __init__(self, nc):
        self.nc = nc
        self.E = {}
        for name, h in (("pe", nc.tensor), ("dve", nc.vector), ("act", nc.scalar),
                        ("pool", nc.gpsimd), ("sp", nc.sync)):
            self.E[name] = Eng(name, h, nc.alloc_semaphore("sem_" + name))
        self.nsem = 5
        self.uid = 0
        self.dram_trs = []
        self.all_dsems = []
        self.dsem_map = {}
        self.free_dsems = []
        self.stack = None
        self.phase_trs = []
        self.psum = nc.alloc_psum_tensor("psum", [128, 4096], F32)
        self.ps_tr = [Tr(True) for _ in range(8)]
        self.ps_ptr = 0

    def sb(self, shape, dt, name=None):
        self.uid += 1
        nm = "%s_%d" % (name or "t", self.uid)
        if self.stack is not None:
            t = self.stack.enter_context(self.nc.sbuf_tensor(nm, list(shape), dt))
        else:
            t = self.nc.alloc_sbuf_tensor(nm, list(shape), dt)
        tr = Tr()
        if self.stack is not None:
            self.phase_trs.append(tr)
        return V(t[tuple(slice(None) for _ in shape)], (tr,))

    def phase_begin(self):
        import contextlib
        assert self.stack is None
        self.stack = contextlib.ExitStack()
        self.phase_trs = []

    def phase_end(self):
        self.finish()
        for tr in self.phase_trs:
            if tr.dsem is not None:
                if not tr.dsem.name.startswith("gsem"):
                    self.free_dsems.append((tr.dsem, tr.dcnt))
                self.all_dsems.remove(tr)
                del self.dsem_map[tr.dsem.num]
                tr.dsem = None
        for tr in self.ps_tr:
            tr.w = None
            tr.r = []
        self.stack.close()
        self.stack = None
        self.phase_trs = []

    def dram(self, name, shape, dt, kind="Internal"):
        t = self.nc.dram_tensor(name, list(shape), dt, kind=kind)
        return t.ap()

    def dv(self, ap):
        return V(ap, (Tr(),))

    def psb(self, b, n=1):
        assert b + n <= 8
        return V(self.psum[:, b * 512:(b + n) * 512], tuple(self.ps_tr[b:b + n]))

    def _wait(self, e, ev):
        if ev is None:
            return
        sem, val = ev
        trd = self.dsem_map.get(sem.num)
        if trd is not None:
            val = trd.dcnt
        if e.waited.get(sem.num, 0) >= val:
            return
        e.h.wait_ge(sem, val)
        e.waited[sem.num] = val

    def _deps(self, e, reads, writes):
        for v in reads:
            for tr in v.trs:
                if tr.w is not None and not (e.name == "pe" and tr.w[0] is e.sem):
                    self._wait(e, tr.w)
                if tr.x:
                    for ev in tr.r:
                        if ev[0] is not e.sem:
                            self._wait(e, ev)
        pe = e.name == "pe"
        for v in writes:
            for tr in v.trs:
                if tr.w is not None and not (pe and tr.w[0] is e.sem):
                    self._wait(e, tr.w)
                for ev in tr.r:
                    if not (pe and ev[0] is e.sem):
                        self._wait(e, ev)

    def _commit(self, ev, reads, writes):
        for v in writes:
            for tr in v.trs:
                tr.w = ev
                tr.r = []
        wset = set(id(tr) for v in writes for tr in v.trs)
        for v in reads:
            for tr in v.trs:
                if id(tr) not in wset:
                    tr.r.append(ev)
                    if len(tr.r) > 24:
                        best = {}
                        for s, val in tr.r:
                            if s.num not in best or best[s.num][1] < val:
                                best[s.num] = (s, val)
                        tr.r = list(best.values())

    def op(self, en, fn, reads, writes):
        e = self.E[en]
        reads = [v for v in reads if isinstance(v, V)]
        self._deps(e, reads, writes)
        inst = fn(e.h)
        e.cnt += 1
        inst.then_inc(e.sem, 1)
        self._commit((e.sem, e.cnt), reads, writes)

    def dma(self, out, in_, q="sp"):
        q = "sp"
        e = self.E[q]
        cand = [v for v in (out, in_) if isinstance(v, V) and not self._is_dram(v)]
        assert cand, "dma needs one SBUF side"
        tr0 = cand[0].trs[0]
        if tr0.dsem is None:
            if self.free_dsems:
                tr0.dsem, tr0.dcnt = self.free_dsems.pop()
            else:
                tr0.dsem = self.nc.alloc_semaphore("dsem_%d" % self.nsem)
                self.nsem += 1
            self.all_dsems.append(tr0)
            self.dsem_map[tr0.dsem.num] = tr0
        reads = [in_] if isinstance(in_, V) else []
        writes = [out] if isinstance(out, V) else []
        for v in reads:
            for tr in v.trs:
                if tr.w is not None:
                    self._wait(e, tr.w)
        for v in writes:
            for tr in v.trs:
                if tr.w is not None and tr.w[0] is not tr0.dsem:
                    self._wait(e, tr.w)
                for ev in tr.r:
                    self._wait(e, ev)
        oap = out.ap if isinstance(out, V) else out
        iap = in_.ap if isinstance(in_, V) else in_
        inst = e.h.dma_start(out=oap, in_=iap)
        tr0.dcnt += 16
        inst.then_inc(tr0.dsem, 16)
        self._commit((tr0.dsem, tr0.dcnt), reads, writes)

    @staticmethod
    def _is_dram(v):
        return str(v.ap.space) == "DRAM"

    def finish(self):
        evs = [(e.sem, e.cnt) for e in self.E.values() if e.cnt]
        evs += [(tr.dsem, tr.dcnt) for tr in self.all_dsems]
        for e in self.E.values():
            for ev in evs:
                if ev[0] is not e.sem:
                    self._wait(e, ev)

    def mm(self, out, lhsT, rhs, start=True, stop=True, nogrp=False):
        kw = {"skip_group_check": True} if nogrp else {}
        self.op("pe", lambda h: h.matmul(out.ap, lhsT.ap, rhs.ap, start=start, stop=stop, **kw),
                [lhsT, rhs] + ([] if start else [out]), [out])

    def tp(self, out, in_, ident):
        self.op("pe", lambda h: h.transpose(out.ap, in_.ap, ident.ap), [in_, ident], [out])

    def act(self, out, in_, func, bias=None, scale=None, accum=None, en="act"):
        kw = {}
        rd = [in_]
        if bias is not None:
            kw["bias"] = bias.ap if isinstance(bias, V) else bias
            rd.append(bias)
        if scale is not None:
            kw["scale"] = scale.ap if isinstance(scale, V) else scale
            rd.append(scale)
        wr = [out]
        if accum is not None:
            kw["accum_out"] = accum.ap
            wr.append(accum)
        self.op(en, lambda h: h.activation(out.ap, in_.ap, func, **kw), rd, wr)

    def tt(self, out, a, b, op, en="dve"):
        self.op(en, lambda h: h.tensor_tensor(out.ap, a.ap, b.ap, op), [a, b], [out])

    def ts(self, out, a, s1, s2, op0, op1=None, en="dve"):
        g = lambda s: s.ap if isinstance(s, V) else s
        if op1 is None:
            self.op(en, lambda h: h.tensor_scalar(out.ap, a.ap, g(s1), 0.0, op0, ALU.add), [a, s1], [out])
        else:
            self.op(en, lambda h: h.tensor_scalar(out.ap, a.ap, g(s1), g(s2), op0, op1),
                    [a, s1, s2], [out])

    def stt(self, out, a, s, b, op0, op1, en="dve"):
        g = lambda x: x.ap if isinstance(x, V) else x
        self.op(en, lambda h: h.scalar_tensor_tensor(out.ap, a.ap, g(s), b.ap, op0, op1),
                [a, s, b], [out])

    def cp(self, out, a, en="dve"):
        if en == "act":
            self.op(en, lambda h: h.activation(out.ap, a.ap, AF.Copy), [a], [out])
        else:
            self.op(en, lambda h: h.tensor_copy(out.ap, a.ap), [a], [out])

    def red(self, out, a, op=ALU.add, en="dve"):
        self.op(en, lambda h: h.tensor_reduce(out.ap, a.ap, AX.X, op), [a], [out])

    def rsqrt(self, out, a, scale, eps):
        self.act(out, a, AF.Sqrt, bias=eps, scale=scale)
        self.op("dve", lambda h: h.reciprocal(out.ap, out.ap), [out], [out])

    def recip(self, out, a):
        self.op("dve", lambda h: h.reciprocal(out.ap, a.ap), [a], [out])

    def memset(self, out, val, en="dve"):
        self.op(en, lambda h: h.memset(out.ap, val), [], [out])


def build_consts():
    p = np.arange(128)
    f = np.arange(128)
    same = (p[:, None] // TS) == (f[None, :] // TS)
    c = {}
    c["ID"] = (p[:, None] == f[None, :])
    c["INCL_P"] = (p[:, None] <= f[None, :])
    c["STRICT_P"] = (p[:, None] < f[None, :])
    c["REM_P"] = (p[:, None] > f[None, :])
    c["UPI"] = -1.0 * (p[:, None] >= f[None, :])
    c["NONES"] = -1.0 * np.ones((128, 128))
    c["ONES"] = np.ones((128, 128))
    c["INCL_S"] = c["INCL_P"] & same
    c["STRICT_S"] = c["STRICT_P"] & same
    c["REM_S"] = c["REM_P"] & same
    c["SEG_S"] = (p[:, None] // TS) == np.arange(NB)[None, :]
    sel = np.zeros((128, 256))
    sel[0, 0:128] = 1.0
    for b in range(NB):
        sel[1 + b, 128 + b * TS:128 + (b + 1) * TS] = 1.0
    c["SEL17"] = sel
    mb = np.zeros((128, NB, TS))
    for s in range(128):
        for t in range(TS):
            if (s % TS) < t:
                mb[s, s // TS, t] = 1.0
    c["MASKB"] = mb.reshape(128, NB * TS)
    c["IOTA"] = p[:, None].astype(np.float32)
    off = {}
    cols = []
    o = 0
    for k, v in c.items():
        v = np.asarray(v, np.float32)
        off[k] = (o, v.shape[1])
        cols.append(v)
        o += v.shape[1]
    A = np.concatenate(cols, axis=1).astype(np.float32)
    bm = np.zeros((128, NB, 128), np.float32)
    for b in range(NB):
        bm[:, b, b * TS:(b + 1) * TS] = 1.0
    return A, off, bm.reshape(128, NB * 128)


CONST_A, COFF, CONST_B = build_consts()
NCA = CONST_A.shape[1]


class Prog:
    def __init__(self, cfg):
        self.NCH = NCH = cfg["NCH"]
        self.NPG = NPG = cfg["NPG"]
        self.NPOOL = NPOOL = cfg["NPOOL"]
        self.LAYERS = tuple(cfg["LAYERS"])
        nc = self.nc = bass.Bass("TRN2", target_bir_lowering=False)
        M = self.M = MK(nc)
        NT = NCH * 128
        self.I = I = {}
        self.O = O = {}

        def di(n, sh, dt=F32):
            I[n] = nc.dram_tensor(n, list(sh), dt, kind="ExternalInput").ap()

        def do(n, sh, dt=F32):
            O[n] = nc.dram_tensor(n, list(sh), dt, kind="ExternalOutput").ap()

        for n, sh in [("xp", [NT, D]), ("xs", [128, D]), ("c17", [17, D]), ("st_rw", [NB, 8, 128, 64]),
                      ("sh_rw", [NB, D]), ("st_gla", [NB, 4, 128, 256]), ("st_hg", [NB, 8, 128, 128]),
                      ("ck", [NPOOL, 128, D]), ("cv", [NPOOL, 128, D]),
                      ("norm_g", [4, D]), ("ada_w", [4, D, 3 * D]), ("ada_b", [4, 3 * D]), ("final_g", [1, D]),
                      ("rw_mix", [6, D]), ("rw_w_rkvg", [4, D, D]), ("rw_w0", [1, D]), ("rw_w1", [D, 64]),
                      ("rw_w2", [64, D]), ("rw_a0", [1, D]), ("rw_a1", [D, 64]), ("rw_a2", [64, D]),
                      ("rw_k_k", [1, D]), ("rw_k_a", [1, D]), ("rw_r_k", [1, D]), ("rw_gn_w", [1, D]),
                      ("rw_gn_b", [1, D]), ("rw_w_o", [D, D]),
                      ("gla_w_in", [D, 3088]), ("gla_w_gk2", [16, 512]), ("gla_b_gk2", [1, 512]),
                      ("gla_gn_w", [1, 256]), ("gla_w_o", [D, D]),
                      ("sb_w_in", [D, 4 * D]), ("sb_bias", [1, 16]), ("sb_w_o", [D, D]),
                      ("hg_w_in", [D, 4 * D]), ("hg_lower", [4, D]), ("hg_gn_w", [1, 128]), ("hg_w_o", [D, D]),
                      ("constA", [128, NCA]), ("constB", [128, NB * 128])]:
            di(n, sh)
        di("ptab", [1, NB * NPG], I32)
        for n, sh in [("y_p", [NT, D]), ("y_s", [128, D]), ("rw_state_p", [8, 128, 64]), ("rw_shift", [17, D]),
                      ("gla_state_p", [4, 128, 256]), ("sb_k_p", [NT, D]), ("sb_v_p", [NT, D]),
                      ("hg_state_p", [8, 128, 128]), ("rw_state_s", [NB, 8, 128, 64]),
                      ("gla_state_s", [NB, 4, 128, 256]), ("sb_k_s", [128, D]), ("sb_v_s", [128, D]),
                      ("hg_state_s", [NB, 8, 128, 128])]:
            do(n, sh)
        xs_ = M.dram("XS", [NT + 128, D], F32)
        os_ = M.dram("OS", [NT + 128, D], F32)
        kt_ = M.dram("KTS", [NCH, 128, D], BF16)
        vb_ = M.dram("VBS", [NCH, 128, D], BF16)
        self.XS = [M.dv(xs_[c * 128:(c + 1) * 128, :]) for c in range(NCH + 1)]
        self.OS = [M.dv(os_[c * 128:(c + 1) * 128, :]) for c in range(NCH + 1)]
        self.KTS = [M.dv(kt_[c]) for c in range(NCH)]
        self.VBS = [M.dv(vb_[c]) for c in range(NCH)]
        self.os_ap = os_
        self.CA = M.sb([128, NCA], F32, "CA")
        self.CAb = M.sb([128, NCA], BF16, "CAb")
        self.BMb = M.sb([128, NB * 128], BF16, "BMb")
        self.scT = M.sb([128, 8, 17], F32, "scT")
        self.AT = {m: M.sb([128, 8, 16], F32, "AT" + m) for m in "ps"}
        self.ST = {m: M.sb([128, 8, 16], F32, "ST" + m) for m in "ps"}
        self.gbc = {m: M.sb([128, D], F32, "gbc" + m) for m in "ps"}
        self.HL = M.sb([128, 8, 17], F32, "HL")
        self.xbuf = [M.sb([128, D], F32, "xbuf") for _ in range(2)]
        self.xi = 0
        self.pref = {}
        self.xn = M.sb([128, D], F32, "xn")
        self.htmp = M.sb([128, D], F32, "htmp")
        self.junk = M.sb([128, D], BF16, "junk")
        self.ssv = M.sb([128, 4], F32, "ssv")
        self.first_layer = self.LAYERS[0]
        self.last_layer = self.LAYERS[-1]
        self.build()

    def c(self, name, rows=128):
        o, w = COFF[name]
        return self.CA[0:rows, o:o + w]

    def cb(self, name, rows=128):
        o, w = COFF[name]
        return self.CAb[0:rows, o:o + w]

    def chunks(self):
        return list(range(self.NCH + 1))

    def mode(self, c):
        return "s" if c == self.NCH else "p"

    def load_w(self, dst, src, stage):
        M = self.M
        ncols = src.shape[1]
        k = 0
        for kc in range(KC):
            for c0 in range(0, ncols, 1024):
                w = min(1024, ncols - c0)
                st = stage[k % len(stage)]
                M.dma(st[:, 0:w], src[kc * 128:(kc + 1) * 128, c0:c0 + w])
                M.cp(dst[:, kc, c0:c0 + w], st[:, 0:w], en=("dve" if k % 2 == 0 else "pool"))
                k += 1

    def bcast_load(self, dst, src_row):
        self.M.dma(dst, src_row.partition_broadcast(dst.shape[0]))

    def setup(self):
        M, I = self.M, self.I
        M.phase_begin()
        M.dma(self.CA, I["constA"])
        M.cp(self.CAb, self.CA)
        st = M.sb([128, NB * 128], F32, "stB")
        M.dma(st, I["constB"])
        M.cp(self.BMb, st)
        c17 = M.sb([17, D], F32, "c17")
        M.dma(c17, I["c17"])
        ps = M.psb(0)
        for kc in range(KC):
            M.tp(ps[:, kc * 17:(kc + 1) * 17], c17[0:17, kc * 128:(kc + 1) * 128], self.c("ID", 17)[:, 0:17])
        M.act(self.scT.rr("p k b -> p (k b)"), ps[:, 0:KC * 17], AF.Silu)
        M.memset(self.HL, 0.0)
        M.phase_end()

    def layer_setup(self, i):
        M, I = self.M, self.I
        M.phase_begin()
        wb = [M.sb([128, D], F32, "adaw") for _ in range(3)]
        psm = [M.psb(b) for b in range(6)]
        k = 0
        for kc in range(KC):
            for cb in range(3):
                wt = wb[k % 3]
                k += 1
                M.dma(wt, I["ada_w"][i, kc * 128:(kc + 1) * 128, cb * 1024:(cb + 1) * 1024])
                for j in range(2):
                    M.mm(psm[cb * 2 + j][0:17, :], self.scT[:, kc, :], wt[:, j * 512:(j + 1) * 512],
                         start=(kc == 0), stop=(kc == KC - 1))
        MT = M.sb([18, 3 * D], F32, "MT")
        bb = M.sb([17, 3 * D], F32, "adab")
        M.memset(MT, 0.0)
        self.bcast_load(bb, I["ada_b"][i])
        for cbk in range(6):
            M.tt(MT[0:17, cbk * 512:(cbk + 1) * 512], psm[cbk][0:17, :], bb[:, cbk * 512:(cbk + 1) * 512], ALU.add)
        M.dma(MT[17:18, D:2 * D], I["norm_g"][i:i + 1, :])
        o, _ = COFF["SEL17"]
        for m, so in (("p", o), ("s", o + 128)):
            for cb in range(2):
                ps = M.psb(6 + cb)
                M.mm(ps, self.CA[0:17, so:so + 128], MT[0:17, 2 * D + cb * 512:2 * D + (cb + 1) * 512])
                M.cp(self.gbc[m][:, cb * 512:(cb + 1) * 512], ps, en="act")
        ps = M.psb(0)
        for blk in range(16):
            M.tp(ps[:, blk * 18:(blk + 1) * 18], MT[0:18, blk * 128:(blk + 1) * 128], self.c("ID", 18)[:, 0:18])
        mt = M.sb([128, 16, 18], F32, "mtT")
        M.cp(mt.rr("p a b -> p (a b)"), ps[:, 0:16 * 18])
        M.cp(self.ST["p"], mt[:, 0:8, 0:1].bc([128, 8, 16]))
        M.cp(self.ST["s"], mt[:, 0:8, 1:17])
        M.stt(self.AT["p"], mt[:, 8:16, 0:1].bc([128, 8, 16]), 1.0, mt[:, 8:16, 17:18].bc([128, 8, 16]),
              ALU.add, ALU.mult)
        M.stt(self.AT["s"], mt[:, 8:16, 1:17], 1.0, mt[:, 8:16, 17:18].bc([128, 8, 16]), ALU.add, ALU.mult)
        M.phase_end()

    def prefetch_x(self, i, c):
        if c > self.NCH or (i, c) in self.pref:
            return
        M, I = self.M, self.I
        xt = self.xbuf[self.xi % 2]
        self.xi += 1
        if i == self.first_layer:
            src = I["xp"][c * 128:(c + 1) * 128, :] if self.mode(c) == "p" else I["xs"]
        else:
            src = self.XS[c]
        M.dma(xt, src)
        self.pref[(i, c)] = xt

    def front(self, i, c, out, bufs=None):
        M, I = self.M, self.I
        m = self.mode(c)
        self.prefetch_x(i, c)
        xt = self.pref.pop((i, c))
        self.prefetch_x(i, c + 1)
        xn, htmp, junk, ssv = bufs if bufs is not None else (self.xn, self.htmp, self.junk, self.ssv)
        ss = ssv[:, 0:1]
        rstd = ssv[:, 1:2]
        M.act(junk, xt, AF.Square, accum=ss)
        M.rsqrt(rstd, ss, 1.0 / D, EPS)
        M.act(xn, xt, AF.Copy, scale=rstd)
        pst = M.psb(6, 2)
        for kc in range(KC):
            M.tp(pst[:, kc * 128:(kc + 1) * 128], xn[:, kc * 128:(kc + 1) * 128], self.c("ID"))
        v4 = lambda t: t.rr("p (k b t) -> p k b t", k=8, b=16)
        M.tt(v4(htmp), v4(pst), self.AT[m].us(3).bc([128, 8, 16, 8]), ALU.mult)
        M.tt(v4(out), v4(htmp), self.ST[m].us(3).bc([128, 8, 16, 8]), ALU.add)
        return xt

    def rw_prep(self):
        M, I = self.M, self.I
        r = {}
        mx = M.sb([6, D], F32, "mx")
        M.dma(mx, I["rw_mix"])
        sc = M.sb([NB, D], F32, "shc")
        M.dma(sc, I["sh_rw"])
        ps = M.psb(0)
        for kc in range(KC):
            M.tp(ps[:, kc * 6:(kc + 1) * 6], mx[0:6, kc * 128:(kc + 1) * 128], self.c("ID", 6)[:, 0:6])
            M.tp(ps[:, 64 + kc * 16:64 + (kc + 1) * 16], sc[0:NB, kc * 128:(kc + 1) * 128],
                 self.c("ID", NB)[:, 0:NB])
        r["mixT"] = M.sb([128, 8, 6], F32, "mixT")
        r["cacheT"] = M.sb([128, 8, 16], F32, "cacheT")
        M.cp(r["mixT"].rr("p k n -> p (k n)"), ps[:, 0:48])
        M.cp(r["cacheT"].rr("p k n -> p (k n)"), ps[:, 64:64 + 128])
        r["carry"] = M.sb([128, 8, 1], F32, "carry")
        M.memset(r["carry"], 0.0)
        r["xx"] = M.sb([128, D], F32, "xx")
        r["mtmp"] = M.sb([128, D], F32, "mtmp")
        return r

    def rw_xx(self, c, h32, r):
        M = self.M
        xx = r["xx"]
        if self.mode(c) == "p":
            v3 = lambda t: t.rr("p (k t) -> p k t", k=8)
            M.tt(v3(xx)[:, :, 1:128], v3(h32)[:, :, 0:127], v3(h32)[:, :, 1:128], ALU.subtract)
            M.tt(v3(xx)[:, :, 0:1], r["carry"], v3(h32)[:, :, 0:1], ALU.subtract)
            M.cp(r["carry"], v3(h32)[:, :, 127:128], en="pool")
        else:
            v4 = lambda t: t.rr("p (k b t) -> p k b t", k=8, b=16)
            M.tt(v4(xx)[:, :, :, 1:8], v4(h32)[:, :, :, 0:7], v4(h32)[:, :, :, 1:8], ALU.subtract)
            M.tt(v4(xx)[:, :, :, 0:1], r["cacheT"].us(3), v4(h32)[:, :, :, 0:1], ALU.subtract)

    def rw_mixed(self, out, h32, r, n, en="dve"):
        M = self.M
        v3 = lambda t: t.rr("p (k t) -> p k t", k=8)
        M.tt(v3(r["mtmp"]), v3(r["xx"]), r["mixT"][:, :, n:n + 1].bc([128, 8, 128]), ALU.mult, en=en)
        M.tt(out, r["mtmp"], h32, ALU.add, en=en)

    def phase2(self, i):
        M, I, O = self.M, self.I, self.O
        kind = i % 4
        M.phase_begin()
        Wg = M.sb([128, 8, D], BF16, "Wg")
        Wo = M.sb([128, 8, D], BF16, "Wo")
        stage = [M.sb([128, D], F32, "stage") for _ in range(2)]
        gsrc = {0: I["rw_w_rkvg"][3], 1: I["gla_w_in"][:, 2048:3072], 2: I["sb_w_in"][:, 3072:4096],
                3: I["hg_w_in"][:, 3072:4096]}[kind]
        osrc = {0: I["rw_w_o"], 1: I["gla_w_o"], 2: I["sb_w_o"], 3: I["hg_w_o"]}[kind]
        self.load_w(Wg, gsrc, stage)
        self.load_w(Wo, osrc, stage)
        NBUF = 2
        hs_ = [M.sb([128, D], F32 if kind == 0 else BF16, "h") for _ in range(NBUF)]
        sgs = [M.sb([128, D], F32, "sg") for _ in range(NBUF)]
        ops = [M.sb([128, D], F32, "op") for _ in range(NBUF)]
        ogTs = [M.sb([128, D], BF16, "ogT") for _ in range(NBUF)]
        t3s = [stage[0], M.sb([128, D], F32, "t3b")]
        fbufs = [(self.xn, self.htmp, self.junk, self.ssv),
                 (M.sb([128, D], F32, "xn2"), M.sb([128, D], F32, "htmp2"), M.sb([128, D], BF16, "junk2"),
                  M.sb([128, 4], F32, "ssv2"))]
        last = (i == self.last_layer)
        if last:
            fg = M.sb([128, D], F32, "fg")
            self.bcast_load(fg, I["final_g"][0])
            t4s = [stage[1], M.sb([128, D], F32, "t4b")]
        if kind == 0:
            r = self.rw_prep()
            xgs = [M.sb([128, D], BF16, "xg") for _ in range(NBUF)]
        M.dma(ops[0], self.OS[0])
        for ci, c in enumerate(self.chunks()):
            p = ci % NBUF
            m = self.mode(c)
            h, sg, op, ogT, t3 = hs_[p], sgs[p], ops[p], ogTs[p], t3s[p]
            ssv = fbufs[p][3]
            xt = self.front(i, c, h, bufs=fbufs[p])
            if c + 1 <= self.NCH:
                M.dma(ops[(ci + 1) % NBUF], self.OS[c + 1])
            gin = h
            if kind == 0:
                self.rw_xx(c, h, r)
                self.rw_mixed(xgs[p], h, r, 5)
                gin = xgs[p]
            psg = M.psb(0, 2)
            for kc in range(KC):
                for j in range(2):
                    M.mm(psg[:, j * 512:(j + 1) * 512], gin[:, kc * 128:(kc + 1) * 128],
                         Wg[:, kc, j * 512:(j + 1) * 512], start=(kc == 0), stop=(kc == KC - 1))
            M.act(sg, psg, AF.Silu)
            M.tt(sg, sg, op, ALU.mult)
            pst = M.psb(2, 2)
            for kc in range(KC):
                M.tp(pst[:, kc * 128:(kc + 1) * 128], sg[:, kc * 128:(kc + 1) * 128], self.c("ID"))
            M.cp(ogT, pst, en="act")
            pso = M.psb(4, 2)
            for kc in range(KC):
                for j in range(2):
                    M.mm(pso[:, j * 512:(j + 1) * 512], ogT[:, kc * 128:(kc + 1) * 128],
                         Wo[:, kc, j * 512:(j + 1) * 512], start=(kc == 0), stop=(kc == KC - 1))
            M.tt(t3, pso, self.gbc[m], ALU.mult)
            M.tt(t3, t3, xt, ALU.add)
            if last:
                t4 = t4s[p]
                ss = ssv[:, 2:3]
                rs = ssv[:, 3:4]
                M.act(fbufs[p][2], t3, AF.Square, accum=ss)
                M.rsqrt(rs, ss, 1.0 / D, EPS)
                M.stt(t4, t3, rs, fg, ALU.mult, ALU.mult)
                dst = O["y_p"][c * 128:(c + 1) * 128, :] if m == "p" else O["y_s"]
                M.dma(dst, t4, q="pool")
            else:
                M.dma(self.XS[c], t3, q="pool")
        M.phase_end()

    def la_phase1(self, i):
        M, I, O = self.M, self.I, self.O
        kind = i % 4
        H, K, Vd = (4, 128, 256) if kind == 1 else (8, 128, 128)
        HK, HV = H * K, H * Vd
        M.phase_begin()
        stage = [M.sb([128, D], F32, "stage") for _ in range(2)]
        if kind == 1:
            W = M.sb([128, 8, 2064], BF16, "W")
            self.load_w(W[:, :, 0:2048], I["gla_w_in"][:, 0:2048], stage)
            self.load_w(W[:, :, 2048:2064], I["gla_w_in"][:, 3072:3088], stage)
            wgk2 = M.sb([16, 512], F32, "wgk2")
            M.dma(wgk2, I["gla_w_gk2"])
            bgk2 = M.sb([1, 512], F32, "bgk2")
            M.dma(bgk2, I["gla_b_gk2"])
            glT = M.sb([16, 128], F32, "glT")
            gnw = M.sb([128, Vd], F32, "gnw")
            self.bcast_load(gnw, I["gla_gn_w"][0])
            coef, qscale = -1.0 / 16.0, K ** -0.5
            st_in, st_out, st_p = I["st_gla"], O["gla_state_s"], O["gla_state_p"]
        else:
            W = M.sb([128, 8, 3 * D], BF16, "W")
            self.load_w(W, I["hg_w_in"][:, 0:3 * D], stage)
            gnw = M.sb([128, Vd], F32, "gnw")
            self.bcast_load(gnw, I["hg_gn_w"][0])
            LB = M.sb([128, D], F32, "LB")
            OMLB = M.sb([128, D], F32, "OMLB")
            et = M.sb([128, D], F32, "et")
            for j in range(4):
                self.bcast_load(stage[0], I["hg_lower"][j])
                M.act(et, stage[0], AF.Exp)
                if j == 0:
                    M.cp(OMLB, et)
                    M.memset(LB, 0.0)
                else:
                    M.tt(OMLB, OMLB, et, ALU.add)
                    if j <= i:
                        M.tt(LB, LB, et, ALU.add)
            M.recip(OMLB, OMLB)
            M.tt(LB, LB, OMLB, ALU.mult)
            M.ts(OMLB, LB, -1.0, 1.0, ALU.mult, ALU.add)
            coef, qscale = 1.0, K ** -0.5
            st_in, st_out, st_p = I["st_hg"], O["hg_state_s"], O["hg_state_p"]
            q32 = M.sb([128, D], F32, "q32")
            k32 = M.sb([128, D], F32, "k32")
        hT = M.sb([128, D], BF16, "hT")
        L = M.sb([128, HK], F32, "L")
        E1 = M.sb([128, HK], F32, "E1")
        E2 = M.sb([128, HK], F32, "E2")
        qt = M.sb([128, HK], BF16, "qt")
        kt = M.sb([128, HK], BF16, "kt")
        kh = M.sb([128, HK], BF16, "kh")
        vbf = M.sb([128, HV], BF16, "vbf")
        qkT = M.sb([128, 2 * H * 128], BF16, "qkT")
        attn = M.sb([128, H * 128], BF16, "attn")
        dcol = M.sb([128, H * 16], F32, "dcol")
        o32 = M.sb([128, HV], F32, "o32")
        sq = M.sb([128, HV], F32, "sq")
        rs = M.sb([128, 2 * H], F32, "rs")
        S32 = M.sb([128, HV], F32, "S32")
        Sbf = M.sb([128, HV], BF16, "Sbf")
        M.memset(S32, 0.0)
        M.memset(Sbf, 0.0)
        qx = [M.sb([128, NB * 128], BF16, "qx") for _ in range(2)]
        kx = [M.sb([128, NB * 128], BF16, "kx") for _ in range(2)]
        ngrp = 2 if Vd == 256 else 1
        nbg = NB // ngrp
        Sf = [M.sb([128, nbg * Vd], F32, "Sf") for _ in range(2)]
        Sb_ = [M.sb([128, nbg * Vd], BF16, "Sb") for _ in range(2)]
        ones1 = self.c("ONES", 1)
        vh = lambda t, n: t.rr("p (h x) -> p h x", h=n)
        for c in self.chunks():
            m = self.mode(c)
            nseg = 1 if m == "p" else NB
            INCL = self.c("INCL_P" if m == "p" else "INCL_S")
            REM = self.c("REM_P" if m == "p" else "REM_S")
            SEG = self.c("ONES")[:, 0:1] if m == "p" else self.c("SEG_S")
            self.front(i, c, hT)

            def proj(ps, c0, n):
                for kc in range(KC):
                    for j in range(n // 512):
                        M.mm(ps[:, j * 512:(j + 1) * 512], hT[:, kc * 128:(kc + 1) * 128],
                             W[:, kc, c0 + j * 512:c0 + (j + 1) * 512], start=(kc == 0), stop=(kc == KC - 1))
            if kind == 1:
                psqk = M.psb(0, 2)
                proj(psqk, 0, 1024)
                psv = M.psb(2, 2)
                proj(psv, 1024, 1024)
                psl = M.psb(4)
                for kc in range(KC):
                    M.mm(psl[0:16, 0:128], W[:, kc, 2048:2064], hT[:, kc * 128:(kc + 1) * 128],
                         start=(kc == 0), stop=(kc == KC - 1))
                M.cp(glT, psl[0:16, 0:128], en="act")
                psg = M.psb(5)
                M.mm(psg, glT, wgk2, start=True, stop=False)
                M.mm(psg, ones1, bgk2, start=False, stop=True)
                M.act(E1, psg, AF.Exp, scale=-1.0)
                M.act(L, E1, AF.Ln, bias=1.0)
                q_src, k_src, v_src = psqk[:, 0:512], psqk[:, 512:1024], psv
                bB, bR, bD, bT, bA, bO, bS = 6, 7, 4, 5, 6, 0, 2
            else:
                psq = M.psb(0, 2)
                proj(psq, 0, 1024)
                psf = M.psb(2, 2)
                proj(psf, 1024, 1024)
                psi = M.psb(4, 2)
                proj(psi, 2048, 1024)
                M.act(q32, psq, AF.Silu)
                M.act(E1, psf, AF.Sigmoid)
                M.tt(E1, E1, OMLB, ALU.mult)
                M.tt(E1, E1, LB, ALU.add)
                M.act(L, E1, AF.Ln)
                M.ts(k32, E1, -1.0, 1.0, ALU.mult, ALU.add)
                M.cp(vbf, psi, en="act")
                q_src, k_src, v_src = q32, k32, None
                bB, bR, bD, bT, bA, bO, bS = 0, 2, 6, 4, 0, 2, 4
            nbk = HK // 512
            psB = M.psb(bB, nbk)
            psR = M.psb(bR, nbk)
            for j in range(nbk):
                M.mm(psB[:, j * 512:(j + 1) * 512], INCL, L[:, j * 512:(j + 1) * 512])
                M.mm(psR[:, j * 512:(j + 1) * 512], REM, L[:, j * 512:(j + 1) * 512])
            M.act(E1, psB, AF.Exp, scale=coef)
            M.stt(qt, q_src, qscale, E1, ALU.mult, ALU.mult)
            M.act(E2, psB, AF.Exp, scale=-coef)
            M.tt(kt, k_src, E2, ALU.mult)
            M.act(E1, psR, AF.Exp, scale=coef)
            M.tt(kh, k_src, E1, ALU.mult)
            if v_src is not None:
                M.cp(vbf, v_src, en="act")
            psD = M.psb(bD)
            for h in range(H):
                M.mm(psD[:, h * nseg:(h + 1) * nseg], L[:, h * K:(h + 1) * K], SEG)
            M.act(dcol[:, 0:H * nseg], psD[:, 0:H * nseg], AF.Exp, scale=coef)
            nTb = (2 * H * 128) // 1024
            psT = M.psb(bT, nTb).bitcast(BF16)
            for h in range(H):
                M.tp(psT[:, h * 128:(h + 1) * 128], qt[:, h * K:(h + 1) * K], self.cb("ID"))
                M.tp(psT[:, (H + h) * 128:(H + h + 1) * 128], kt[:, h * K:(h + 1) * K], self.cb("ID"))
            M.cp(qkT, psT)
            psA = M.psb(bA, (H * 128) // 512)
            for h in range(H):
                M.mm(psA[:, h * 128:(h + 1) * 128], qkT[:, (H + h) * 128:(H + h + 1) * 128],
                     qkT[:, h * 128:(h + 1) * 128])
            M.tt(vh(attn, H), vh(psA, H), INCL.us(1).bc([128, H, 128]), ALU.mult)
            psO = M.psb(bO, 2)
            if m == "p":
                for h in range(H):
                    M.mm(psO[:, h * Vd:(h + 1) * Vd], attn[:, h * 128:(h + 1) * 128], vbf[:, h * Vd:(h + 1) * Vd],
                         start=True, stop=False)
                    M.mm(psO[:, h * Vd:(h + 1) * Vd], qkT[:, h * 128:(h + 1) * 128], Sbf[:, h * Vd:(h + 1) * Vd],
                         start=False, stop=True)
                psS = M.psb(bS, 2)
                for h in range(H):
                    M.mm(psS[:, h * Vd:(h + 1) * Vd], kh[:, h * K:(h + 1) * K], vbf[:, h * Vd:(h + 1) * Vd])
                M.tt(vh(S32, H), vh(S32, H), dcol[:, 0:H].us(2).bc([128, H, Vd]), ALU.mult)
                M.tt(S32, S32, psS, ALU.add)
                M.cp(Sbf, S32, en="pool")
            else:
                k = 0
                for h in range(H):
                    M.mm(psO[:, h * Vd:(h + 1) * Vd], attn[:, h * 128:(h + 1) * 128], vbf[:, h * Vd:(h + 1) * Vd],
                         start=True, stop=False)
                    qx_, kx_ = qx[h % 2], kx[h % 2]
                    v3 = lambda t: t.rr("p (b x) -> p b x", b=NB)
                    M.tt(v3(qx_), qkT[:, h * 128:(h + 1) * 128].us(1).bc([128, NB, 128]), v3(self.BMb),
                         ALU.mult, en="pool")
                    M.tt(v3(kx_), kh[:, h * K:(h + 1) * K].us(1).bc([128, NB, 128]),
                         self.cb("SEG_S").us(2).bc([128, NB, 128]), ALU.mult, en="pool")
                    for g in range(ngrp):
                        b0 = g * nbg
                        sf, sb_ = Sf[k % 2], Sb_[k % 2]
                        k += 1
                        vb = lambda t: t.rr("p (b x) -> p b x", b=nbg)
                        M.dma(vb(sf), st_in[b0:b0 + nbg, h].rearrange("b k v -> k b v"))
                        M.cp(sb_, sf, en="act")
                        for bb in range(nbg):
                            b = b0 + bb
                            M.mm(psO[:, h * Vd:(h + 1) * Vd], qx_[:, b * 128:(b + 1) * 128],
                                 sb_[:, bb * Vd:(bb + 1) * Vd], start=False, stop=(b == NB - 1))
                        psS = M.psb(4, 4)
                        for bb in range(nbg):
                            b = b0 + bb
                            M.mm(psS[:, bb * Vd:(bb + 1) * Vd], kx_[:, b * 128:(b + 1) * 128],
                                 vbf[:, h * Vd:(h + 1) * Vd])
                        M.tt(vb(sf), vb(sf), dcol[:, h * NB + b0:h * NB + b0 + nbg].us(2).bc([128, nbg, Vd]),
                             ALU.mult)
                        M.tt(sf, sf, psS[:, 0:nbg * Vd], ALU.add)
                        M.dma(st_out[b0:b0 + nbg, h].rearrange("b k v -> k b v"), vb(sf), q="pool")
            M.cp(o32, psO, en="act")
            M.act(sq, psO, AF.Square)
            M.red(rs[:, 0:H], vh(sq, H))
            M.rsqrt(rs[:, 0:H], rs[:, 0:H], 1.0 / Vd, EPS)
            M.tt(vh(o32, H), vh(o32, H), rs[:, 0:H].us(2).bc([128, H, Vd]), ALU.mult)
            M.tt(vh(o32, H), vh(o32, H), gnw.us(1).bc([128, H, Vd]), ALU.mult)
            M.dma(self.OS[c], o32, q="pool")
        M.dma(st_p.rearrange("h k v -> k h v"), vh(S32, H), q="pool")
        M.phase_end()

    def rw_phase1(self, i):
        M, I, O = self.M, self.I, self.O
        NCH = self.NCH
        M.phase_begin()
        stage = [M.sb([128, D], F32, "stage") for _ in range(2)]
        W = M.sb([128, 8, 3 * D], BF16, "W")
        for n in range(3):
            self.load_w(W[:, :, n * D:(n + 1) * D], I["rw_w_rkvg"][n], stage)
        w1 = M.sb([128, 8, 64], BF16, "w1")
        a1 = M.sb([128, 8, 64], BF16, "a1")
        self.load_w(w1, I["rw_w1"], stage)
        self.load_w(a1, I["rw_a1"], stage)
        w2x = M.sb([65, D], BF16, "w2x")
        a2x = M.sb([65, D], BF16, "a2x")
        for dst, s2, s0 in ((w2x, "rw_w2", "rw_w0"), (a2x, "rw_a2", "rw_a0")):
            M.dma(stage[0][0:64, :], I[s2])
            M.dma(stage[0][64:65, :], I[s0])
            M.cp(dst, stage[0][0:65, :])
        bcs = {}
        for n in ("rw_k_k", "rw_k_a", "rw_r_k", "rw_gn_w", "rw_gn_b"):
            bcs[n] = M.sb([128, D], BF16, n)
            self.bcast_load(stage[1], I[n][0])
            M.cp(bcs[n], stage[1])
        tuT = M.sb([65, 128], BF16, "tuT")
        auT = M.sb([65, 128], BF16, "auT")
        M.memset(tuT[64:65, :], 1.0)
        M.memset(auT[64:65, :], 1.0)
        r = self.rw_prep()
        E, t1 = stage
        o32, sq = r["xx"], r["mtmp"]
        h32 = M.sb([128, D], F32, "h32")
        kk = h32
        r32 = M.sb([128, D], F32, "r32")
        k32 = M.sb([128, D], F32, "k32")
        lw = M.sb([128, D], F32, "lw")
        a32 = M.sb([128, D], F32, "a32")
        xT = [M.sb([128, D], BF16, "xT") for _ in range(2)]
        X1, U = xT
        rt = M.sb([128, D], BF16, "rt")
        kt_ = M.sb([128, D], BF16, "kt")
        bt = M.sb([128, D], BF16, "bt")
        at = M.sb([128, D], BF16, "at")
        kh, bh = rt, kt_
        vbf = M.sb([128, D], BF16, "vbf")
        ARt = M.sb([128, 2 * D], BF16, "ARt")
        KBt = M.sb([128, 2 * D], BF16, "KBt")
        AK = M.sb([128, 16, 128], BF16, "AK")
        RK = M.sb([128, 16, 128], BF16, "RK")
        RB = M.sb([128, 16, 128], BF16, "RB")
        TT = M.sb([128, 16, 128], BF16, "TT")
        NS = [[M.sb([128, 4, 128], BF16, "nm") for _ in range(3)] for _ in range(2)]
        S32 = M.sb([128, 512], F32, "S32")
        Sbf = M.sb([128, 512], BF16, "Sbf")
        M.memset(S32, 0.0)
        M.memset(Sbf, 0.0)
        dcol = M.sb([128, 8 * NB], F32, "dcol")
        st = M.sb([128, 80], F32, "st")
        XP = [E.bitcast(BF16), t1.bitcast(BF16), r32.bitcast(BF16), k32.bitcast(BF16)]
        Sf = lw
        Sb_ = at
        hlo = a32[0:17, :]
        vh = lambda t: t.rr("p (h x) -> p h x", h=16)
        v3k = lambda t: t.rr("p (k t) -> p k t", k=8)
        v4k = lambda t: t.rr("p (k b t) -> p k b t", k=8, b=16)
        vbx = lambda t: t.rr("p (b x) -> p b x", b=NB)
        IDb = self.cb("ID")

        def projx(ps, x_, c0):
            for kc in range(KC):
                for j in range(2):
                    M.mm(ps[:, j * 512:(j + 1) * 512], x_[:, kc * 128:(kc + 1) * 128],
                         W[:, kc, c0 + j * 512:c0 + (j + 1) * 512], start=(kc == 0), stop=(kc == KC - 1))

        for c in self.chunks():
            m = self.mode(c)
            nseg = 1 if m == "p" else NB
            INCL = self.c("INCL_P" if m == "p" else "INCL_S")
            STRICT = self.c("STRICT_P" if m == "p" else "STRICT_S")
            REM = self.c("REM_P" if m == "p" else "REM_S")
            SEG = self.c("ONES")[:, 0:1] if m == "p" else self.c("SEG_S")
            nlev = 6 if m == "p" else 2
            self.front(i, c, h32)
            self.rw_xx(c, h32, r)
            if m == "p" and c == NCH - 1:
                M.cp(self.HL[:, :, 0:1], v3k(h32)[:, :, 127:128], en="pool")
            if m == "s":
                M.cp(self.HL[:, :, 1:17], v4k(h32)[:, :, :, 7], en="pool")
            self.rw_mixed(xT[0], h32, r, 0)
            psr = M.psb(0, 2)
            projx(psr, xT[0], 0)
            self.rw_mixed(xT[1], h32, r, 2)
            psk = M.psb(2, 2)
            projx(psk, xT[1], D)
            self.rw_mixed(xT[0], h32, r, 3)
            psv = M.psb(4, 2)
            projx(psv, xT[0], 2 * D)
            self.rw_mixed(xT[1], h32, r, 1)
            psu = M.psb(6)
            for kc in range(KC):
                M.mm(psu[0:64, 0:128], w1[:, kc, :], xT[1][:, kc * 128:(kc + 1) * 128],
                     start=(kc == 0), stop=(kc == KC - 1))
            M.act(tuT[0:64, :], psu[0:64, 0:128], AF.Tanh)
            self.rw_mixed(xT[0], h32, r, 4)
            psu2 = M.psb(7)
            for kc in range(KC):
                M.mm(psu2[0:64, 0:128], a1[:, kc, :], xT[0][:, kc * 128:(kc + 1) * 128],
                     start=(kc == 0), stop=(kc == KC - 1))
            M.cp(auT[0:64, :], psu2[0:64, 0:128], en="act")
            M.cp(r32, psr, en="act")
            M.cp(k32, psk, en="act")
            M.cp(vbf, psv, en="act")
            psw = M.psb(0, 2)
            psa = M.psb(2, 2)
            for j in range(2):
                M.mm(psw[:, j * 512:(j + 1) * 512], tuT[0:65, :], w2x[0:65, j * 512:(j + 1) * 512])
                M.mm(psa[:, j * 512:(j + 1) * 512], auT[0:65, :], a2x[0:65, j * 512:(j + 1) * 512])
            if DBG < 1:
                continue
            M.act(lw, psw, AF.Sigmoid)
            M.act(a32, psa, AF.Sigmoid)
            M.tt(kk, k32, bcs["rw_k_k"], ALU.mult)
            M.tt(t1, kk, kk, ALU.mult)
            M.red(st[:, 0:16], vh(t1))
            M.act(st[:, 0:16], st[:, 0:16], AF.Sqrt)
            M.ts(st[:, 0:16], st[:, 0:16], 1e-12, None, ALU.max)
            M.recip(st[:, 0:16], st[:, 0:16])
            M.tt(vh(kk), vh(kk), st[:, 0:16].us(2).bc([128, 16, 64]), ALU.mult)
            M.stt(t1, a32, -1.0, bcs["rw_k_a"], ALU.add, ALU.mult)
            M.stt(k32, t1, 1.0, k32, ALU.add, ALU.mult)
            M.tt(a32, kk, a32, ALU.mult)
            M.tt(t1, r32, k32, ALU.mult)
            M.tt(t1, t1, bcs["rw_r_k"], ALU.mult)
            M.red(st[:, 16:32], vh(t1))
            psB = M.psb(4, 2)
            psX = M.psb(6, 2)
            psR = M.psb(0, 2)
            for j in range(2):
                sl = slice(j * 512, (j + 1) * 512)
                M.mm(psB[:, sl], INCL, lw[:, sl])
                M.mm(psX[:, sl], STRICT, lw[:, sl])
                M.mm(psR[:, sl], REM, lw[:, sl])
            M.act(E, psB, AF.Exp, scale=CDEC)
            M.tt(rt, r32, E, ALU.mult)
            M.act(E, psB, AF.Exp, scale=-CDEC)
            M.tt(kt_, k32, E, ALU.mult)
            M.tt(bt, a32, E, ALU.mult)
            M.act(E, psX, AF.Exp, scale=CDEC)
            M.stt(at, kk, -1.0, E, ALU.mult, ALU.mult)
            psT = M.psb(2, 2).bitcast(BF16)
            psT2 = M.psb(4, 2).bitcast(BF16)
            for pr in range(8):
                bl = slice(pr * 128, (pr + 1) * 128)
                M.tp(psT[:, (pr * 2) * 128:(pr * 2 + 1) * 128], at[:, bl], IDb)
                M.tp(psT[:, (pr * 2 + 1) * 128:(pr * 2 + 2) * 128], rt[:, bl], IDb)
                M.tp(psT2[:, (pr * 2) * 128:(pr * 2 + 1) * 128], kt_[:, bl], IDb)
                M.tp(psT2[:, (pr * 2 + 1) * 128:(pr * 2 + 2) * 128], bt[:, bl], IDb)
            M.cp(ARt, psT, en="act")
            M.cp(KBt, psT2)
            M.act(E, psR, AF.Exp, scale=CDEC)
            M.tt(kh, k32, E, ALU.mult)
            M.tt(bh, a32, E, ALU.mult)
            psD = M.psb(6)
            for pr in range(8):
                M.mm(psD[:, pr * nseg:(pr + 1) * nseg], lw[:, pr * 128:(pr + 1) * 128], SEG)
            M.act(dcol[:, 0:8 * nseg], psD[:, 0:8 * nseg], AF.Exp, scale=CDEC)
            if DBG < 2.1:
                continue
            for hg in range(4):
                psA1 = M.psb(0, 2)
                psA2 = M.psb(2, 2)
                psM = M.psb(4, 2)
                for hh in range(4):
                    h = hg * 4 + hh
                    pr, pb = h // 2, (h % 2) * 64
                    KT_h = KBt[pb:pb + 64, (pr * 2) * 128:(pr * 2 + 1) * 128]
                    BT_h = KBt[pb:pb + 64, (pr * 2 + 1) * 128:(pr * 2 + 2) * 128]
                    AR_h = ARt[pb:pb + 64, pr * 256:(pr + 1) * 256]
                    AT_h = ARt[pb:pb + 64, (pr * 2) * 128:(pr * 2 + 1) * 128]
                    ca = (hh % 2) * 512 + (hh // 2) * 256
                    cm = (hh % 2) * 512 + (hh // 2) * 128
                    M.mm(psA1[:, ca:ca + 256], KT_h, AR_h)
                    M.mm(psA2[:, ca:ca + 256], BT_h, AR_h)
                    M.mm(psM[:, cm:cm + 128], AT_h, BT_h)
                if DBG < 2.2:
                    continue
                va = lambda t, a: t.rr("p (h2 hp a t) -> p h2 hp a t", h2=2, hp=2, a=2)[:, :, :, a, :]
                vo = lambda t: t.rr("p (hp h2) t -> p h2 hp t", h2=2)
                hs = slice(hg * 4, (hg + 1) * 4)
                sb4 = STRICT.us(1).us(1).bc([128, 2, 2, 128])
                ib4 = INCL.us(1).us(1).bc([128, 2, 2, 128])
                P, PT, X = NS[0]
                M.tt(vo(AK[:, hs, :]), va(psA1, 0), sb4, ALU.mult)
                M.tt(vo(RK[:, hs, :]), va(psA1, 1), ib4, ALU.mult)
                M.tt(vo(P), va(psA2, 0), sb4, ALU.mult)
                M.tt(vo(RB[:, hs, :]), va(psA2, 1), ib4, ALU.mult)
                vm = psM.rr("p (h2 x) -> p h2 x", h2=2)[:, :, 0:256].rr("p h2 (hp t) -> p h2 hp t", hp=2)
                M.tt(vo(PT), vm, REM.us(1).us(1).bc([128, 2, 2, 128]), ALU.mult)
                if DBG < 2.4:
                    continue
                M.tt(X, P, IDb.us(1).bc([128, 4, 128]), ALU.add, en="pool")
                if DBG < 2.6:
                    continue
                cur = 0
                for lev in range(1, nlev + 1):
                    lastl = (lev == nlev)
                    Pn, PTn, Xn = NS[1 - cur]
                    psP = M.psb(6)
                    psPT = M.psb(7)
                    psXn = M.psb(0)
                    for hh in range(4):
                        bl = slice(hh * 128, (hh + 1) * 128)
                        if not lastl:
                            M.mm(psP[:, bl], PT[:, hh, :], P[:, hh, :])
                        M.mm(psPT[:, bl], P[:, hh, :], PT[:, hh, :])
                    if not lastl:
                        M.cp(Pn.rr("p h t -> p (h t)"), psP, en="act")
                    M.cp(PTn.rr("p h t -> p (h t)"), psPT)
                    for hh in range(4):
                        bl = slice(hh * 128, (hh + 1) * 128)
                        M.mm(psXn[:, bl], PTn[:, hh, :], X[:, hh, :])
                    M.tt(Xn.rr("p h t -> p (h t)"), psXn, X.rr("p h t -> p (h t)"), ALU.add)
                    P, PT, X = Pn, PTn, Xn
                    cur = 1 - cur
                M.cp(TT[:, hs, :], X, en="pool")
            if DBG < 3:
                continue
            psX1 = M.psb(0, 2)
            psU = M.psb(2, 2)
            psO = M.psb(4, 2)

            def scan(prs, smp):
                heads = [2 * pr + h2 for pr in prs for h2 in (0, 1)]
                cs = slice(heads[0] * 64, (heads[-1] + 1) * 64)
                for h in heads:
                    pr, pb = h // 2, (h % 2) * 64
                    hb = slice(h * 64, (h + 1) * 64)
                    if not smp:
                        M.mm(psX1[:, hb], ARt[pb:pb + 64, (pr * 2) * 128:(pr * 2 + 1) * 128],
                             Sbf[pb:pb + 64, pr * 64:(pr + 1) * 64], start=True, stop=False)
                    else:
                        for b in range(NB):
                            M.mm(psX1[:, hb], XP[0][pb:pb + 64, b * 128:(b + 1) * 128],
                                 Sb_[pb:pb + 64, b * 64:(b + 1) * 64], start=(b == 0), stop=False)
                    M.mm(psX1[:, hb], AK[:, h, :], vbf[:, hb], start=False, stop=True)
                M.cp(X1[:, cs], psX1[:, cs], en="act")
                for h in heads:
                    hb = slice(h * 64, (h + 1) * 64)
                    M.mm(psU[:, hb], TT[:, h, :], X1[:, hb])
                M.cp(U[:, cs], psU[:, cs])
                for h in heads:
                    pr, pb = h // 2, (h % 2) * 64
                    hb = slice(h * 64, (h + 1) * 64)
                    if not smp:
                        M.mm(psO[:, hb], ARt[pb:pb + 64, (pr * 2 + 1) * 128:(pr * 2 + 2) * 128],
                             Sbf[pb:pb + 64, pr * 64:(pr + 1) * 64], start=True, stop=False)
                    else:
                        for b in range(NB):
                            M.mm(psO[:, hb], XP[1][pb:pb + 64, b * 128:(b + 1) * 128],
                                 Sb_[pb:pb + 64, b * 64:(b + 1) * 64], start=(b == 0), stop=False)
                    M.mm(psO[:, hb], RB[:, h, :], U[:, hb], start=False, stop=False)
                    M.mm(psO[:, hb], RK[:, h, :], vbf[:, hb], start=False, stop=True)
                if not smp:
                    psS = M.psb(6)
                    for h in heads:
                        pr, pb = h // 2, (h % 2) * 64
                        hb = slice(h * 64, (h + 1) * 64)
                        ob = psS[pb:pb + 64, pr * 64:(pr + 1) * 64]
                        M.mm(ob, bh[:, hb], U[:, hb], start=True, stop=False)
                        M.mm(ob, kh[:, hb], vbf[:, hb], start=False, stop=True)
                    v8 = lambda t: t.rr("p (a x) -> p a x", a=8)
                    M.tt(v8(S32), v8(S32), dcol[:, 0:8].us(2).bc([128, 8, 64]), ALU.mult)
                    M.tt(S32, S32, psS, ALU.add)
                    M.cp(Sbf, S32, en="pool")
                else:
                    pr = prs[0]
                    psS = M.psb(6, 2)
                    for h2 in (0, 1):
                        h = 2 * pr + h2
                        pb = h2 * 64
                        hb = slice(h * 64, (h + 1) * 64)
                        for b in range(NB):
                            ob = psS[pb:pb + 64, b * 64:(b + 1) * 64]
                            M.mm(ob, XP[2][:, b * 128 + pb:b * 128 + pb + 64], U[:, hb], start=True, stop=False)
                            M.mm(ob, XP[3][:, b * 128 + pb:b * 128 + pb + 64], vbf[:, hb], start=False, stop=True)
                    v16 = lambda t: t.rr("p (b x) -> p b x", b=NB)
                    M.tt(v16(Sf), v16(Sf), dcol[:, pr * NB:(pr + 1) * NB].us(2).bc([128, NB, 64]), ALU.mult)
                    M.tt(Sf, Sf, psS, ALU.add)
                    M.dma(O["rw_state_s"][:, pr].rearrange("b p v -> p b v"), v16(Sf))

            if m == "p":
                scan(list(range(8)), False)
            else:
                segb = self.cb("SEG_S").us(2).bc([128, NB, 128])
                for pr in range(8):
                    M.dma(vbx(Sf), I["st_rw"][:, pr].rearrange("b p v -> p b v"))
                    M.cp(Sb_, Sf, en="act")
                    M.tt(vbx(XP[0]), ARt[:, (pr * 2) * 128:(pr * 2 + 1) * 128].us(1).bc([128, NB, 128]),
                         vbx(self.BMb), ALU.mult, en="pool")
                    M.tt(vbx(XP[1]), ARt[:, (pr * 2 + 1) * 128:(pr * 2 + 2) * 128].us(1).bc([128, NB, 128]),
                         vbx(self.BMb), ALU.mult, en="pool")
                    M.tt(vbx(XP[2]), bh[:, pr * 128:(pr + 1) * 128].us(1).bc([128, NB, 128]), segb, ALU.mult)
                    M.tt(vbx(XP[3]), kh[:, pr * 128:(pr + 1) * 128].us(1).bc([128, NB, 128]), segb, ALU.mult)
                    scan([pr], True)
            if DBG < 4:
                continue
            M.cp(o32, psO, en="act")
            M.act(sq, psO, AF.Square)
            s1, s2, mean, var = st[:, 32:48], st[:, 48:64], st[:, 64:80], st[:, 48:64]
            M.red(s1, vh(o32))
            M.red(s2, vh(sq))
            M.ts(mean, s1, 1.0 / 64, None, ALU.mult)
            M.tt(s1, mean, mean, ALU.mult)
            M.stt(var, s2, 1.0 / 64, s1, ALU.mult, ALU.subtract)
            M.rsqrt(var, var, 1.0, GN_EPS)
            M.tt(vh(o32), vh(o32), mean.us(2).bc([128, 16, 64]), ALU.subtract)
            M.tt(vh(o32), vh(o32), var.us(2).bc([128, 16, 64]), ALU.mult)
            M.tt(o32, o32, bcs["rw_gn_w"], ALU.mult)
            M.tt(o32, o32, bcs["rw_gn_b"], ALU.add)
            M.tt(vh(sq), vh(vbf), st[:, 16:32].us(2).bc([128, 16, 64]), ALU.mult)
            M.tt(o32, o32, sq, ALU.add)
            M.dma(self.OS[c], o32)
        M.dma(O["rw_state_p"].rearrange("a p v -> p a v"), S32.rr("p (a x) -> p a x", a=8))
        ps = M.psb(0, 2)
        for kc in range(KC):
            M.tp(ps[0:17, kc * 128:(kc + 1) * 128], self.HL[:, kc, :], self.c("ID"))
        M.cp(hlo, ps[0:17, :])
        M.dma(O["rw_shift"], hlo)
        M.phase_end()

    def sb_phase1(self, i):
        M, I, O = self.M, self.I, self.O
        NCH, NPG = self.NCH, self.NPG
        NT = NCH * 128
        M.phase_begin()
        stage = [M.sb([128, D], F32, "stage") for _ in range(2)]
        W = M.sb([128, 8, 3 * D], BF16, "W")
        self.load_w(W, I["sb_w_in"][:, 0:3 * D], stage)
        b16 = M.sb([65, 16], F32, "b16")
        M.dma(b16[0:1, :], I["sb_bias"])
        M.dma(b16[64:65, :], I["sb_bias"])
        brow = M.sb([65, 16 * 128], BF16, "brow")
        for pb in (0, 64):
            M.cp(brow[pb:pb + 1, :].rr("p (h t) -> p h t", h=16), b16[pb:pb + 1, :].us(2).bc([1, 16, 128]))
        onesb = self.cb("ONES")
        NUPI = self.cb("UPI")
        NONES = self.cb("NONES")
        STRb = self.cb("STRICT_P")
        hT = M.sb([128, D], BF16, "hT")
        q_bf = M.sb([128, D], BF16, "q_bf")
        k_bf = M.sb([128, D], BF16, "k_bf")
        k32, v32 = stage
        vbf = M.sb([128, D], BF16, "vbf")
        qT = M.sb([128, D], BF16, "qT")
        kTc = M.sb([128, D], BF16, "kTc")
        Kb = [M.sb([128, D], BF16, "Kb") for _ in range(2)]
        Vb = [M.sb([128, D], BF16, "Vb") for _ in range(2)]
        eb = [M.sb([128, D], F32, "e") for _ in range(2)]
        spb = [M.sb([128, D], BF16, "sp") for _ in range(2)]
        wb = [M.sb([128, D], BF16, "w") for _ in range(2)]
        e, sp, w = eb[0], spb[0], wb[0]
        Cs32 = [M.sb([128, D], F32, "Cs32") for _ in range(2)]
        Csb = [M.sb([128, D], BF16, "Csb") for _ in range(2)]
        o32 = M.sb([128, D], F32, "o32")
        kp32 = [M.sb([128, D], F32, "kp32") for _ in range(2)]
        vp32 = [M.sb([128, D], F32, "vp32") for _ in range(2)]
        kpb = M.sb([128, D], BF16, "kpb")
        vh8 = lambda t: t.rr("p (h t) -> p h t", h=8)
        if DBG > 45:
            pt_i = M.sb([128, NB * NPG], I32, "pt_i")
            M.dma(pt_i, I["ptab"][0].partition_broadcast(128))
            pt_f = M.sb([128, NB * NPG], F32, "pt_f")
            M.cp(pt_f, pt_i)
            M.ts(pt_f, pt_f, 128.0, self.c("IOTA"), ALU.mult, ALU.add)
            self.rows = M.sb([128, NB * NPG], I32, "rows")
            M.cp(self.rows, pt_f)

        def proj(ps, c0):
            for kc in range(KC):
                for j in range(2):
                    M.mm(ps[:, j * 512:(j + 1) * 512], hT[:, kc * 128:(kc + 1) * 128],
                         W[:, kc, c0 + j * 512:c0 + (j + 1) * 512], start=(kc == 0), stop=(kc == KC - 1))

        nk = 0
        zk = 0
        for c in self.chunks():
            m = self.mode(c)
            self.front(i, c, hT)
            psq = M.psb(0, 2)
            proj(psq, 0)
            psk = M.psb(2, 2)
            proj(psk, D)
            psv = M.psb(4, 2)
            proj(psv, 2 * D)
            M.cp(k32, psk, en="act")
            M.cp(v32, psv, en="act")
            if m == "p":
                M.dma(O["sb_k_p"][c * 128:(c + 1) * 128, :], k32)
                M.dma(O["sb_v_p"][c * 128:(c + 1) * 128, :], v32)
            else:
                M.dma(O["sb_k_s"], k32)
                M.dma(O["sb_v_s"], v32)
            if DBG < 25:
                continue
            M.cp(vbf, psv)
            M.ts(q_bf, psq, 0.125, None, ALU.mult)
            M.cp(k_bf, psk)
            if DBG < 26:
                continue
            psTq = M.psb(6).bitcast(BF16)
            psTk = M.psb(7).bitcast(BF16)
            for pr in range(8):
                bl = slice(pr * 128, (pr + 1) * 128)
                M.tp(psTq[:, bl], q_bf[:, bl], self.cb("ID"))
                M.tp(psTk[:, bl], k_bf[:, bl], self.cb("ID"))
            M.cp(qT, psTq, en="act")
            M.cp(kTc, psTk)
            if DBG < 27:
                continue
            if m == "p":
                M.dma(self.VBS[c], vbf)
                M.dma(self.KTS[c], kTc)
                psO = M.psb(4, 2)
                ZR = [M.psb(0, 2), M.psb(2, 2), M.psb(6, 2)]
                colz = lambda hl: (hl % 2) * 512 + (hl // 2) * 128
                str8 = STRb.us(1).bc([128, 8, 128])
                pend = None
                for j in range(c, -1, -1):
                    diag = (j == c)
                    if diag:
                        Kb_, Vb_ = kTc, vbf
                    else:
                        Kb_, Vb_ = Kb[nk % 2], Vb[nk % 2]
                        nk += 1
                        M.dma(Kb_, self.KTS[j])
                        M.dma(Vb_, self.VBS[j])
                    ZB = [ZR[zk % 3], ZR[(zk + 1) % 3]]
                    zk += 2
                    for hh in range(2):
                        for hl in range(8):
                            h = hh * 8 + hl
                            pr, pb = h // 2, (h % 2) * 64
                            ob = ZB[hh][:, colz(hl):colz(hl) + 128]
                            M.mm(ob, Kb_[pb:pb + 64, pr * 128:(pr + 1) * 128], qT[pb:pb + 64, pr * 128:(pr + 1) * 128],
                                 start=(hl < 2), stop=False)
                            M.mm(ob, onesb[pb:pb + 1, 0:128], brow[pb:pb + 1, h * 128:(h + 1) * 128],
                                 start=False, stop=(hl >= 6))
                    if pend is not None:
                        pend()
                    for hh in range(2):
                        M.act(eb[hh], ZB[hh], AF.Exp)
                        M.act(spb[hh], eb[hh], AF.Ln, bias=1.0)
                        if diag:
                            M.tt(vh8(spb[hh]), vh8(spb[hh]), str8, ALU.mult)
                    for hh in range(2):
                        for bk in range(2):
                            bl = slice(bk * 512, (bk + 1) * 512)
                            M.mm(ZB[hh][:, bl], NUPI, spb[hh][:, bl], start=False, stop=diag, nogrp=True)
                            if not diag:
                                M.mm(ZB[hh][:, bl], NONES, Csb[hh][:, bl], start=False, stop=True, nogrp=True)
                    for hh in range(2):
                        M.act(wb[hh], ZB[hh], AF.Exp)
                        if diag:
                            M.tt(vh8(wb[hh]), vh8(wb[hh]), str8, ALU.mult)
                    if j > 0:
                        for hh in range(2):
                            if diag:
                                M.cp(Cs32[hh], spb[hh])
                            else:
                                M.tt(Cs32[hh], Cs32[hh], spb[hh], ALU.add)
                            M.cp(Csb[hh], Cs32[hh], en="pool")

                    def pv(j=j, diag=diag, Vb_=Vb_):
                        for hh in range(2):
                            for hl in range(8):
                                h = hh * 8 + hl
                                M.mm(psO[:, h * 64:(h + 1) * 64], wb[hh][:, colz(hl):colz(hl) + 128],
                                     Vb_[:, h * 64:(h + 1) * 64], start=(diag and hl == 0),
                                     stop=(j == 0 and hl == 7))
                    pend = pv
                pend()
                M.cp(o32, psO, en="act")
                M.dma(self.OS[c], o32)
            elif DBG < 50:
                pass
            else:
                brs = M.sb([65, 128], BF16, "brs")
                for pb in (0, 64):
                    M.cp(brs[pb:pb + 1, :].rr("p (h t) -> p h t", h=16), b16[pb:pb + 1, :].us(2).bc([1, 16, TS]))
                mbA = self.cb("MASKB")
                v2 = lambda t: t.rr("p (a x) -> p a x", a=2)
                v16 = lambda t: t.rr("p (h t) -> p h t", h=16)
                pz = lambda ps: ps.rr("p (a x) -> p a x", a=2)[:, :, 0:64]
                cc = lambda h: (h % 2) * 64 + (h // 2) * TS
                cp_ = lambda h: (h % 2) * 512 + (h // 2) * TS
                c32, cbf = Cs32[0][:, 0:128], Csb[0][:, 0:128]
                ZS = [M.psb(0, 2), M.psb(2, 2)]
                psTs = [M.psb(6).bitcast(BF16), M.psb(7).bitcast(BF16)]
                e2 = [M.sb([128, 128], F32, "e2") for _ in range(2)]
                sp2 = [M.sb([128, 128], BF16, "sp2") for _ in range(2)]
                w2 = [M.sb([128, 128], BF16, "w2") for _ in range(2)]
                kpb2 = [kpb, M.sb([128, D], BF16, "kpb2")]
                ob32 = o32[0:TS, :]
                psOb = M.psb(4, 2)
                blocks = ["new"] + list(range(NPG - 1, -1, -1))
                units = [(b, bi) for b in range(NB) for bi in range(len(blocks))]
                kv = {}

                def prep_qk(u):
                    b, bi = units[u]
                    blk = blocks[bi]
                    p = u % 2
                    if blk == "new":
                        Kb_, Vb_ = kTc, vbf
                    else:
                        Kb_, Vb_ = Kb[p], Vb[p]
                        self.page_dma(kp32[p], I["ck"], b * NPG + blk)
                        self.page_dma(vp32[p], I["cv"], b * NPG + blk)
                        M.cp(kpb2[p], kp32[p])
                        for pr in range(8):
                            bl = slice(pr * 128, (pr + 1) * 128)
                            M.tp(psTs[p][:, bl], kpb2[p][:, bl], self.cb("ID"))
                        M.cp(Kb_, psTs[p], en="act")
                        M.cp(Vb_, vp32[p])
                    kv[u] = Vb_
                    for h in range(16):
                        pr, pb = h // 2, (h % 2) * 64
                        ob = ZS[p][:, cp_(h):cp_(h) + TS]
                        M.mm(ob, Kb_[pb:pb + 64, pr * 128:(pr + 1) * 128],
                             qT[pb:pb + 64, pr * 128 + b * TS:pr * 128 + (b + 1) * TS], start=(h < 2), stop=False)
                        M.mm(ob, onesb[pb:pb + 1, 0:128], brs[pb:pb + 1, h * TS:(h + 1) * TS],
                             start=False, stop=(h >= 14))

                def mask(t, u):
                    b, bi = units[u]
                    if bi == 0:
                        mb = mbA[:, b * TS:(b + 1) * TS].us(1).bc([128, 16, TS])
                        M.tt(v16(t), v16(t), mb, ALU.mult)

                def act1(u):
                    p = u % 2
                    M.act(v2(e2[p]), pz(ZS[p]), AF.Exp)
                    M.act(sp2[p], e2[p], AF.Ln, bias=1.0)
                    mask(sp2[p], u)

                def tail(u):
                    p = u % 2
                    new = units[u][1] == 0
                    for a in range(2):
                        M.mm(ZS[p][:, a * 512:a * 512 + 64], NUPI, sp2[p][:, a * 64:(a + 1) * 64],
                             start=False, stop=new, nogrp=True)
                        if not new:
                            M.mm(ZS[p][:, a * 512:a * 512 + 64], NONES, cbf[:, a * 64:(a + 1) * 64],
                                 start=False, stop=True, nogrp=True)

                def act2(u):
                    p = u % 2
                    M.act(v2(w2[p]), pz(ZS[p]), AF.Exp)
                    mask(w2[p], u)

                def csum(u):
                    p = u % 2
                    bi = units[u][1]
                    if bi == len(blocks) - 1:
                        return
                    if bi == 0:
                        M.cp(c32, sp2[p])
                    else:
                        M.tt(c32, c32, sp2[p], ALU.add)
                    M.cp(cbf, c32)

                def pv(u):
                    p = u % 2
                    b, bi = units[u]
                    new, lastb = (bi == 0), (bi == len(blocks) - 1)
                    Vb_ = kv.pop(u)
                    for h in range(16):
                        M.mm(psOb[0:TS, h * 64:(h + 1) * 64], w2[p][:, cc(h):cc(h) + TS],
                             Vb_[:, h * 64:(h + 1) * 64], start=(new and h % 8 == 0),
                             stop=(lastb and h % 8 == 7))
                    if lastb:
                        M.cp(ob32, psOb[0:TS, :], en="act")
                        M.dma(V(self.os_ap[NT + b * TS:NT + (b + 1) * TS, :], self.OS[NCH].trs), ob32)

                n = len(units)
                prep_qk(0)
                for u in range(n):
                    act1(u)
                    if u >= 1:
                        pv(u - 1)
                    if u + 1 < n:
                        prep_qk(u + 1)
                    tail(u)
                    act2(u)
                    csum(u)
                pv(n - 1)
        M.phase_end()

    def page_dma(self, out, pool_ap, slot):
        M = self.M
        e = M.E["pool"]
        tr0 = out.trs[0]
        if tr0.dsem is None:
            tr0.dsem = self.nc.alloc_semaphore("gsem_%d" % M.nsem)
            M.nsem += 1
            M.all_dsems.append(tr0)
            M.dsem_map[tr0.dsem.num] = tr0
        for tr in self.rows.trs:
            M._wait(e, tr.w)
        for tr in out.trs:
            if tr.w is not None and tr.w[0] is not tr0.dsem:
                M._wait(e, tr.w)
            for ev in tr.r:
                M._wait(e, ev)
        inst = e.h.indirect_dma_start(
            out=out.ap, out_offset=None, in_=pool_ap.rearrange("n p f -> (n p) f"),
            in_offset=bass.IndirectOffsetOnAxis(ap=self.rows.ap[:, slot:slot + 1], axis=0))
        tr0.dcnt += 16
        inst.then_inc(tr0.dsem, 16)
        M._commit((tr0.dsem, tr0.dcnt), [self.rows], [out])

    def build(self):
        self.setup()
        for i in self.LAYERS:
            self.layer_setup(i)
            kind = i % 4
            if kind == 0:
                self.rw_phase1(i)
            elif kind == 2:
                self.sb_phase1(i)
            else:
                self.la_phase1(i)
            self.phase2(i)
        self.M.finish()


def make_in_maps(inp, cfg):
    NCH, NPG = cfg["NCH"], cfg["NPG"]
    f = lambda a: np.ascontiguousarray(np.asarray(a, dtype=np.float32))
    shared = {}
    for n in ["norm_g", "ada_w", "ada_b"]:
        shared[n] = f(inp[n])
    shared["final_g"] = f(inp["final_g"]).reshape(1, D)
    shared["hg_lower"] = f(inp["hg_lower"])
    for n in ["rw_mix", "rw_w_rkvg", "rw_w0", "rw_w1", "rw_w2", "rw_a0", "rw_a1", "rw_a2", "rw_k_k", "rw_k_a",
              "rw_r_k", "rw_gn_w", "rw_gn_b", "rw_w_o", "gla_w_in", "gla_w_gk2", "gla_b_gk2", "gla_gn_w",
              "gla_w_o", "sb_w_in", "sb_bias", "sb_w_o", "hg_w_in", "hg_gn_w", "hg_w_o"]:
        a = f(inp[n])[0]
        if a.ndim == 1:
            a = a.reshape(1, -1)
        shared[n] = np.ascontiguousarray(a)
    shared["constA"] = CONST_A
    shared["constB"] = CONST_B
    ck = f(inp["cache_sb_k"])[0].reshape(-1, 128, D)
    cv = f(inp["cache_sb_v"])[0].reshape(-1, 128, D)
    shared["ck"] = ck[:cfg["NPOOL"]]
    shared["cv"] = cv[:cfg["NPOOL"]]
    maps = []
    for c in range(NCORE):
        sl = slice(c * NB, (c + 1) * NB)
        m = dict(shared)
        m["xp"] = f(inp["x_prompt"][c // 2]).reshape(NCH * 128, D)
        m["xs"] = f(inp["x_sample"][sl]).reshape(NB * TS, D)
        m["c17"] = np.concatenate([f(inp["c_prompt"][c // 2]).reshape(1, D), f(inp["c_sample"][sl])], axis=0)
        s = f(inp["state_rwkv"][0, sl])
        m["st_rw"] = np.ascontiguousarray(s.transpose(0, 1, 3, 2).reshape(NB, 8, 128, 64))
        m["sh_rw"] = f(inp["cache_rwkv_shift"][0, sl])
        m["st_gla"] = f(inp["state_gla"][0, sl])
        m["st_hg"] = f(inp["state_hgrn"][0, sl])
        m["ptab"] = np.ascontiguousarray(np.asarray(inp["page_table"], dtype=np.int32)[sl].reshape(1, NB * NPG))
        maps.append(m)
    return maps


def assemble(res, cfg, nseq_prompt):
    NCH = cfg["NCH"]
    T = NCH * 128
    ev = [res[2 * s] for s in range(nseq_prompt)]
    y_p = np.stack([r["y_p"].reshape(T, D) for r in ev])
    y_s = np.concatenate([r["y_s"].reshape(NB, TS, D) for r in res], axis=0)

    def rwst(a):
        sh = a.shape[:-3]
        a = a.reshape(*sh, 8, 2, 64, 64).reshape(*sh, 16, 64, 64)
        return np.ascontiguousarray(np.swapaxes(a, -1, -2))
    rw_state_p = np.stack([rwst(r["rw_state_p"]) for r in ev])[None]
    rw_shift_p = np.stack([r["rw_shift"][0] for r in ev])[None]
    gla_state_p = np.stack([r["gla_state_p"] for r in ev])[None]
    sb_k_p = np.stack([r["sb_k_p"].reshape(T, 16, 64) for r in ev])[None]
    sb_v_p = np.stack([r["sb_v_p"].reshape(T, 16, 64) for r in ev])[None]
    hg_state_p = np.stack([r["hg_state_p"] for r in ev])[None]
    rw_state_s = np.concatenate([rwst(r["rw_state_s"]) for r in res], axis=0)[None]
    rw_shift_s = np.concatenate([r["rw_shift"][1:17] for r in res], axis=0)[None]
    gla_state_s = np.concatenate([r["gla_state_s"] for r in res], axis=0)[None]
    sb_k_s = np.concatenate([r["sb_k_s"].reshape(NB, TS, 16, 64) for r in res], axis=0)[None]
    sb_v_s = np.concatenate([r["sb_v_s"].reshape(NB, TS, 16, 64) for r in res], axis=0)[None]
    hg_state_s = np.concatenate([r["hg_state_s"] for r in res], axis=0)[None]
    outs = (y_p, y_s, rw_state_p, rw_shift_p, gla_state_p, sb_k_p, sb_v_p, hg_state_p,
            rw_state_s, rw_shift_s, gla_state_s, sb_k_s, sb_v_s, hg_state_s)
    return tuple(np.ascontiguousarray(o, dtype=np.float32) for o in outs)


def run(inp, cfg):
    prog = Prog(cfg)
    maps = make_in_maps(inp, cfg)
    r = run_bass_kernel_spmd(prog.nc, maps, core_ids=list(range(NCORE)))
    return assemble(r.results, cfg, np.asarray(inp["x_prompt"]).shape[0])


def kernel(**inputs):
    T = np.asarray(inputs["x_prompt"]).shape[1]
    npg = np.asarray(inputs["page_table"]).shape[1]
    npool = np.asarray(inputs["cache_sb_k"]).shape[1]
    cfg = {"NCH": T // 128, "NPG": npg, "NPOOL": npool, "LAYERS": (0, 1, 2, 3)}
    return run(inputs, cfg)
```
